# Optimizing a Trainium2 kernel written in Bass

```python
import math
import numpy as np
import jax
import jax.numpy as jnp
from jax import lax

D_MODEL = 2048
BATCH = 32
SEQ = 256
DEPTH = 2
DEC_BATCH = 2
DEC_SEQ = 1024
PAST_LEN = 256

GRID_W = 64
SSD_INNER = D_MODEL // 2
SSD_HEAD_DIM = 64
SSD_HEADS = SSD_INNER // SSD_HEAD_DIM
SSD_GROUPS = 2
SSD_STATE = 128
SSD_CONV = 3
SSD_XBC = SSD_INNER + 2 * SSD_GROUPS * SSD_STATE
SSD_CHUNK = 128
DIFF_HEAD_DIM = 64
DIFF_WIDTH = D_MODEL // 2
DIFF_HEADS = DIFF_WIDTH // (2 * DIFF_HEAD_DIM)
NA_HEAD_DIM = 64
NA_WIDTH = D_MODEL // 2
NA_HEADS = NA_WIDTH // NA_HEAD_DIM
NA_KH_MAX = 8
NA_KW = 16
NA_QB = 16
NA_KB = NA_QB + NA_KW
D_FF = 256 * ((8 * D_MODEL // 3 + 255) // 256)
FFN_CONV = 3
ROPE_BASE = 10000.0
Q_BLOCK = 128
EPS = 1e-6
IN_SIZES = (SSD_INNER, SSD_XBC, 2 * SSD_HEADS, DIFF_WIDTH, DIFF_WIDTH, DIFF_WIDTH, NA_WIDTH, NA_WIDTH, NA_WIDTH, 3 * D_MODEL)
IN_COLS = sum(IN_SIZES)

kernel_name = 'hybrid_ssd_diffattn_natten_flow_step'

F32 = jnp.float32


def rms_norm(x, w):
    xf = x.astype(F32)
    y = xf * lax.rsqrt(jnp.mean(xf * xf, axis=-1, keepdims=True) + EPS)
    return (y * w.astype(F32)).astype(x.dtype)


def dwconv_centred(x, w, bias):
    k = w.shape[0]
    y = lax.conv_general_dilated(x, w[:, None, :].astype(x.dtype), window_strides=(1,),
                                 padding=[(k // 2, k // 2)], dimension_numbers=('NWC', 'WIO', 'NWC'),
                                 feature_group_count=x.shape[-1])
    return y + bias.astype(x.dtype)


def _segsum_exp(a):
    t = a.shape[-1]
    acum = jnp.cumsum(a, axis=-1)
    diff = acum[..., :, None] - acum[..., None, :]
    mask = jnp.tril(jnp.ones((t, t), dtype=bool))
    return jnp.exp(jnp.where(mask, diff, -jnp.inf))


def ssd_scan(x, dt, a, bm, cm, h0):
    b, l, h, p = x.shape
    n = bm.shape[-1]
    nc = l // SSD_CHUNK
    xd = (x.astype(F32) * dt[..., None]).reshape(b, nc, SSD_CHUNK, h, p)
    bc = bm.astype(F32).reshape(b, nc, SSD_CHUNK, h, n)
    cc = cm.astype(F32).reshape(b, nc, SSD_CHUNK, h, n)
    la = (dt * a).reshape(b, nc, SSD_CHUNK, h).transpose(0, 3, 1, 2)
    acum = jnp.cumsum(la, axis=-1)
    lmat = _segsum_exp(la)
    gmat = jnp.einsum('bclhn,bcshn->bhcls', cc, bc) * lmat
    y_diag = jnp.einsum('bhcls,bcshp->bclhp', gmat, xd)
    decay_states = jnp.exp(acum[..., -1:] - acum)
    states = jnp.einsum('bclhn,bhcl,bclhp->bchpn', bc, decay_states, xd)
    states = jnp.concatenate([h0.astype(F32)[:, None], states], axis=1)
    chunk_a = jnp.pad(acum[..., -1], ((0, 0), (0, 0), (1, 0)))
    decay_chunk = _segsum_exp(chunk_a)
    new_states = jnp.einsum('bhzc,bchpn->bzhpn', decay_chunk, states)
    states_in, h_final = new_states[:, :-1], new_states[:, -1]
    y_off = jnp.einsum('bclhn,bchpn,bhcl->bclhp', cc, states_in, jnp.exp(acum))
    return (y_diag + y_off).reshape(b, l, h, p), h_final


def _rotate(x, ang):
    f = ang.shape[-1]
    shape = (ang.shape[0],) + (1,) * (x.ndim - 3) + (f,)
    cos = jnp.cos(ang).reshape(shape)
    sin = jnp.sin(ang).reshape(shape)
    x1, x2 = x[..., :f], x[..., f:]
    return jnp.concatenate([x1 * cos - x2 * sin, x1 * sin + x2 * cos], axis=-1)


def apply_axial_rope(x):
    l, dim = x.shape[1], x.shape[-1]
    t = jnp.arange(l)
    row = (t // GRID_W).astype(F32)
    col = (t % GRID_W).astype(F32)
    n_freq = dim // 4
    inv = ROPE_BASE ** (-jnp.arange(n_freq, dtype=F32) / n_freq)
    xf = x.astype(F32)
    half = dim // 2
    out = jnp.concatenate([_rotate(xf[..., :half], row[:, None] * inv),
                           _rotate(xf[..., half:], col[:, None] * inv)], axis=-1)
    return out.astype(x.dtype)


def diff_attention(q, k, v, lam, subln_w, lam_init):
    b, lq, hh, _, d = q.shape
    nb = lq // Q_BLOCK
    qb = q.reshape(b, nb, Q_BLOCK, hh, 2, d).transpose(1, 0, 2, 3, 4, 5)
    scale = d ** -0.5

    def block(qi):
        s = jnp.einsum('bqhtd,bkhtd->bhtqk', qi, k, preferred_element_type=F32) * scale
        p = jax.nn.softmax(s, axis=-1)
        att = p[:, :, 0] - lam * p[:, :, 1]
        return jnp.einsum('bhqk,bkhe->bqhe', att.astype(v.dtype), v)

    o = lax.map(block, qb).transpose(1, 0, 2, 3, 4).reshape(b, lq, hh, v.shape[-1])
    o = rms_norm(o, subln_w) * (1.0 - lam_init)
    return o.reshape(b, lq, hh * v.shape[-1])


def softmax_attention(q, k, v):
    b, lq, hh, d = q.shape
    nb = lq // Q_BLOCK
    qb = q.reshape(b, nb, Q_BLOCK, hh, d).transpose(1, 0, 2, 3, 4)
    scale = d ** -0.5

    def block(qi):
        s = jnp.einsum('bqhd,bkhd->bhqk', qi, k, preferred_element_type=F32) * scale
        p = jax.nn.softmax(s, axis=-1)
        return jnp.einsum('bhqk,bkhd->bqhd', p.astype(v.dtype), v)

    return lax.map(block, qb).transpose(1, 0, 2, 3, 4).reshape(b, lq, hh * d)


def neighborhood_attention(q, k, v, kc, vc, rpb):
    b, l, hh, d = q.shape
    rows = l // GRID_W
    kh = min(NA_KH_MAX, rows)
    ncb = GRID_W // NA_QB
    r = np.arange(rows)
    row_idx = np.clip(r - kh // 2, 0, rows - kh)[:, None] + np.arange(kh)[None, :]
    qcol = np.arange(ncb)[:, None] * NA_QB + np.arange(NA_QB)[None, :]
    band0 = np.clip(np.arange(ncb) * NA_QB - NA_KW // 2, 0, GRID_W - NA_KB)
    kcol = band0[:, None] + np.arange(NA_KB)[None, :]
    wstart = np.clip(qcol - NA_KW // 2, 0, GRID_W - NA_KW)
    valid = (kcol[:, None, :] >= wstart[:, :, None]) & (kcol[:, None, :] < wstart[:, :, None] + NA_KW)
    dr = row_idx - r[:, None] + NA_KH_MAX - 1
    dc = np.clip(kcol[:, None, :] - qcol[:, :, None] + NA_KW - 1, 0, 2 * NA_KW - 2)
    bias = rpb.astype(F32)[:, dr[:, None, None, :, None], dc[None, :, :, None, :]]
    bias = jnp.transpose(bias, (1, 2, 3, 0, 4, 5))
    qg = q.reshape(b, rows, ncb, NA_QB, hh, d)
    kg = k.reshape(b, rows, GRID_W, hh, d)[:, row_idx][:, :, :, kcol]
    vg = v.reshape(b, rows, GRID_W, hh, d)[:, row_idx][:, :, :, kcol]
    scale = d ** -0.5
    s_loc = jnp.einsum('brjuhd,brijkhd->brjuhik', qg, kg, preferred_element_type=F32) * scale + bias
    s_loc = jnp.where(valid[:, :, None, None, :], s_loc, -jnp.inf)
    s_ctx = jnp.einsum('brjuhd,bmhd->brjuhm', qg, kc, preferred_element_type=F32) * scale
    n_loc = kh * NA_KB
    s = jnp.concatenate([s_loc.reshape(s_loc.shape[:5] + (n_loc,)), s_ctx], axis=-1)
    p = jax.nn.softmax(s, axis=-1).astype(v.dtype)
    p_loc = p[..., :n_loc].reshape(s_loc.shape)
    o = (jnp.einsum('brjuhik,brijkhd->brjuhd', p_loc, vg)
         + jnp.einsum('brjuhm,bmhd->brjuhd', p[..., n_loc:], vc))
    return o.reshape(b, l, hh * d)


def mixer(h, P, layer, cache):
    b, l, _ = h.shape
    odt = h.dtype
    u = h @ P['w_in']
    splits = np.cumsum(IN_SIZES)[:-1].tolist()
    z, xbc, dt_raw, qd, kd, vd, qn, kn, vn, g = jnp.split(u, splits, axis=-1)

    xbc = jax.nn.silu(dwconv_centred(xbc, P['ssd_conv_w'], P['ssd_conv_b']))
    xs, bm, cm = jnp.split(xbc, [SSD_INNER, SSD_INNER + SSD_GROUPS * SSD_STATE], axis=-1)
    xs = xs.reshape(b, l, SSD_HEADS, SSD_HEAD_DIM)
    rep = SSD_HEADS // SSD_GROUPS
    bm = jnp.repeat(bm.reshape(b, l, SSD_GROUPS, SSD_STATE), rep, axis=2)
    cm = jnp.repeat(cm.reshape(b, l, SSD_GROUPS, SSD_STATE), rep, axis=2)
    dt = jax.nn.softplus(dt_raw.reshape(b, l, 2, SSD_HEADS).astype(F32) + P['ssd_dt_bias'].astype(F32))
    a = -jnp.exp(P['ssd_a_log'].astype(F32))
    if cache is None:
        h0 = jnp.zeros((b, 2, SSD_HEADS, SSD_HEAD_DIM, SSD_STATE), F32)
    else:
        h0 = cache[4]
    flip = lambda t: jnp.flip(t, axis=1)
    y_f, hf = ssd_scan(xs, dt[:, :, 0], a[0], bm, cm, h0[:, 0])
    y_r, hr = ssd_scan(flip(xs), flip(dt[:, :, 1]), a[1], flip(bm), flip(cm), h0[:, 1])
    y_a = y_f + flip(y_r) + P['ssd_d'].astype(F32)[:, None] * xs.astype(F32)
    y_a = y_a.reshape(b, l, SSD_INNER) * jax.nn.silu(z.astype(F32))
    y_a = rms_norm(y_a.reshape(b, l, SSD_GROUPS, SSD_INNER // SSD_GROUPS),
                   P['ssd_norm_w'].reshape(SSD_GROUPS, SSD_INNER // SSD_GROUPS)).reshape(b, l, SSD_INNER).astype(odt)
    ssm_state = jnp.stack([hf, hr], axis=1).astype(odt)

    qd = rms_norm(qd.reshape(b, l, DIFF_HEADS, 2, DIFF_HEAD_DIM), P['diff_q_norm'])
    kd = rms_norm(kd.reshape(b, l, DIFF_HEADS, 2, DIFF_HEAD_DIM), P['diff_k_norm'])
    vd = vd.reshape(b, l, DIFF_HEADS, 2 * DIFF_HEAD_DIM)
    lam_init = 0.8 - 0.6 * math.exp(-0.3 * layer)
    lp = P['diff_lam'].astype(F32)
    lam = jnp.exp(jnp.sum(lp[0] * lp[1])) - jnp.exp(jnp.sum(lp[2] * lp[3])) + lam_init
    if cache is None:
        y_b = diff_attention(qd, kd, vd, lam, P['diff_subln_w'], lam_init)
    else:
        keys = jnp.concatenate([apply_axial_rope(kd), cache[0]], axis=1)
        vals = jnp.concatenate([vd, cache[1]], axis=1)
        y_b = diff_attention(apply_axial_rope(qd), keys, vals, lam, P['diff_subln_w'], lam_init)

    qn = rms_norm(qn.reshape(b, l, NA_HEADS, NA_HEAD_DIM), P['na_q_norm'])
    kn = rms_norm(kn.reshape(b, l, NA_HEADS, NA_HEAD_DIM), P['na_k_norm'])
    vn = vn.reshape(b, l, NA_HEADS, NA_HEAD_DIM)
    if cache is None:
        y_c = softmax_attention(qn, kn, vn)
    else:
        y_c = neighborhood_attention(qn, kn, vn, cache[2], cache[3], P['na_rpb'])

    g_a, g_b, g_c = jnp.split(jax.nn.sigmoid(g), 3, axis=-1)
    merged = (g_a * (y_a @ P['w_branch_a']) + g_b * (y_b @ P['w_branch_b'])
              + g_c * (y_c @ P['w_branch_c']))
    return merged @ P['w_out'], (kd, vd, kn, vn, ssm_state)


def conv_ffn(h, P):
    u = dwconv_centred(h @ P['ffn_w_up'], P['ffn_conv_w'], P['ffn_conv_b'])
    val, gt = jnp.split(u, 2, axis=-1)
    return (jax.nn.silu(gt) * val) @ P['ffn_w_down']


def trunk_layer(x, mod, P, layer, cache):
    sh1, sc1, g1, sh2, sc2, g2 = jnp.split(mod[:, None, :].astype(x.dtype), 6, axis=-1)
    h = rms_norm(x, P['norm1_w']) * (1 + sc1) + sh1
    m, ctx = mixer(h, P, layer, cache)
    x = x + g1 * m
    h = rms_norm(x, P['norm2_w']) * (1 + sc2) + sh2
    x = x + g2 * conv_ffn(h, P)
    return x, ctx


def setup_inputs(seed: int = 0) -> dict:
    key = jax.random.key(seed)
    ks = iter(jax.random.split(key, 48))

    def nrm(shape, scale):
        return scale * jax.random.normal(next(ks), shape, F32)

    d = D_MODEL
    dt0 = jnp.exp(jax.random.uniform(next(ks), (DEPTH, 2, SSD_HEADS), F32, math.log(1e-3), math.log(1e-1)))
    return {
        'x_prompt': nrm((BATCH, SEQ, d), 1.0),
        'x_sample': nrm((DEC_BATCH, DEC_SEQ, d), 1.0),
        'c': nrm((DEC_BATCH, d), 1.0),
        'cache_diff_k': nrm((DEC_BATCH, DEPTH, PAST_LEN, DIFF_HEADS, 2, DIFF_HEAD_DIM), 1.0),
        'cache_diff_v': nrm((DEC_BATCH, DEPTH, PAST_LEN, DIFF_HEADS, 2 * DIFF_HEAD_DIM), 1.0),
        'cache_na_k': nrm((DEC_BATCH, DEPTH, PAST_LEN, NA_HEADS, NA_HEAD_DIM), 1.0),
        'cache_na_v': nrm((DEC_BATCH, DEPTH, PAST_LEN, NA_HEADS, NA_HEAD_DIM), 1.0),
        'state_ssm': nrm((DEC_BATCH, DEPTH, 2, SSD_HEADS, SSD_HEAD_DIM, SSD_STATE), 0.5),
        'c_ctx': nrm((d,), 1.0),
        'norm1_w': 1.0 + nrm((DEPTH, d), 0.02),
        'norm2_w': 1.0 + nrm((DEPTH, d), 0.02),
        'w_ada': nrm((DEPTH, d, 6 * d), 0.5 * d ** -0.5),
        'b_ada': nrm((DEPTH, 6 * d), 0.02),
        'w_in': nrm((DEPTH, d, IN_COLS), d ** -0.5),
        'ssd_conv_w': nrm((DEPTH, SSD_CONV, SSD_XBC), SSD_CONV ** -0.5),
        'ssd_conv_b': nrm((DEPTH, SSD_XBC), 0.02),
        'ssd_dt_bias': jnp.log(jnp.expm1(dt0)),
        'ssd_a_log': jnp.log(jax.random.uniform(next(ks), (DEPTH, 2, SSD_HEADS), F32, 1.0, 16.0)),
        'ssd_d': 1.0 + nrm((DEPTH, SSD_HEADS), 0.02),
        'ssd_norm_w': 1.0 + nrm((DEPTH, SSD_INNER), 0.02),
        'diff_q_norm': 1.0 + nrm((DEPTH, DIFF_HEAD_DIM), 0.02),
        'diff_k_norm': 1.0 + nrm((DEPTH, DIFF_HEAD_DIM), 0.02),
        'diff_lam': nrm((DEPTH, 4, DIFF_HEAD_DIM), 0.1),
        'diff_subln_w': 1.0 + nrm((DEPTH, 2 * DIFF_HEAD_DIM), 0.02),
        'na_q_norm': 1.0 + nrm((DEPTH, NA_HEAD_DIM), 0.02),
        'na_k_norm': 1.0 + nrm((DEPTH, NA_HEAD_DIM), 0.02),
        'na_rpb': nrm((DEPTH, NA_HEADS, 2 * NA_KH_MAX - 1, 2 * NA_KW - 1), 0.1),
        'w_branch_a': nrm((DEPTH, SSD_INNER, d), SSD_INNER ** -0.5),
        'w_branch_b': nrm((DEPTH, DIFF_WIDTH, d), DIFF_WIDTH ** -0.5),
        'w_branch_c': nrm((DEPTH, NA_WIDTH, d), NA_WIDTH ** -0.5),
        'w_out': nrm((DEPTH, d, d), d ** -0.5),
        'ffn_w_up': nrm((DEPTH, d, 2 * D_FF), d ** -0.5),
        'ffn_conv_w': nrm((DEPTH, FFN_CONV, 2 * D_FF), FFN_CONV ** -0.5),
        'ffn_conv_b': nrm((DEPTH, 2 * D_FF), 0.02),
        'ffn_w_down': nrm((DEPTH, D_FF, d), D_FF ** -0.5),
    }


def reference(x_prompt, x_sample, c, cache_diff_k, cache_diff_v, cache_na_k, cache_na_v, state_ssm, c_ctx,
              norm1_w, norm2_w, w_ada, b_ada, w_in, ssd_conv_w, ssd_conv_b, ssd_dt_bias, ssd_a_log, ssd_d,
              ssd_norm_w, diff_q_norm, diff_k_norm, diff_lam, diff_subln_w, na_q_norm, na_k_norm, na_rpb,
              w_branch_a, w_branch_b, w_branch_c, w_out, ffn_w_up, ffn_conv_w, ffn_conv_b, ffn_w_down):
    y_p = x_prompt
    y_s = x_sample
    new_dk, new_dv, new_nk, new_nv, new_s = [], [], [], [], []
    for l in range(DEPTH):
        P = dict(norm1_w=norm1_w[l], norm2_w=norm2_w[l], w_in=w_in[l], ssd_conv_w=ssd_conv_w[l],
                 ssd_conv_b=ssd_conv_b[l], ssd_dt_bias=ssd_dt_bias[l], ssd_a_log=ssd_a_log[l], ssd_d=ssd_d[l],
                 ssd_norm_w=ssd_norm_w[l], diff_q_norm=diff_q_norm[l], diff_k_norm=diff_k_norm[l],
                 diff_lam=diff_lam[l], diff_subln_w=diff_subln_w[l], na_q_norm=na_q_norm[l],
                 na_k_norm=na_k_norm[l], na_rpb=na_rpb[l], w_branch_a=w_branch_a[l], w_branch_b=w_branch_b[l],
                 w_branch_c=w_branch_c[l], w_out=w_out[l], ffn_w_up=ffn_w_up[l], ffn_conv_w=ffn_conv_w[l],
                 ffn_conv_b=ffn_conv_b[l], ffn_w_down=ffn_w_down[l])
        mod_ctx = jax.nn.silu(c_ctx)[None, :] @ w_ada[l] + b_ada[l]
        y_p, ctx_t = trunk_layer(y_p, mod_ctx, P, l, None)
        new_dk.append(ctx_t[0])
        new_dv.append(ctx_t[1])
        new_nk.append(ctx_t[2])
        new_nv.append(ctx_t[3])
        new_s.append(ctx_t[4])
        mod_lat = jax.nn.silu(c) @ w_ada[l] + b_ada[l]
        cache = (cache_diff_k[:, l], cache_diff_v[:, l], cache_na_k[:, l], cache_na_v[:, l], state_ssm[:, l])
        y_s, _ = trunk_layer(y_s, mod_lat, P, l, cache)
    return (y_p, y_s, jnp.stack(new_dk, axis=1), jnp.stack(new_dv, axis=1), jnp.stack(new_nk, axis=1),
            jnp.stack(new_nv, axis=1), jnp.stack(new_s, axis=1))
```

```python
import math
from contextlib import ExitStack
import numpy as np
import concourse.bass as bass
import concourse.mybir as mybir
from concourse.bass_utils import run_bass_kernel_spmd

F32 = mybir.dt.float32
BF16 = mybir.dt.bfloat16
AF = mybir.ActivationFunctionType
ALU = mybir.AluOpType
AX = mybir.AxisListType

D = 2048
T = 1024
EPS = 1e-6
NCORES = 8
DEPTH = 2
IN_COLS = 14880
DFF = 5632
C_Z, C_XBC, C_DT, C_QD, C_KD, C_VD, C_QN, C_KN, C_VN, C_G = 0, 1024, 2560, 2592, 3616, 4640, 5664, 6688, 7712, 8736
NEG = -30000.0


class _Rec:
    __slots__ = ("w", "r")

    def __init__(self):
        self.w = None
        self.r = {}


class KB:
    def __init__(self):
        self.nc = bass.Bass("TRN2", target_bir_lowering=False)
        nc = self.nc
        self.eng = {"pe": nc.tensor, "act": nc.scalar, "dve": nc.vector, "pool": nc.gpsimd, "sp": nc.sync}
        self.sems = {e: nc.alloc_semaphore("sem_" + e) for e in self.eng}
        self.cnt = {e: 0 for e in self.eng}
        self.waited = {e: {} for e in self.eng}
        self.recs = {}
        self.dsems = []
        self.dma_of = {}
        self.free_d = []
        self.tracked = set()
        self.dram = set()
        self.uid = 0

    def name(self, base):
        self.uid += 1
        return "%s_%d" % (base, self.uid)

    def track(self, t):
        self.tracked.add(t.name)
        return t

    def sb(self, name, shape, dtype=F32):
        return self.track(self.nc.alloc_sbuf_tensor(self.name(name), list(shape), dtype))

    def sbg(self, es, name, shape, dtype=F32):
        t = es.enter_context(self.nc.sbuf_tensor(self.name(name), list(shape), dtype))
        return self.track(t)

    def ps(self, name, shape, dtype=F32):
        return self.track(self.nc.alloc_psum_tensor(name, list(shape), dtype))

    def _recs_for(self, tn):
        if tn not in self.recs:
            self.recs[tn] = _Rec()
        return self.recs[tn]

    def _split(self, outs, ins):
        rd, wr = [], []
        for a in ins:
            if a is None or isinstance(a, (int, float)):
                continue
            tn = a.tensor.name
            if tn in self.tracked:
                rd.append(tn)
        for a in outs:
            tn = a.tensor.name
            if tn in self.tracked:
                wr.append(tn)
        return rd, wr

    def _deps(self, rd, wr):
        deps = {}

        def need(s, v):
            if deps.get(s, 0) < v:
                deps[s] = v
        for tn in rd:
            rec = self._recs_for(tn)
            if rec.w is not None:
                need(*rec.w)
            if tn.startswith("psb"):
                for s, v in rec.r.items():
                    need(s, v)
        for tn in wr:
            rec = self._recs_for(tn)
            if rec.w is not None:
                need(*rec.w)
            for s, v in rec.r.items():
                need(s, v)
        return deps

    def _semobj(self, s):
        if s in self.sems:
            return self.sems[s]
        return self.dsems[int(s[1:])][0]

    def _waits(self, e, deps):
        for s, v in deps.items():
            if e == "pe" and s == "pe":
                continue
            if self.waited[e].get(s, 0) >= v:
                continue
            self.eng[e].wait_ge(self._semobj(s), v)
            self.waited[e][s] = v

    def _update(self, rd, wr, s, v):
        for tn in rd:
            self.recs[tn].r[s] = v
        for tn in wr:
            rec = self.recs[tn]
            rec.w = (s, v)
            rec.r = {}

    def op(self, e, fname, outs, ins, *args, **kw):
        rd, wr = self._split(outs, ins)
        self._waits(e, self._deps(rd, wr))
        i = getattr(self.eng[e], fname)(*args, **kw)
        self.cnt[e] += 1
        i.then_inc(self.sems[e], 1)
        self._update(rd, wr, e, self.cnt[e])
        return i

    def dma(self, q, out, in_):
        otn, itn = out.tensor.name, in_.tensor.name
        rd, wr = [], []
        if itn in self.tracked:
            rd.append(itn)
        if otn in self.tracked:
            wr.append(otn)
        if otn in self.tracked and otn not in self.dram:
            sbn = otn
        else:
            sbn = itn
        self._waits(q, self._deps(rd, wr))
        idx = self.dma_of.get(sbn)
        if idx is None:
            if self.free_d:
                idx = self.free_d.pop()
            else:
                idx = len(self.dsems)
                self.dsems.append([self.nc.alloc_semaphore("dsem%d" % idx), 0])
            self.dma_of[sbn] = idx
        ds = self.dsems[idx]
        i = self.eng[q].dma_start(out=out, in_=in_)
        ds[1] += 16
        i.then_inc(ds[0], 16)
        self._update(rd, wr, "D%d" % idx, ds[1])
        return i

    def barrier(self):
        for e in self.eng:
            deps = {}
            for f in self.eng:
                if f != e and self.cnt[f] > 0:
                    deps[f] = self.cnt[f]
            for i, (s, c) in enumerate(self.dsems):
                if c > 0:
                    deps["D%d" % i] = c
            self._waits(e, deps)
        for rec in self.recs.values():
            rec.w = None
            rec.r = {}
        self.dma_of = {}
        self.free_d = list(range(len(self.dsems)))

    def finish(self):
        for i, (s, c) in enumerate(self.dsems):
            if c > 0 and self.waited["sp"].get("D%d" % i, 0) < c:
                self.eng["sp"].wait_ge(s, c)


class Cols:
    def __init__(self):
        self.off = {}
        self.n = 0

    def add(self, name, w):
        self.off[name] = (self.n, w)
        self.n += w

    def __call__(self, name):
        return self.off[name]


def make_cols():
    c = Cols()
    c.add("csil", 32)
    for l in range(DEPTH):
        p = "L%d_" % l
        for nm, w in [("n1w", 16), ("n2w", 16), ("bada", 96), ("scw", 36), ("scb", 12), ("fcw", 264), ("fcb", 88),
                      ("snw", 8), ("sdd", 8), ("dqn", 1), ("dkn", 1), ("sub", 1), ("nqn", 1), ("nkn", 1),
                      ("dtb", 32), ("alog", 32), ("lam", 256)]:
            c.add(p + nm, w)
    return c


COLS = make_cols()
N_MATS = 6
N_MATF = 5


class _Stop(Exception):
    pass


MARKS = []


def build_program(stop=None, only=None):
    k = KB()
    nc = k.nc
    taps = {}

    def ck(label, **aps):
        MARKS.append((label, k.cnt["pe"]))
        if stop != label:
            return
        k.barrier()
        for nm, ap in aps.items():
            d = nc.dram_tensor("dbg_" + nm, list(ap.shape), ap.dtype, kind="ExternalOutput").ap()
            k.dma("sp", d, ap)
            taps[nm] = d
        raise _Stop()

    def din(name, shape, dt=F32):
        return nc.dram_tensor(name, list(shape), dt, kind="ExternalInput").ap()

    def dout(name, shape, dt=F32):
        return nc.dram_tensor(name, list(shape), dt, kind="ExternalOutput").ap()

    def dscr(name, shape, dt=F32):
        t = nc.dram_tensor(name, list(shape), dt, kind="Internal")
        k.tracked.add(t.name)
        k.dram.add(t.name)
        return t.ap()

    xin = {"ctx": din("xp", [128, 16, T]), "lat": din("xs", [128, 16, T])}
    yout = {"ctx": dout("ypT", [128, 16, T]), "lat": dout("ysT", [128, 16, T])}
    cst_d = din("cst", [128, COLS.n])
    matb_d = din("matb", [128, N_MATS, 128])
    matf_d = din("matf", [128, N_MATF, 128])
    rope_d = din("rope", [128, 2, T])
    pmt_d = din("pmT", [128, 128])
    tb_d = din("tb", [DEPTH, 16, 128, 1920])
    rowoh_d = din("rowoh", [16, 8, 128])
    rowpen_d = din("rowpen", [16, T])
    cdk_d = din("cdk", [DEPTH, 128, 8, 256])
    cdv_d = din("cdv", [DEPTH, 128, 2, 1024])
    cnk_d = din("cnk", [DEPTH, 128, 8, 256])
    cnv_d = din("cnv", [DEPTH, 128, 2, 1024])
    sst_d = din("sst", [DEPTH, 128, 2048])
    w_ada = din("w_ada", [DEPTH, D, 6 * D])
    w_in = din("w_in", [DEPTH, D, IN_COLS])
    w_br = [din("w_branch_a", [DEPTH, 1024, D]), din("w_branch_b", [DEPTH, 1024, D]), din("w_branch_c", [DEPTH, 1024, D])]
    w_out = din("w_out", [DEPTH, D, D])
    w_up = din("ffn_w_up", [DEPTH, D, 2 * DFF])
    w_dn = din("ffn_w_down", [DEPTH, DFF, D])
    ndk_o = dout("ndk", [DEPTH, 128, 8, T])
    ndv_o = dout("ndv", [DEPTH, 128, 8, T])
    nnk_o = dout("nnk", [DEPTH, 128, 8, T])
    nnv_o = dout("nnv", [DEPTH, 128, 8, T])
    nss_o = dout("nss", [DEPTH, 4, 128, 2048])
    xmid1 = {g: dscr("xmid1_" + g, [128, 16, T]) for g in ("ctx", "lat")}
    xmid2 = {g: dscr("xmid2_" + g, [128, 16, T]) for g in ("ctx", "lat")}

    cst = k.sb("cst", [128, COLS.n])
    matb = k.sb("matb", [128, N_MATS, 128], BF16)
    matf = k.sb("matf", [128, N_MATF, 128])
    h = k.sb("h", [128, 16, T], BF16)
    wslots = [k.sb("wslot%d" % i, [128, 16, 256], BF16) for i in range(2)]
    mod_sb = [k.sb("mod%d" % l, [128, 96, 2]) for l in range(DEPTH)]
    der = k.sb("der", [128, DEPTH, 2, 2, 16])
    smallv = k.sb("smallv", [128, DEPTH, 8])
    nega = k.sb("nega", [128, DEPTH, 32])
    PS = [k.ps("psb%d" % i, [128, 512]) for i in range(8)]
    IDENT, ONES, ONES2048, ONES512, ONES128, BLK64 = [matb[:, i, :] for i in range(6)]
    M_LE, M_GE, M_LT, M_GT, ONESF = [matf[:, i, :] for i in range(5)]

    def ccol(name, i=0, w=1):
        o, _ = COLS(name)
        return cst[:, o + i:o + i + w]

    def mm(out, lhsT, rhs, start=True, stop=True, **kw):
        return k.op("pe", "matmul", [out], [lhsT, rhs], out, lhsT, rhs, start=start, stop=stop, **kw)

    def act(out, in_, func, bias=None, scale=None):
        kw = {}
        ins = [in_]
        if bias is not None:
            kw["bias"] = bias
            ins.append(bias)
        if scale is not None:
            kw["scale"] = scale
            ins.append(scale)
        return k.op("act", "activation", [out], ins, out=out, in_=in_, func=func, **kw)

    def tt(out, in0, in1, op, e="dve"):
        return k.op(e, "tensor_tensor", [out], [in0, in1], out=out, in0=in0, in1=in1, op=op)

    def ts(out, in0, s1, s2, op0, op1=None, e="dve"):
        kw = dict(out=out, in0=in0, scalar1=s1, scalar2=s2, op0=op0)
        if op1 is not None:
            kw["op1"] = op1
        return k.op(e, "tensor_scalar", [out], [in0, s1, s2], **kw)

    def stt(out, in0, scalar, in1, op0, op1):
        return k.op("dve", "scalar_tensor_tensor", [out], [in0, scalar, in1], out=out, in0=in0, scalar=scalar,
                    in1=in1, op0=op0, op1=op1)

    def cp(out, in_, e="dve"):
        return k.op(e, "tensor_copy", [out], [in_], out=out, in_=in_)

    def rsqrt_to(out, in_):
        act(out, in_, AF.Ln, bias=EPS)
        act(out, out, AF.Exp, scale=-0.5)

    wrr = [0]

    def wslot():
        s = wslots[wrr[0] % len(wslots)]
        wrr[0] += 1
        return s

    def wload(W3, c0, w, KC):
        s = wslot()
        hk = max(1, KC // 2)
        for k0 in range(0, KC, hk):
            k.dma("pool", s[:, k0:k0 + hk, :w], W3[:, k0:k0 + hk, c0:c0 + w])
        return s

    prr = [0]

    def bankpair():
        i = prr[0] % 3
        prr[0] += 1
        return [PS[2 * i], PS[2 * i + 1]]

    def linear_fm(W3, col0, ncols, KC, rhs, consume, banks_fn=bankpair, defer=1):
        m = 0
        pend = []
        for c0 in range(col0, col0 + ncols, 256):
            w = min(256, col0 + ncols - c0)
            s = wload(W3, c0, w, KC)
            for mc in range(w // 128):
                banks = banks_fn()
                for kc in range(KC):
                    for hf in range(2):
                        mm(banks[hf][:], s[:, kc, mc * 128:(mc + 1) * 128], rhs(kc, hf), start=(kc == 0), stop=(kc == KC - 1))
                pend.append((m, banks))
                if len(pend) > defer:
                    consume(*pend.pop(0))
                m += 1
            bg_step()
        while pend:
            consume(*pend.pop(0))

    def wview(w, l):
        return w[l].rearrange("(c p) n -> p c n", p=128)

    k.dma("sp", cst[:], cst_d)
    k.dma("pool", matb[:], matb_d)
    k.dma("sp", matf[:], matf_d)
    scb = k.sb("scb", [128, 16, 2], BF16)
    o, _ = COLS("csil")
    act(scb[:].rearrange("p a b -> p (a b)"), cst[:, o:o + 32], AF.Silu)
    def mod_block(l, b):
        W3 = wview(w_ada, l)
        ob, _ = COLS("L%d_bada" % l)
        if bgslots:
            s = bgslots[b % 2]
            for k0 in (0, 8):
                k.dma("pool", s[:, k0:k0 + 8, :], W3[:, k0:k0 + 8, b * 256:(b + 1) * 256])
        else:
            s = wload(W3, b * 256, 256, 16)
        for mc in range(2):
            m = 2 * b + mc
            pb = PS[7]
            for kc in range(16):
                mm(pb[:, 2 * mc:2 * mc + 2], s[:, kc, mc * 128:(mc + 1) * 128], scb[:, kc, :], start=(kc == 0), stop=(kc == 15))
            tt(mod_sb[l][:, m, :], pb[:, 2 * mc:2 * mc + 2], cst[:, ob + m:ob + m + 1].broadcast_to([128, 2]), ALU.add)

    def mod_der(l, which):
        onw, _ = COLS("L%d_%s" % (l, "n1w" if which == 0 else "n2w"))
        base = 16 if which == 0 else 64
        for g in range(2):
            stt(der[:, l, g, which, :], mod_sb[l][:, base:base + 16, g], 1.0, cst[:, onw:onw + 16], ALU.add, ALU.mult)

    bg = []
    bgslots = []
    for b in range(16):
        mod_block(0, b)
    mod_der(0, 0)
    for b in range(16, 48):
        bg.append(lambda b=b: mod_block(0, b))
    bg.append(lambda: mod_der(0, 1))
    BG_L0 = len(bg)
    for b in range(48):
        bg.append(lambda b=b: mod_block(1, b))
    bg.append(lambda: (mod_der(1, 0), mod_der(1, 1)))
    bgpos = [0]

    def bg_step(n=1, force=False):
        if not force and not bgslots:
            return
        for _ in range(n):
            if bgpos[0] < len(bg):
                bg[bgpos[0]]()
                bgpos[0] += 1

    def bg_until(pos):
        while bgpos[0] < min(pos, len(bg)):
            bg_step(force=True)

    for l in range(DEPTH):
        lam_init = 0.8 - 0.6 * math.exp(-0.3 * l)
        ol, _ = COLS("L%d_lam" % l)
        ltmp = k.sb("ltmp", [128, 128])
        lred = k.sb("lred", [128, 2])
        tt(ltmp[:, 0:64], cst[:, ol:ol + 64], cst[:, ol + 64:ol + 128], ALU.mult)
        tt(ltmp[:, 64:128], cst[:, ol + 128:ol + 192], cst[:, ol + 192:ol + 256], ALU.mult)
        k.op("dve", "tensor_reduce", [lred[:]], [ltmp[:]], out=lred[:], in_=ltmp[:].rearrange("p (a b) -> p a b", a=2),
             axis=AX.X, op=ALU.add)
        act(lred[:], lred[:], AF.Exp)
        tt(smallv[:, l, 0:1], lred[:, 1:2], lred[:, 0:1], ALU.subtract)
        ts(smallv[:, l, 0:1], smallv[:, l, 0:1], -lam_init, None, ALU.add)
        ts(smallv[:, l, 1:2], ccol("L%d_sub" % l), 1.0 - lam_init, None, ALU.mult)
        ts(smallv[:, l, 2:3], ccol("L%d_dqn" % l), 0.125, None, ALU.mult)
        ts(smallv[:, l, 3:4], ccol("L%d_nqn" % l), 0.125, None, ALU.mult)
        oa, _ = COLS("L%d_alog" % l)
        act(nega[:, l, :], cst[:, oa:oa + 32], AF.Exp)
        ts(nega[:, l, :], nega[:, l, :], -1.0, None, ALU.mult)

    stopped = False
    try:
        ck("setup", mod0=mod_sb[0][:], der=der[:], smallv=smallv[:], nega=nega[:])
    except _Stop:
        stopped = True
    GROUPS = {"ctx": dict(gi=0, segs=[(0, 256), (256, 256), (512, 256), (768, 256)]),
              "lat": dict(gi=1, segs=[(0, 1024)])}

    def run_pass(l, g):
        G = GROUPS[g]
        gi = G["gi"]
        segs = G["segs"]
        nseg = len(segs)
        L = segs[0][1]
        lat = (g == "lat")
        P = "L%d_" % l
        src = xin[g] if l == 0 else xmid2[g]
        dst = xmid2[g] if l == 0 else yout[g]
        WIN = wview(w_in, l)

        def hrhs(kc, hf):
            return h[:, kc, hf * 512:(hf + 1) * 512]

        def rmsnorm_mod(es, Xres, which):
            sq = [k.sbg(es, "sq", [128, 512], BF16) for _ in range(2)]
            tmp = [k.sbg(es, "ntmp", [128, 512]) for _ in range(2)]
            rs = k.sbg(es, "rs", [128, 512])
            sh_base = 0 if which == 0 else 48
            for hf in range(2):
                sl = slice(hf * 512, (hf + 1) * 512)
                for c in range(16):
                    act(sq[c % 2][:], Xres[:, c, sl], AF.Square)
                    mm(PS[6][:], ONES2048, sq[c % 2][:], start=(c == 0), stop=(c == 15))
                rsqrt_to(rs[:], PS[6][:])
                for c in range(16):
                    stt(tmp[c % 2][:], Xres[:, c, sl], der[:, l, gi, which, c:c + 1], rs[:], ALU.mult, ALU.mult)
                    act(h[:, c, sl], tmp[c % 2][:], AF.Identity, bias=mod_sb[l][:, sh_base + c, gi:gi + 1])

        if l == 1:
            bg_until(len(bg))
        with ExitStack() as es:
            Xres = k.sbg(es, "Xres", [128, 16, T])
            for c4 in range(4):
                k.dma("sp", Xres[:, 4 * c4:4 * c4 + 4, :], src[:, 4 * c4:4 * c4 + 4, :])
            rmsnorm_mod(es, Xres, 0)
            k.barrier()
        ck("norm1_%s%d" % (g, l), h=h[:])

        def conv_bufs(es):
            ub = k.sbg(es, "ub", [128, 1032])
            k.op("dve", "memset", [ub[:]], [], ub[:], 0.0)
            accs = [k.sbg(es, "cacc", [128, T]) for _ in range(2)]
            return dict(ub=ub, accs=accs, i=0)

        def conv_evac(cb, banks, wname, bname, ci, nch, out_ap, func):
            ub = cb["ub"]
            ubv = ub[:, 0:nseg * (L + 2)].rearrange("p (s l) -> p s l", s=nseg)
            acc = cb["accs"][cb["i"] % 2]
            cb["i"] += 1
            accv = acc[:].rearrange("p (s l) -> p s l", s=nseg)
            for hf in range(2):
                if lat:
                    act(ub[:, 1 + 512 * hf:513 + 512 * hf], banks[hf][:], AF.Copy)
                else:
                    act(ubv[:, 2 * hf:2 * hf + 2, 1:L + 1], banks[hf][:].rearrange("p (s l) -> p s l", s=2), AF.Copy)
            ow, _ = COLS(P + wname)
            obb, _ = COLS(P + bname)
            w0 = cst[:, ow + 0 * nch + ci:ow + 0 * nch + ci + 1]
            w1 = cst[:, ow + 1 * nch + ci:ow + 1 * nch + ci + 1]
            w2 = cst[:, ow + 2 * nch + ci:ow + 2 * nch + ci + 1]
            bb = cst[:, obb + ci:obb + ci + 1]
            ts(accv, ubv[:, :, 1:L + 1], w1, bb, ALU.mult, ALU.add)
            stt(accv, ubv[:, :, 0:L], w0, accv, ALU.mult, ALU.add)
            stt(accv, ubv[:, :, 2:L + 2], w2, accv, ALU.mult, ALU.add)
            if func is not None:
                act(out_ap, acc[:], func)
            return acc

        with ExitStack() as es2:
            ybr = k.sbg(es2, "ybr", [128, 8, T], BF16)
            sg_all = [k.sbg(es2, "sg", [128, 512]) for _ in range(4)]

            def gated_branch(bi, first):
                WB = wview(w_br[bi], l)
                sg = sg_all
                gcol = C_G + bi * D
                esb = ExitStack()
                bsl = [k.sbg(esb, "bslot", [128, 8, 256], BF16) for _ in range(2)]
                for c0 in range(0, D, 256):
                    sgw = wload(WIN, gcol + c0, 256, 16)
                    sbw = bsl[(c0 // 256) % 2]
                    k.dma("pool", sbw[:], WB[:, :, c0:c0 + 256])
                    for mc in range(2):
                        m = c0 // 128 + mc
                        gb = [PS[0], PS[1]] if m % 2 == 0 else [PS[2], PS[3]]
                        pb = [PS[4], PS[5]] if m % 2 == 0 else [PS[6], PS[7]]
                        for kc in range(16):
                            for hf in range(2):
                                mm(gb[hf][:], sgw[:, kc, mc * 128:(mc + 1) * 128], hrhs(kc, hf), start=(kc == 0), stop=(kc == 15))
                        for kc in range(8):
                            for hf in range(2):
                                mm(pb[hf][:], sbw[:, kc, mc * 128:(mc + 1) * 128], ybr[:, kc, hf * 512:(hf + 1) * 512],
                                   start=(kc == 0), stop=(kc == 7))
                        for hf in range(2):
                            sl = slice(hf * 512, (hf + 1) * 512)
                            sg_ = sg[2 * (m % 2) + hf]
                            act(sg_[:], gb[hf][:], AF.Sigmoid)
                            if first:
                                tt(merged[:, m, sl], sg_[:], pb[hf][:], ALU.mult)
                            else:
                                tt(sg_[:], sg_[:], pb[hf][:], ALU.mult)
                                tt(merged[:, m, sl], sg_[:], merged[:, m, sl], ALU.add)
                    bg_step()
                k.barrier()
                esb.close()

            with ExitStack() as es:
                xbcT = k.sbg(es, "xbcT", [128, 12, T], BF16)
                yT = k.sbg(es, "yT", [128, 8, T], BF16)
                sinr = k.sbg(es, "sinr", [128, 8, 1024], BF16)
                with ExitStack() as esc:
                    cb = conv_bufs(esc)
                    linear_fm(WIN, C_XBC, 1536, 16, hrhs,
                              lambda m, banks: conv_evac(cb, banks, "scw", "scb", m, 12, xbcT[:, m, :], AF.Silu))
                    k.barrier()
                ck("xbc_%s%d" % (g, l), xbcT=xbcT[:])
                wdt = wslot()
                k.dma("pool", wdt[:, :, 0:32], WIN[:, :, C_DT:C_DT + 32])
                for tl in range(8):
                    for kc in range(16):
                        mm(PS[4][:, tl * 32:(tl + 1) * 32], h[:, kc, tl * 128:(tl + 1) * 128], wdt[:, kc, 0:32],
                           start=(kc == 0), stop=(kc == 15))
                dt = k.sbg(es, "dt", [128, 8, 32])
                la = k.sbg(es, "la", [128, 8, 32])
                odt, _ = COLS(P + "dtb")
                tt(dt[:], PS[4][:, 0:256].rearrange("p (a b) -> p a b", a=8),
                   cst[:, odt:odt + 32].unsqueeze(1).broadcast_to([128, 8, 32]), ALU.add)
                act(dt[:], dt[:], AF.Exp)
                act(dt[:], dt[:], AF.Ln, bias=1.0)
                tt(la[:], dt[:], nega[:, l, :].unsqueeze(1).broadcast_to([128, 8, 32]), ALU.mult)
                ck("dt_%s%d" % (g, l), dt=dt[:], la=la[:], xbcT=xbcT[:])

                ess = ExitStack()
                xs_tm = k.sbg(ess, "xs_tm", [128, 1024], BF16)
                b_tm = k.sbg(ess, "b_tm", [128, 256], BF16)
                xdd = k.sbg(ess, "xdd", [128, 1024], BF16)
                xdf = k.sbg(ess, "xdf", [128, 1024], BF16)
                xdr = k.sbg(ess, "xdr", [128, 1024], BF16)
                dec = k.sbg(ess, "dec", [128, 16])
                ea = k.sbg(ess, "ea", [128, 32])
                S = [k.sbg(ess, "Sst%d" % d, [128, 1024]) for d in range(2)]
                Sfb = k.sbg(ess, "Sfb", [128, 1024], BF16)
                bcm = k.sbg(ess, "bcm", [128, 2, 2, 128])
                r1s = [k.sbg(ess, "r1", [128, 512]) for _ in range(2)]
                gts = [[k.sbg(ess, "gt%d" % d, [128, 512], BF16) for d in range(2)] for _ in range(2)]
                css = [[k.sbg(ess, "cs%d" % d, [128, 512], BF16) for d in range(2)] for _ in range(2)]
                etmps = [k.sbg(ess, "etmp", [128, 512]) for _ in range(2)]
                PSB = PS[5][:].bitcast(BF16)

                def to_tm(tl):
                    tsl = slice(tl * 128, (tl + 1) * 128)
                    for c in range(8):
                        k.op("pe", "transpose", [PS[5][:]], [xbcT[:], IDENT], PSB[:, c * 128:(c + 1) * 128], xbcT[:, c, tsl], IDENT)
                    cp(xs_tm[:], PSB)
                    for c in range(2):
                        k.op("pe", "transpose", [PS[5][:]], [xbcT[:], IDENT], PSB[:, c * 128:(c + 1) * 128], xbcT[:, 8 + c, tsl], IDENT)
                    cp(b_tm[:], PSB[:, 0:256])

                def chunk_state(tl, d):
                    lad = la[:, tl, d * 16:(d + 1) * 16]
                    mm(PS[4][:, 0:16], M_GT if d == 0 else M_LT, lad)
                    mm(PS[4][:, 32:64], ONESF, la[:, tl, :])
                    act(dec[:], PS[4][:, 0:16], AF.Exp)
                    act(ea[:], PS[4][:, 32:64], AF.Exp)
                    tt(dec[:], dec[:], dt[:, tl, d * 16:(d + 1) * 16], ALU.mult)
                    tt(xdd[:].rearrange("p (a b) -> p a b", a=16), xs_tm[:].rearrange("p (a b) -> p a b", a=16),
                       dec[:].unsqueeze(2).broadcast_to([128, 16, 64]), ALU.mult)
                    for gg in range(2):
                        mm(PS[6 + gg][:], b_tm[:, gg * 128:(gg + 1) * 128], xdd[:, gg * 512:(gg + 1) * 512])

                def state_update(d):
                    Sv = S[d][:].rearrange("p (a b) -> p a b", a=16)
                    tt(Sv, Sv, ea[:, d * 16:(d + 1) * 16].unsqueeze(2).broadcast_to([128, 16, 64]), ALU.mult)
                    for gg in range(2):
                        tt(S[d][:, gg * 512:(gg + 1) * 512], S[d][:, gg * 512:(gg + 1) * 512], PS[6 + gg][:], ALU.add)

                for si, (t0, Ls) in enumerate(segs):
                    tiles = list(range(t0 // 128, (t0 + Ls) // 128))
                    if lat:
                        k.dma("sp", S[1][:], sst_d[l, :, 1024:2048])
                        k.dma("sp", S[0][:], sst_d[l, :, 0:1024])
                    else:
                        k.op("dve", "memset", [S[1][:]], [], S[1][:], 0.0)
                        k.op("dve", "memset", [S[0][:]], [], S[0][:], 0.0)
                    for tl in reversed(tiles):
                        to_tm(tl)
                        ck("s1", xs_tm=xs_tm[:], b_tm=b_tm[:])
                        cp(sinr[:, tl, :], S[1][:])
                        chunk_state(tl, 1)
                        ck("s2", dec=dec[:], ea=ea[:], xdd=xdd[:])
                        state_update(1)
                        ck("s3", S1=S[1][:])
                    if not lat:
                        k.dma("sp", nss_o[l, si, :, 1024:2048], S[1][:])
                    for tl in tiles:
                        tsl = slice(tl * 128, (tl + 1) * 128)
                        to_tm(tl)
                        cp(Sfb[:], S[0][:])
                        tt(xdf[:].rearrange("p (a b) -> p a b", a=16), xs_tm[:].rearrange("p (a b) -> p a b", a=16),
                           dt[:, tl, 0:16].unsqueeze(2).broadcast_to([128, 16, 64]), ALU.mult)
                        tt(xdr[:].rearrange("p (a b) -> p a b", a=16), xs_tm[:].rearrange("p (a b) -> p a b", a=16),
                           dt[:, tl, 16:32].unsqueeze(2).broadcast_to([128, 16, 64]), ALU.mult)
                        for gg in range(2):
                            mm(PS[4][:, gg * 128:(gg + 1) * 128], xbcT[:, 8 + gg, tsl], xbcT[:, 10 + gg, tsl])
                        for gg in range(2):
                            tt(bcm[:, gg, 0, :], PS[4][:, gg * 128:(gg + 1) * 128], M_LE, ALU.mult)
                            tt(bcm[:, gg, 1, :], PS[4][:, gg * 128:(gg + 1) * 128], M_GE, ALU.mult)
                        ck("s4", bcm=bcm[:], xdf=xdf[:])
                        for b4 in range(4):
                            gg = b4 // 2
                            gt_ = gts[b4 % 2]
                            cs_ = css[b4 % 2]
                            for d in range(2):
                                r1 = r1s[d]
                                etmp = etmps[d]
                                ladb = la[:, tl, d * 16 + 4 * b4:d * 16 + 4 * b4 + 4]
                                tt(r1[:].rearrange("p (a b) -> p a b", a=4),
                                   (M_LE if d == 0 else M_GE).unsqueeze(1).broadcast_to([128, 4, 128]),
                                   ladb.unsqueeze(2).broadcast_to([128, 4, 128]), ALU.mult)
                                mm(PS[0 + 2 * d][:], M_GT if d == 0 else M_LT, r1[:])
                                mm(PS[1 + 2 * d][:], ONESF, r1[:])
                                act(etmp[:], PS[0 + 2 * d][:], AF.Exp)
                                tt(gt_[d][:].rearrange("p (a b) -> p a b", a=4), etmp[:].rearrange("p (a b) -> p a b", a=4),
                                   bcm[:, gg, d, :].unsqueeze(1).broadcast_to([128, 4, 128]), ALU.mult)
                                act(etmp[:], PS[1 + 2 * d][:], AF.Exp)
                                tt(cs_[d][:].rearrange("p (a b) -> p a b", a=4), etmp[:].rearrange("p (a b) -> p a b", a=4),
                                   xbcT[:, 10 + gg, tsl].unsqueeze(1).broadcast_to([128, 4, 128]), ALU.mult)
                            ck("s5", gt0=gt_[0][:], gt1=gt_[1][:], cs0=cs_[0][:], cs1=cs_[1][:], r1=r1[:])
                            for j in range(4):
                                hh = 4 * b4 + j
                                pr, e = hh // 2, hh % 2
                                yo = PS[6 + pr // 4][e * 64:(e + 1) * 64, (pr % 4) * 128:(pr % 4 + 1) * 128]
                                hs = slice(hh * 64, (hh + 1) * 64)
                                js = slice(j * 128, (j + 1) * 128)
                                tp = (0, e * 64)
                                mm(yo, xdf[:, hs], gt_[0][:, js], start=True, stop=False, tile_position=tp)
                                mm(yo, Sfb[:, hs], cs_[0][:, js], start=False, stop=False, tile_position=tp)
                                mm(yo, xdr[:, hs], gt_[1][:, js], start=False, stop=False, tile_position=tp)
                                mm(yo, sinr[:, tl, hs], cs_[1][:, js], start=False, stop=True, tile_position=tp)
                        ck("s6", xdr=xdr[:])
                        osd, _ = COLS(P + "sdd")
                        for pr in range(8):
                            stt(yT[:, pr, tsl], xbcT[:, pr, tsl], cst[:, osd + pr:osd + pr + 1],
                                PS[6 + pr // 4][:, (pr % 4) * 128:(pr % 4 + 1) * 128], ALU.mult, ALU.add)
                        ck("s7", yT=yT[:])
                        chunk_state(tl, 0)
                        state_update(0)
                        ck("s8", S0=S[0][:])
                    if not lat:
                        k.dma("sp", nss_o[l, si, :, 0:1024], S[0][:])
                    ck("seg%d" % si, S0=S[0][:], yT=yT[:])
                ck("scan_%s%d" % (g, l), yT=yT[:], sinr=sinr[:], S0=S[0][:], S1=S[1][:])
                k.barrier()
                ess.close()
                sz = [k.sbg(es, "sz", [128, 512]) for _ in range(2)]

                def z_consume(m, banks):
                    for hf in range(2):
                        sl = slice(hf * 512, (hf + 1) * 512)
                        act(sz[hf][:], banks[hf][:], AF.Silu)
                        tt(yT[:, m, sl], yT[:, m, sl], sz[hf][:], ALU.mult)
                linear_fm(WIN, C_Z, 1024, 16, hrhs, z_consume)
                sqg = [k.sbg(es, "sqg", [128, 512], BF16) for _ in range(2)]
                rsg = k.sbg(es, "rsg", [128, 512])
                osn, _ = COLS(P + "snw")
                for gg in range(2):
                    for hf in range(2):
                        sl = slice(hf * 512, (hf + 1) * 512)
                        for c4 in range(4):
                            act(sqg[c4 % 2][:], yT[:, 4 * gg + c4, sl], AF.Square)
                            mm(PS[6][:], ONES512, sqg[c4 % 2][:], start=(c4 == 0), stop=(c4 == 3))
                        rsqrt_to(rsg[:], PS[6][:])
                        for c4 in range(4):
                            c = 4 * gg + c4
                            stt(ybr[:, c, sl], yT[:, c, sl], cst[:, osn + c:osn + c + 1], rsg[:], ALU.mult, ALU.mult)
                k.barrier()
            ck("ya_%s%d" % (g, l), ybr=ybr[:])
            merged = k.sbg(es2, "merged", [128, 16, T], BF16)
            esbg = ExitStack()
            if bgpos[0] < len(bg):
                bgslots.extend([k.sbg(esbg, "bgslot", [128, 16, 256], BF16) for _ in range(2)])
            gated_branch(0, True)
            k.barrier()
            ck("mA_%s%d" % (g, l), merged=merged[:])

            def attention_branch(diff):
                with ExitStack() as es:
                    nk = 1280 if lat else 1024
                    nkt = nk // 128
                    qT = k.sbg(es, "qT", [128, 8, T], BF16)
                    kT = k.sbg(es, "kT", [128, 8, nk], BF16)
                    v_tm = k.sbg(es, "v_tm", [128, nkt, 1024], BF16)
                    nsq = [k.sbg(es, "nsq", [128, 512], BF16) for _ in range(2)]
                    nrs = k.sbg(es, "nrs", [128, 512])
                    sti = [0]
                    cq_, ck_, cv_ = (C_QD, C_KD, C_VD) if diff else (C_QN, C_KN, C_VN)
                    wq = smallv[:, l, 2:3] if diff else smallv[:, l, 3:4]
                    wk = ccol(P + ("dkn" if diff else "nkn"))
                    ko_d = ndk_o if diff else nnk_o
                    vo_d = ndv_o if diff else nnv_o
                    rope = lat and diff
                    if rope:
                        rope_sb = k.sbg(es, "rope_sb", [128, 2, T])
                        pmT = k.sbg(es, "pmT", [128, 128])
                        k.dma("sp", rope_sb[:], rope_d)
                        k.dma("sp", pmT[:], pmt_d)
                        rtmp = k.sbg(es, "rtmp", [128, 512])
                    if lat:
                        k.dma("pool", kT[:, :, 1024:1280], (cdk_d if diff else cnk_d)[l])
                        k.dma("pool", v_tm[:, 8:10, :], (cdv_d if diff else cnv_d)[l])

                    esq = ExitStack()
                    stage = [k.sbg(esq, "stage", [128, 512]) for _ in range(4)]
                    nrs2 = [nrs, k.sbg(esq, "nrs2", [128, 512])]
                    if rope:
                        rtmps = [rtmp, k.sbg(esq, "rtmp2", [128, 512])]

                    def qk_consume(dest, wcol, is_k):
                        pend_rope = []

                        def f(m, banks):
                            sts = []
                            for hf in range(2):
                                st = stage[sti[0] % 4]
                                sti[0] += 1
                                sts.append(st)
                                act(nsq[hf][:], banks[hf][:], AF.Square)
                                act(st[:], banks[hf][:], AF.Copy)
                            while pend_rope:
                                pend_rope.pop(0)()
                            for hf in range(2):
                                mm(PS[6 + hf][:], BLK64, nsq[hf][:])
                            for hf in range(2):
                                rsqrt_to(nrs2[hf][:], PS[6 + hf][:])
                                stt(sts[hf][:], sts[hf][:], wcol, nrs2[hf][:], ALU.mult, ALU.mult)
                            for hf in range(2):
                                sl = slice(hf * 512, (hf + 1) * 512)
                                st = sts[hf]
                                if rope:
                                    def rp(st=st, hf=hf, sl=sl, m=m):
                                        mm(PS[6 + hf][:], pmT[:], st[:])
                                        tt(rtmps[hf][:], PS[6 + hf][:], rope_sb[:, 1, sl], ALU.mult)
                                        tt(st[:], st[:], rope_sb[:, 0, sl], ALU.mult)
                                        tt(dest[:, m, sl], st[:], rtmps[hf][:], ALU.add)
                                    pend_rope.append(rp)
                                else:
                                    act(dest[:, m, sl], st[:], AF.Copy)
                                    if is_k and not lat:
                                        k.dma("sp", ko_d[l, :, m, sl], st[:])

                        def flush():
                            while pend_rope:
                                pend_rope.pop(0)()
                        f.flush = flush
                        return f
                    cf = qk_consume(qT, wq, False)
                    linear_fm(WIN, cq_, 1024, 16, hrhs, cf)
                    cf.flush()
                    ck("q1", qT=qT[:])
                    cf = qk_consume(kT, wk, True)
                    linear_fm(WIN, ck_, 1024, 16, hrhs, cf)
                    cf.flush()
                    ck("q2", kT=kT[:])
                    esv = ExitStack()
                    vT = [k.sbg(esv, "vT", [128, T], BF16) for _ in range(2)]
                    PSBv = PS[6][:].bitcast(BF16)

                    def v_consume(m, banks):
                        vt_ = vT[m % 2]
                        for hf in range(2):
                            sl = slice(hf * 512, (hf + 1) * 512)
                            if not lat:
                                st = stage[sti[0] % 4]
                                sti[0] += 1
                                act(st[:], banks[hf][:], AF.Copy)
                                k.dma("sp", vo_d[l, :, m, sl], st[:])
                                cp(vt_[:, sl], st[:])
                            else:
                                act(vt_[:, sl], banks[hf][:], AF.Copy)
                        for tl in range(8):
                            k.op("pe", "transpose", [PS[6][:]], [vt_[:], IDENT], PSBv[:, tl * 128:(tl + 1) * 128],
                                 vt_[:, tl * 128:(tl + 1) * 128], IDENT)
                        cp(v_tm[:, 0:8, m * 128:(m + 1) * 128], PSBv.rearrange("p (a b) -> p a b", a=8))
                    linear_fm(WIN, cv_, 1024, 16, hrhs, v_consume)
                    k.barrier()
                    esv.close()
                    esq.close()
                    ck("qkv%d_%s%d" % (int(diff), g, l), qT=qT[:], kT=kT[:], v_tm=v_tm[:])
                    pT = [k.sbg(es, "pT", [128, 512], BF16) for _ in range(4)]
                    pti = [0]
                    rd = [k.sbg(es, "rd", [128, 512]) for _ in range(2 if diff else 1)]
                    if diff:
                        oa = k.sbg(es, "oa", [128, 512])
                        ob2 = k.sbg(es, "ob2", [128, 512])
                    if lat:
                        qblocks = [(hf * 512, 512, list(range(10))) for hf in range(2)]
                    else:
                        qblocks = [(s0, Ls, [s0 // 128, s0 // 128 + 1]) for (s0, Ls) in segs]
                    if lat and not diff:
                        tbs = [k.sbg(es, "tbs", [128, 1920], BF16) for _ in range(4)]
                        rowoh = k.sbg(es, "rowoh", [16, 8, 128], BF16)
                        rowpen = k.sbg(es, "rowpen", [16, T], BF16)
                        k.dma("pool", rowoh[:], rowoh_d)
                        k.dma("pool", rowpen[:], rowpen_d)
                    sbi = [0]

                    def sbank():
                        b = PS[sbi[0] % 4]
                        sbi[0] += 1
                        return b
                    LOOK = 2

                    def run_pipe(steps, score_fn, consume_fn):
                        banks = {}
                        n = len(steps)
                        for i in range(min(LOOK, n)):
                            banks[i] = score_fn(steps[i])
                        for i in range(n):
                            if i + LOOK < n:
                                banks[i + LOOK] = score_fn(steps[i + LOOK])
                            consume_fn(steps[i], banks.pop(i))

                    if diff:
                        steps = []
                        for hh in range(8):
                            for (q0, nq, ktl) in qblocks:
                                for ji, j in enumerate(ktl):
                                    for t2 in range(2):
                                        steps.append((hh, q0, nq, ji, j, t2, len(ktl)))

                        def score_fn(st_):
                            hh, q0, nq, ji, j, t2, nkt_ = st_
                            ps_ = slice(t2 * 64, (t2 + 1) * 64)
                            sb_ = sbank()
                            mm(sb_[:, 0:nq], kT[ps_, hh, j * 128:(j + 1) * 128], qT[ps_, hh, q0:q0 + nq])
                            return sb_

                        def consume_fn(st_, sb_):
                            hh, q0, nq, ji, j, t2, nkt_ = st_
                            qs = slice(q0, q0 + nq)
                            p_ = pT[pti[0] % 4]
                            pti[0] += 1
                            act(p_[:, 0:nq], sb_[:, 0:nq], AF.Exp)
                            mm(PS[4 + 2 * t2][:, 0:nq], v_tm[:, j, hh * 128:(hh + 1) * 128], p_[:, 0:nq],
                               start=(ji == 0), stop=(ji == nkt_ - 1))
                            mm(PS[5 + 2 * t2][:, 0:nq], ONES, p_[:, 0:nq], start=(ji == 0), stop=(ji == nkt_ - 1))
                            if not (ji == nkt_ - 1 and t2 == 1):
                                return
                            for t3 in range(2):
                                act(rd[t3][:, 0:nq], PS[5 + 2 * t3][:, 0:nq], AF.Ln)
                                act(rd[t3][:, 0:nq], rd[t3][:, 0:nq], AF.Exp, scale=-1.0)
                            tt(oa[:, 0:nq], PS[4][:, 0:nq], rd[0][:, 0:nq], ALU.mult)
                            tt(ob2[:, 0:nq], PS[6][:, 0:nq], rd[1][:, 0:nq], ALU.mult)
                            stt(oa[:, 0:nq], ob2[:, 0:nq], smallv[:, l, 0:1], oa[:, 0:nq], ALU.mult, ALU.add)
                            act(nsq[0][:, 0:nq], oa[:, 0:nq], AF.Square)
                            mm(PS[5][:, 0:nq], ONES128, nsq[0][:, 0:nq])
                            rsqrt_to(nrs[:, 0:nq], PS[5][:, 0:nq])
                            stt(ybr[:, hh, qs], oa[:, 0:nq], smallv[:, l, 1:2], nrs[:, 0:nq], ALU.mult, ALU.mult)
                        run_pipe(steps, score_fn, consume_fn)
                    else:
                        if lat:
                            qblocks = [(0, 512, [0, 1, 2, 3, 4, 5, 8, 9]), (512, 512, [2, 3, 4, 5, 6, 7, 8, 9])]
                        steps = []
                        for hp in range(8):
                            for (q0, nq, ktl) in qblocks:
                                for ji, j in enumerate(ktl):
                                    for e in range(2):
                                        steps.append((hp, q0, nq, ji, j, e, len(ktl)))
                        tb_loaded = set()

                        def score_fn(st_):
                            hp, q0, nq, ji, j, e, nkt_ = st_
                            ps_ = slice(e * 64, (e + 1) * 64)
                            hf = q0 // 512
                            local = lat and j < 8
                            if local and hp not in tb_loaded:
                                tb_loaded.add(hp)
                                for e2 in range(2):
                                    k.dma("pool", tbs[2 * (hp % 2) + e2][:], tb_d[l, 2 * hp + e2])
                            sb_ = sbank()
                            mm(sb_[:, 0:nq], kT[ps_, hp, j * 128:(j + 1) * 128], qT[ps_, hp, q0:q0 + nq], start=True, stop=(not local))
                            if local:
                                i0 = 14 - 2 * j + 8 * hf
                                mm(sb_[:, 0:nq], IDENT, tbs[2 * (hp % 2) + e][:, i0 * 64:i0 * 64 + 512], start=False, stop=False)
                                mm(sb_[:, 0:nq], rowoh[:, j, :], rowpen[:, q0:q0 + nq], start=False, stop=True)
                            return sb_

                        def consume_fn(st_, sb_):
                            hp, q0, nq, ji, j, e, nkt_ = st_
                            ps_ = slice(e * 64, (e + 1) * 64)
                            p_ = pT[pti[0] % 4]
                            pti[0] += 1
                            act(p_[:, 0:nq], sb_[:, 0:nq], AF.Exp)
                            hh = 2 * hp + e
                            tp = (0, e * 64)
                            mm(PS[4][ps_, 0:nq], v_tm[:, j, hh * 64:(hh + 1) * 64], p_[:, 0:nq],
                               start=(ji == 0), stop=(ji == nkt_ - 1), tile_position=tp)
                            mm(PS[5][ps_, 0:nq], ONES[:, 0:64], p_[:, 0:nq],
                               start=(ji == 0), stop=(ji == nkt_ - 1), tile_position=tp)
                            if not (ji == nkt_ - 1 and e == 1):
                                return
                            act(rd[0][:, 0:nq], PS[5][:, 0:nq], AF.Ln)
                            act(rd[0][:, 0:nq], rd[0][:, 0:nq], AF.Exp, scale=-1.0)
                            tt(ybr[:, hp, q0:q0 + nq], PS[4][:, 0:nq], rd[0][:, 0:nq], ALU.mult)
                        run_pipe(steps, score_fn, consume_fn)
                    k.barrier()

            attention_branch(True)
            ck("yb_%s%d" % (g, l), ybr=ybr[:])
            gated_branch(1, False)
            k.barrier()
            attention_branch(False)
            ck("yc_%s%d" % (g, l), ybr=ybr[:])
            gated_branch(2, False)
            k.barrier()
            ck("merged_%s%d" % (g, l), merged=merged[:])

            bg_until(BG_L0)
            k.barrier()
            del bgslots[:]
            esbg.close()
            with ExitStack() as es:
                Xres = k.sbg(es, "Xres2", [128, 16, T])
                xi = [k.sbg(es, "xi", [128, T]) for _ in range(2)]
                og1 = 32

                def wout_consume(m, banks):
                    x_ = xi[m % 2]
                    k.dma("sp", x_[:], src[:, m, :])
                    for hf in range(2):
                        sl = slice(hf * 512, (hf + 1) * 512)
                        stt(Xres[:, m, sl], banks[hf][:], mod_sb[l][:, og1 + m, gi:gi + 1], x_[:, sl], ALU.mult, ALU.add)
                linear_fm(wview(w_out, l), 0, D, 16, lambda kc, hf: merged[:, kc, hf * 512:(hf + 1) * 512], wout_consume)
                for c4 in range(4):
                    k.dma("sp", xmid1[g][:, 4 * c4:4 * c4 + 4, :], Xres[:, 4 * c4:4 * c4 + 4, :])
                rmsnorm_mod(es, Xres, 1)
                k.barrier()
                ck("x1_%s%d" % (g, l), Xres=Xres[:], h=h[:])

        with ExitStack() as es:
            aT = k.sbg(es, "aT", [128, 44, T], BF16)
            esf = ExitStack()
            if bgpos[0] < len(bg):
                bgslots.extend([k.sbg(esf, "bgslot", [128, 16, 256], BF16) for _ in range(2)])
            cb = conv_bufs(esf)
            vt = [k.sbg(esf, "vt", [128, T]) for _ in range(2)]
            gtm = k.sbg(esf, "gtm", [128, T])
            WUP = wview(w_up, l)
            for fp in range(22):
                sv = wload(WUP, fp * 256, 256, 16)
                sgt = wload(WUP, DFF + fp * 256, 256, 16)
                for mc in range(2):
                    f = 2 * fp + mc
                    banks = bankpair()
                    for kc in range(16):
                        for hf in range(2):
                            mm(banks[hf][:], sv[:, kc, mc * 128:(mc + 1) * 128], hrhs(kc, hf), start=(kc == 0), stop=(kc == 15))
                    conv_evac(cb, banks, "fcw", "fcb", f, 88, vt[mc][:], AF.Identity)
                for mc in range(2):
                    f = 2 * fp + mc
                    banks = bankpair()
                    for kc in range(16):
                        for hf in range(2):
                            mm(banks[hf][:], sgt[:, kc, mc * 128:(mc + 1) * 128], hrhs(kc, hf), start=(kc == 0), stop=(kc == 15))
                    conv_evac(cb, banks, "fcw", "fcb", 44 + f, 88, gtm[:], AF.Silu)
                    tt(aT[:, f, :], vt[mc][:], gtm[:], ALU.mult)
                bg_step(2)
            bg_until(len(bg))
            k.barrier()
            del bgslots[:]
            esf.close()
            ck("aT_%s%d" % (g, l), aT=aT[:])
            dsl_own = k.sbg(es, "dslot", [128, 44, 256], BF16)
            dsl_h = h[:, 0:11, :].rearrange("p a (b c) -> p (a b) c", c=256)
            xi = [k.sbg(es, "xi2", [128, T]) for _ in range(2)]
            xo = [k.sbg(es, "xo", [128, T]) for _ in range(2)]
            WDN = wview(w_dn, l)
            og2 = 80
            for mb in range(8):
                ds_ = dsl_own[:] if mb % 2 == 0 else dsl_h
                for k0 in range(0, 44, 11):
                    k.dma("pool", ds_[:, k0:k0 + 11, :], WDN[:, k0:k0 + 11, mb * 256:(mb + 1) * 256])
                for mc in range(2):
                    m = 2 * mb + mc
                    x_ = xi[m % 2]
                    k.dma("sp", x_[:], xmid1[g][:, m, :])
                    banks = bankpair()
                    for kc in range(44):
                        for hf in range(2):
                            mm(banks[hf][:], ds_[:, kc, mc * 128:(mc + 1) * 128], aT[:, kc, hf * 512:(hf + 1) * 512],
                               start=(kc == 0), stop=(kc == 43))
                    o_ = xo[m % 2]
                    for hf in range(2):
                        sl = slice(hf * 512, (hf + 1) * 512)
                        stt(o_[:, sl], banks[hf][:], mod_sb[l][:, og2 + m, gi:gi + 1], x_[:, sl], ALU.mult, ALU.add)
                    k.dma("sp", dst[:, m, :], o_[:])
            k.barrier()
        if stop == "pass_%s%d" % (g, l):
            with ExitStack() as es:
                rb = k.sbg(es, "rb", [128, 16, T])
                k.dma("sp", rb[:], dst)
                ck("pass_%s%d" % (g, l), x2=rb[:])

    try:
        for g in ("ctx", "lat"):
            for l in range(DEPTH):
                if not stopped and (only is None or (g, l) in only):
                    run_pass(l, g)
    except _Stop:
        pass
    k.finish()
    print("program: %d instructions, %d dma sems" % (nc.n_instructions(), len(k.dsems)))
    return nc


def _fm(v):
    v = np.asarray(v, np.float32)
    return np.ascontiguousarray(v.reshape(-1, 128).T)


def _structural():
    idx = np.arange(128)
    kk, ll = idx[:, None], idx[None, :]
    matb = np.zeros((128, N_MATS, 128), np.float32)
    matb[:, 0] = np.eye(128)
    matb[:, 1] = 1.0
    matb[:, 2] = 1.0 / 2048
    matb[:, 3] = 1.0 / 512
    matb[:, 4] = 1.0 / 128
    matb[:, 5] = ((kk // 64) == (ll // 64)) / 64.0
    matf = np.zeros((128, N_MATF, 128), np.float32)
    matf[:, 0] = kk <= ll
    matf[:, 1] = kk >= ll
    matf[:, 2] = kk < ll
    matf[:, 3] = kk > ll
    matf[:, 4] = 1.0
    t = np.arange(T)
    row = (t // 64).astype(np.float32)
    col = (t % 64).astype(np.float32)
    inv = (10000.0 ** (-np.arange(16, dtype=np.float32) / 16)).astype(np.float32)
    rope = np.zeros((128, 2, T), np.float32)
    pm = np.zeros((128, 128), np.float32)
    for p in range(128):
        d = p % 64
        pos = row if d < 32 else col
        ang = pos * inv[d % 16]
        rope[p, 0] = np.cos(ang)
        rope[p, 1] = np.sin(ang)
        if d % 32 < 16:
            pm[p, p + 16] = -1.0
        else:
            pm[p, p - 16] = 1.0
    pmT = np.ascontiguousarray(pm.T)
    rowoh = np.zeros((16, 8, 128), np.float32)
    for j in range(8):
        for kx in range(128):
            rowoh[2 * j + kx // 64, j, kx] = 1.0
    rowpen = np.zeros((16, T), np.float32)
    for qr in range(16):
        w0 = min(max(qr - 4, 0), 8)
        for kr in range(16):
            if not (w0 <= kr < w0 + 8):
                rowpen[kr, qr * 64:(qr + 1) * 64] = NEG
    return matb, matf, rope, pmT, rowoh, rowpen


def _na_tables(rpb):
    kc = np.arange(64)[:, None]
    qc = np.arange(64)[None, :]
    wst = np.clip(qc - 8, 0, 48)
    colok = (kc >= wst) & (kc < wst + 16)
    dc = np.clip(kc - qc + 15, 0, 30)
    out = np.full((DEPTH, 16, 2, 64, 30, 64), NEG, np.float32)
    for krl in range(2):
        for i in range(30):
            a = 21 - i + krl
            if 0 <= a <= 14:
                blk = rpb[:, :, a][:, :, dc]
                out[:, :, krl, :, i, :] = np.where(colok[None, None], blk, NEG)
    return np.ascontiguousarray(out.reshape(DEPTH, 16, 128, 1920))


_PROG = {}


def kernel(**inp):
    f32 = lambda a: np.ascontiguousarray(np.asarray(a, np.float32))
    x_prompt, x_sample = f32(inp["x_prompt"]), f32(inp["x_sample"])
    matb, matf, rope, pmT, rowoh, rowpen = _structural()
    tb = _na_tables(f32(inp["na_rpb"]))
    shared = {}
    for nm in ["w_ada", "w_in", "w_branch_a", "w_branch_b", "w_branch_c", "w_out", "ffn_w_up", "ffn_w_down"]:
        shared[nm] = f32(inp[nm])
    shared.update(matb=matb, matf=matf, rope=rope, pmT=pmT, rowoh=rowoh, rowpen=rowpen, tb=tb)

    def to_fm(x):
        return np.ascontiguousarray(x.reshape(T, 16, 128).transpose(2, 1, 0))

    in_maps = []
    for ci in range(NCORES):
        sb = ci // 4
        m = dict(shared)
        m["xp"] = to_fm(x_prompt[4 * ci:4 * ci + 4].reshape(T, D))
        m["xs"] = to_fm(x_sample[sb])
        cst = np.zeros((128, COLS.n), np.float32)

        def put(name, arr):
            o, w = COLS(name)
            assert arr.shape == (128, w), (name, arr.shape, w)
            cst[:, o:o + w] = arr
        cs = np.stack([_fm(inp["c_ctx"]), _fm(inp["c"][sb])], axis=2)
        put("csil", cs.reshape(128, 32))
        for l in range(DEPTH):
            p = "L%d_" % l
            put(p + "n1w", _fm(inp["norm1_w"][l]))
            put(p + "n2w", _fm(inp["norm2_w"][l]))
            put(p + "bada", _fm(inp["b_ada"][l]))
            put(p + "scw", np.concatenate([_fm(inp["ssd_conv_w"][l][j]) for j in range(3)], axis=1))
            put(p + "scb", _fm(inp["ssd_conv_b"][l]))
            put(p + "fcw", np.concatenate([_fm(inp["ffn_conv_w"][l][j]) for j in range(3)], axis=1))
            put(p + "fcb", _fm(inp["ffn_conv_b"][l]))
            put(p + "snw", _fm(inp["ssd_norm_w"][l]))
            put(p + "sdd", _fm(np.repeat(f32(inp["ssd_d"][l]), 64)))
            rep2 = lambda v: np.tile(f32(v), 2).reshape(128, 1)
            put(p + "dqn", rep2(inp["diff_q_norm"][l]))
            put(p + "dkn", rep2(inp["diff_k_norm"][l]))
            put(p + "sub", f32(inp["diff_subln_w"][l]).reshape(128, 1))
            put(p + "nqn", rep2(inp["na_q_norm"][l]))
            put(p + "nkn", rep2(inp["na_k_norm"][l]))
            put(p + "dtb", np.tile(f32(inp["ssd_dt_bias"][l]).reshape(1, 32), (128, 1)))
            put(p + "alog", np.tile(f32(inp["ssd_a_log"][l]).reshape(1, 32), (128, 1)))
            put(p + "lam", np.tile(f32(inp["diff_lam"][l]).reshape(1, 256), (128, 1)))
        m["cst"] = cst
        cdk = f32(inp["cache_diff_k"][sb])
        m["cdk"] = np.ascontiguousarray(cdk.transpose(0, 3, 4, 2, 1).reshape(DEPTH, 128, 8, 256))
        cdv = f32(inp["cache_diff_v"][sb])
        m["cdv"] = np.ascontiguousarray(cdv.reshape(DEPTH, 2, 128, 1024).transpose(0, 2, 1, 3))
        cnk = f32(inp["cache_na_k"][sb])
        m["cnk"] = np.ascontiguousarray(cnk.reshape(DEPTH, 256, 8, 128).transpose(0, 3, 2, 1))
        cnv = f32(inp["cache_na_v"][sb])
        m["cnv"] = np.ascontiguousarray(cnv.reshape(DEPTH, 2, 128, 1024).transpose(0, 2, 1, 3))
        sst = f32(inp["state_ssm"][sb])
        m["sst"] = np.ascontiguousarray(sst.transpose(0, 4, 1, 2, 3).reshape(DEPTH, 128, 2048))
        in_maps.append(m)

    if "nc" not in _PROG:
        _PROG["nc"] = build_program()
    res = run_bass_kernel_spmd(_PROG["nc"], in_maps, core_ids=list(range(NCORES)))
    R = res.results

    def from_fm(a):
        return a.transpose(2, 1, 0).reshape(T, D)

    B, S = 32, 256
    y_p = np.zeros((B, S, D), np.float32)
    y_s = np.zeros((2, 1024, D), np.float32)
    ndk = np.zeros((B, DEPTH, S, 8, 2, 64), np.float32)
    ndv = np.zeros((B, DEPTH, S, 8, 128), np.float32)
    nnk = np.zeros((B, DEPTH, S, 16, 64), np.float32)
    nnv = np.zeros((B, DEPTH, S, 16, 64), np.float32)
    nss = np.zeros((B, DEPTH, 2, 16, 64, 128), np.float32)
    for ci in range(NCORES):
        r = R[ci]
        y_p[4 * ci:4 * ci + 4] = from_fm(np.asarray(r["ypT"])).reshape(4, S, D)
        if ci % 4 == 0:
            y_s[ci // 4] = from_fm(np.asarray(r["ysT"]))
        kd = np.asarray(r["ndk"])
        ndk[4 * ci:4 * ci + 4] = kd.reshape(DEPTH, 2, 64, 8, 4, S).transpose(4, 0, 5, 3, 1, 2)
        vd = np.asarray(r["ndv"])
        ndv[4 * ci:4 * ci + 4] = vd.reshape(DEPTH, 128, 8, 4, S).transpose(3, 0, 4, 2, 1)
        kn = np.asarray(r["nnk"])
        nnk[4 * ci:4 * ci + 4] = kn.reshape(DEPTH, 2, 64, 8, 4, S).transpose(4, 0, 5, 3, 1, 2).reshape(4, DEPTH, S, 16, 64)
        vn = np.asarray(r["nnv"])
        nnv[4 * ci:4 * ci + 4] = vn.reshape(DEPTH, 2, 64, 8, 4, S).transpose(4, 0, 5, 3, 1, 2).reshape(4, DEPTH, S, 16, 64)
        ss = np.asarray(r["nss"])
        nss[4 * ci:4 * ci + 4] = ss.reshape(DEPTH, 4, 128, 2, 16, 64).transpose(1, 0, 3, 4, 5, 2)
    return (y_p, y_s, ndk, ndv, nnk, nnv, nss)
```

```python
import math
from contextlib import ExitStack
import numpy as np
import concourse.bass as bass
import concourse.mybir as mybir
from concourse.bass_utils import run_bass_kernel_spmd

F32 = mybir.dt.float32
BF16 = mybir.dt.bfloat16
AF = mybir.ActivationFunctionType
ALU = mybir.AluOpType
AX = mybir.AxisListType

D = 2048
T = 1024
EPS = 1e-6
NCORES = 8
DEPTH = 2
IN_COLS = 14880
DFF = 5632
C_Z, C_XBC, C_DT, C_QD, C_KD, C_VD, C_QN, C_KN, C_VN, C_G = 0, 1024, 2560, 2592, 3616, 4640, 5664, 6688, 7712, 8736
NEG = -30000.0


class _Rec:
    __slots__ = ("w", "r")

    def __init__(self):
        self.w = None
        self.r = {}


class KB:
    def __init__(self):
        self.nc = bass.Bass("TRN2", target_bir_lowering=False)
        nc = self.nc
        self.eng = {"pe": nc.tensor, "act": nc.scalar, "dve": nc.vector, "pool": nc.gpsimd, "sp": nc.sync}
        self.sems = {e: nc.alloc_semaphore("sem_" + e) for e in self.eng}
        self.cnt = {e: 0 for e in self.eng}
        self.waited = {e: {} for e in self.eng}
        self.recs = {}
        self.dsems = []
        self.dma_of = {}
        self.free_d = []
        self.tracked = set()
        self.dram = set()
        self.uid = 0

    def name(self, base):
        self.uid += 1
        return "%s_%d" % (base, self.uid)

    def track(self, t):
        self.tracked.add(t.name)
        return t

    def sb(self, name, shape, dtype=F32):
        return self.track(self.nc.alloc_sbuf_tensor(self.name(name), list(shape), dtype))

    def sbg(self, es, name, shape, dtype=F32):
        t = es.enter_context(self.nc.sbuf_tensor(self.name(name), list(shape), dtype))
        return self.track(t)

    def ps(self, name, shape, dtype=F32):
        return self.track(self.nc.alloc_psum_tensor(name, list(shape), dtype))

    def _recs_for(self, tn):
        if tn not in self.recs:
            self.recs[tn] = _Rec()
        return self.recs[tn]

    def _split(self, outs, ins):
        rd, wr = [], []
        for a in ins:
            if a is None or isinstance(a, (int, float)):
                continue
            tn = a.tensor.name
            if tn in self.tracked:
                rd.append(tn)
        for a in outs:
            tn = a.tensor.name
            if tn in self.tracked:
                wr.append(tn)
        return rd, wr

    def _deps(self, rd, wr):
        deps = {}

        def need(s, v):
            if deps.get(s, 0) < v:
                deps[s] = v
        for tn in rd:
            rec = self._recs_for(tn)
            if rec.w is not None:
                need(*rec.w)
            if tn.startswith("psb"):
                for s, v in rec.r.items():
                    need(s, v)
        for tn in wr:
            rec = self._recs_for(tn)
            if rec.w is not None:
                need(*rec.w)
            for s, v in rec.r.items():
                need(s, v)
        return deps

    def _semobj(self, s):
        if s in self.sems:
            return self.sems[s]
        return self.dsems[int(s[1:])][0]

    def _waits(self, e, deps):
        for s, v in deps.items():
            if e == "pe" and s == "pe":
                continue
            if self.waited[e].get(s, 0) >= v:
                continue
            self.eng[e].wait_ge(self._semobj(s), v)
            self.waited[e][s] = v

    def _update(self, rd, wr, s, v):
        for tn in rd:
            self.recs[tn].r[s] = v
        for tn in wr:
            rec = self.recs[tn]
            rec.w = (s, v)
            rec.r = {}

    def op(self, e, fname, outs, ins, *args, **kw):
        rd, wr = self._split(outs, ins)
        self._waits(e, self._deps(rd, wr))
        i = getattr(self.eng[e], fname)(*args, **kw)
        self.cnt[e] += 1
        i.then_inc(self.sems[e], 1)
        self._update(rd, wr, e, self.cnt[e])
        return i

    def dma(self, q, out, in_):
        otn, itn = out.tensor.name, in_.tensor.name
        rd, wr = [], []
        if itn in self.tracked:
            rd.append(itn)
        if otn in self.tracked:
            wr.append(otn)
        if otn in self.tracked and otn not in self.dram:
            sbn = otn
        else:
            sbn = itn
        self._waits(q, self._deps(rd, wr))
        idx = self.dma_of.get(sbn)
        if idx is None:
            if self.free_d:
                idx = self.free_d.pop()
            else:
                idx = len(self.dsems)
                self.dsems.append([self.nc.alloc_semaphore("dsem%d" % idx), 0])
            self.dma_of[sbn] = idx
        ds = self.dsems[idx]
        i = self.eng[q].dma_start(out=out, in_=in_)
        ds[1] += 16
        i.then_inc(ds[0], 16)
        self._update(rd, wr, "D%d" % idx, ds[1])
        return i

    def barrier(self):
        for e in self.eng:
            deps = {}
            for f in self.eng:
                if f != e and self.cnt[f] > 0:
                    deps[f] = self.cnt[f]
            for i, (s, c) in enumerate(self.dsems):
                if c > 0:
                    deps["D%d" % i] = c
            self._waits(e, deps)
        for rec in self.recs.values():
            rec.w = None
            rec.r = {}
        self.dma_of = {}
        self.free_d = list(range(len(self.dsems)))

    def finish(self):
        for i, (s, c) in enumerate(self.dsems):
            if c > 0 and self.waited["sp"].get("D%d" % i, 0) < c:
                self.eng["sp"].wait_ge(s, c)


class Cols:
    def __init__(self):
        self.off = {}
        self.n = 0

    def add(self, name, w):
        self.off[name] = (self.n, w)
        self.n += w

    def __call__(self, name):
        return self.off[name]


def make_cols():
    c = Cols()
    c.add("csil", 32)
    for l in range(DEPTH):
        p = "L%d_" % l
        for nm, w in [("n1w", 16), ("n2w", 16), ("bada", 96), ("scw", 36), ("scb", 12), ("fcw", 264), ("fcb", 88),
                      ("snw", 8), ("sdd", 8), ("dqn", 1), ("dkn", 1), ("sub", 1), ("nqn", 1), ("nkn", 1),
                      ("dtb", 32), ("alog", 32), ("lam", 256)]:
            c.add(p + nm, w)
    return c


COLS = make_cols()
N_MATS = 6
N_MATF = 5


class _Stop(Exception):
    pass


MARKS = []


def build_program(stop=None, only=None):
    k = KB()
    nc = k.nc
    taps = {}

    def ck(label, **aps):
        MARKS.append((label, k.cnt["pe"]))
        if stop != label:
            return
        k.barrier()
        for nm, ap in aps.items():
            d = nc.dram_tensor("dbg_" + nm, list(ap.shape), ap.dtype, kind="ExternalOutput").ap()
            k.dma("sp", d, ap)
            taps[nm] = d
        raise _Stop()

    def din(name, shape, dt=F32):
        return nc.dram_tensor(name, list(shape), dt, kind="ExternalInput").ap()

    def dout(name, shape, dt=F32):
        return nc.dram_tensor(name, list(shape), dt, kind="ExternalOutput").ap()

    def dscr(name, shape, dt=F32):
        t = nc.dram_tensor(name, list(shape), dt, kind="Internal")
        k.tracked.add(t.name)
        k.dram.add(t.name)
        return t.ap()

    xin = {"ctx": din("xp", [128, 16, T]), "lat": din("xs", [128, 16, T])}
    yout = {"ctx": dout("ypT", [128, 16, T]), "lat": dout("ysT", [128, 16, T])}
    cst_d = din("cst", [128, COLS.n])
    matb_d = din("matb", [128, N_MATS, 128])
    matf_d = din("matf", [128, N_MATF, 128])
    rope_d = din("rope", [128, 2, T])
    pmt_d = din("pmT", [128, 128])
    tb_d = din("tb", [DEPTH, 16, 128, 1920])
    rowoh_d = din("rowoh", [16, 8, 128])
    rowpen_d = din("rowpen", [16, T])
    cdk_d = din("cdk", [DEPTH, 128, 8, 256])
    cdv_d = din("cdv", [DEPTH, 128, 2, 1024])
    cnk_d = din("cnk", [DEPTH, 128, 8, 256])
    cnv_d = din("cnv", [DEPTH, 128, 2, 1024])
    sst_d = din("sst", [DEPTH, 128, 2048])
    w_ada = din("w_ada", [DEPTH, D, 6 * D])
    w_in = din("w_in", [DEPTH, D, IN_COLS])
    w_br = [din("w_branch_a", [DEPTH, 1024, D]), din("w_branch_b", [DEPTH, 1024, D]), din("w_branch_c", [DEPTH, 1024, D])]
    w_out = din("w_out", [DEPTH, D, D])
    w_up = din("ffn_w_up", [DEPTH, D, 2 * DFF])
    w_dn = din("ffn_w_down", [DEPTH, DFF, D])
    ndk_o = dout("ndk", [DEPTH, 128, 8, T])
    ndv_o = dout("ndv", [DEPTH, 128, 8, T])
    nnk_o = dout("nnk", [DEPTH, 128, 8, T])
    nnv_o = dout("nnv", [DEPTH, 128, 8, T])
    nss_o = dout("nss", [DEPTH, 4, 128, 2048])
    xmid1 = {g: dscr("xmid1_" + g, [128, 16, T]) for g in ("ctx", "lat")}
    xmid2 = {g: dscr("xmid2_" + g, [128, 16, T]) for g in ("ctx", "lat")}

    cst = k.sb("cst", [128, COLS.n])
    matb = k.sb("matb", [128, N_MATS, 128], BF16)
    matf = k.sb("matf", [128, N_MATF, 128])
    h = k.sb("h", [128, 16, T], BF16)
    wslots = [k.sb("wslot%d" % i, [128, 16, 256], BF16) for i in range(2)]
    mod_sb = [k.sb("mod%d" % l, [128, 96, 2]) for l in range(DEPTH)]
    der = k.sb("der", [128, DEPTH, 2, 2, 16])
    smallv = k.sb("smallv", [128, DEPTH, 8])
    nega = k.sb("nega", [128, DEPTH, 32])
    PS = [k.ps("psb%d" % i, [128, 512]) for i in range(8)]
    IDENT, ONES, ONES2048, ONES512, ONES128, BLK64 = [matb[:, i, :] for i in range(6)]
    M_LE, M_GE, M_LT, M_GT, ONESF = [matf[:, i, :] for i in range(5)]

    def ccol(name, i=0, w=1):
        o, _ = COLS(name)
        return cst[:, o + i:o + i + w]

    def mm(out, lhsT, rhs, start=True, stop=True, **kw):
        return k.op("pe", "matmul", [out], [lhsT, rhs], out, lhsT, rhs, start=start, stop=stop, **kw)

    def act(out, in_, func, bias=None, scale=None):
        kw = {}
        ins = [in_]
        if bias is not None:
            kw["bias"] = bias
            ins.append(bias)
        if scale is not None:
            kw["scale"] = scale
            ins.append(scale)
        return k.op("act", "activation", [out], ins, out=out, in_=in_, func=func, **kw)

    def tt(out, in0, in1, op, e="dve"):
        return k.op(e, "tensor_tensor", [out], [in0, in1], out=out, in0=in0, in1=in1, op=op)

    def ts(out, in0, s1, s2, op0, op1=None, e="dve"):
        kw = dict(out=out, in0=in0, scalar1=s1, scalar2=s2, op0=op0)
        if op1 is not None:
            kw["op1"] = op1
        return k.op(e, "tensor_scalar", [out], [in0, s1, s2], **kw)

    def stt(out, in0, scalar, in1, op0, op1):
        return k.op("dve", "scalar_tensor_tensor", [out], [in0, scalar, in1], out=out, in0=in0, scalar=scalar,
                    in1=in1, op0=op0, op1=op1)

    def cp(out, in_, e="dve"):
        return k.op(e, "tensor_copy", [out], [in_], out=out, in_=in_)

    def rsqrt_to(out, in_):
        act(out, in_, AF.Ln, bias=EPS)
        act(out, out, AF.Exp, scale=-0.5)

    wrr = [0]

    def wslot():
        s = wslots[wrr[0] % len(wslots)]
        wrr[0] += 1
        return s

    def wload(W3, c0, w, KC):
        s = wslot()
        hk = max(1, KC // 2)
        for k0 in range(0, KC, hk):
            k.dma("pool", s[:, k0:k0 + hk, :w], W3[:, k0:k0 + hk, c0:c0 + w])
        return s

    prr = [0]

    def bankpair():
        i = prr[0] % 3
        prr[0] += 1
        return [PS[2 * i], PS[2 * i + 1]]

    def linear_fm(W3, col0, ncols, KC, rhs, consume, banks_fn=bankpair, defer=1):
        m = 0
        pend = []
        for c0 in range(col0, col0 + ncols, 256):
            w = min(256, col0 + ncols - c0)
            s = wload(W3, c0, w, KC)
            for mc in range(w // 128):
                banks = banks_fn()
                for kc in range(KC):
                    for hf in range(2):
                        mm(banks[hf][:], s[:, kc, mc * 128:(mc + 1) * 128], rhs(kc, hf), start=(kc == 0), stop=(kc == KC - 1))
                pend.append((m, banks))
                if len(pend) > defer:
                    consume(*pend.pop(0))
                m += 1
            bg_step()
        while pend:
            consume(*pend.pop(0))

    def wview(w, l):
        return w[l].rearrange("(c p) n -> p c n", p=128)

    k.dma("sp", cst[:], cst_d)
    k.dma("pool", matb[:], matb_d)
    k.dma("sp", matf[:], matf_d)
    scb = k.sb("scb", [128, 16, 2], BF16)
    o, _ = COLS("csil")
    act(scb[:].rearrange("p a b -> p (a b)"), cst[:, o:o + 32], AF.Silu)
    def mod_block(l, b):
        W3 = wview(w_ada, l)
        ob, _ = COLS("L%d_bada" % l)
        if bgslots:
            s = bgslots[b % 2]
            for k0 in (0, 8):
                k.dma("pool", s[:, k0:k0 + 8, :], W3[:, k0:k0 + 8, b * 256:(b + 1) * 256])
        else:
            s = wload(W3, b * 256, 256, 16)
        for mc in range(2):
            m = 2 * b + mc
            pb = PS[7]
            for kc in range(16):
                mm(pb[:, 2 * mc:2 * mc + 2], s[:, kc, mc * 128:(mc + 1) * 128], scb[:, kc, :], start=(kc == 0), stop=(kc == 15))
            tt(mod_sb[l][:, m, :], pb[:, 2 * mc:2 * mc + 2], cst[:, ob + m:ob + m + 1].broadcast_to([128, 2]), ALU.add)

    def mod_der(l, which):
        onw, _ = COLS("L%d_%s" % (l, "n1w" if which == 0 else "n2w"))
        base = 16 if which == 0 else 64
        for g in range(2):
            stt(der[:, l, g, which, :], mod_sb[l][:, base:base + 16, g], 1.0, cst[:, onw:onw + 16], ALU.add, ALU.mult)

    bg = []
    bgslots = []
    for b in range(16):
        mod_block(0, b)
    mod_der(0, 0)
    for b in range(16, 48):
        bg.append(lambda b=b: mod_block(0, b))
    bg.append(lambda: mod_der(0, 1))
    BG_L0 = len(bg)
    for b in range(48):
        bg.append(lambda b=b: mod_block(1, b))
    bg.append(lambda: (mod_der(1, 0), mod_der(1, 1)))
    bgpos = [0]

    def bg_step(n=1, force=False):
        if not force and not bgslots:
            return
        for _ in range(n):
            if bgpos[0] < len(bg):
                bg[bgpos[0]]()
                bgpos[0] += 1

    def bg_until(pos):
        while bgpos[0] < min(pos, len(bg)):
            bg_step(force=True)

    for l in range(DEPTH):
        lam_init = 0.8 - 0.6 * math.exp(-0.3 * l)
        ol, _ = COLS("L%d_lam" % l)
        ltmp = k.sb("ltmp", [128, 128])
        lred = k.sb("lred", [128, 2])
        tt(ltmp[:, 0:64], cst[:, ol:ol + 64], cst[:, ol + 64:ol + 128], ALU.mult)
        tt(ltmp[:, 64:128], cst[:, ol + 128:ol + 192], cst[:, ol + 192:ol + 256], ALU.mult)
        k.op("dve", "tensor_reduce", [lred[:]], [ltmp[:]], out=lred[:], in_=ltmp[:].rearrange("p (a b) -> p a b", a=2),
             axis=AX.X, op=ALU.add)
        act(lred[:], lred[:], AF.Exp)
        tt(smallv[:, l, 0:1], lred[:, 1:2], lred[:, 0:1], ALU.subtract)
        ts(smallv[:, l, 0:1], smallv[:, l, 0:1], -lam_init, None, ALU.add)
        ts(smallv[:, l, 1:2], ccol("L%d_sub" % l), 1.0 - lam_init, None, ALU.mult)
        ts(smallv[:, l, 2:3], ccol("L%d_dqn" % l), 0.125, None, ALU.mult)
        ts(smallv[:, l, 3:4], ccol("L%d_nqn" % l), 0.125, None, ALU.mult)
        oa, _ = COLS("L%d_alog" % l)
        act(nega[:, l, :], cst[:, oa:oa + 32], AF.Exp)
        ts(nega[:, l, :], nega[:, l, :], -1.0, None, ALU.mult)

    stopped = False
    try:
        ck("setup", mod0=mod_sb[0][:], der=der[:], smallv=smallv[:], nega=nega[:])
    except _Stop:
        stopped = True
    GROUPS = {"ctx": dict(gi=0, segs=[(0, 256), (256, 256), (512, 256), (768, 256)]),
              "lat": dict(gi=1, segs=[(0, 1024)])}

    def run_pass(l, g):
        G = GROUPS[g]
        gi = G["gi"]
        segs = G["segs"]
        nseg = len(segs)
        L = segs[0][1]
        lat = (g == "lat")
        P = "L%d_" % l
        src = xin[g] if l == 0 else xmid2[g]
        dst = xmid2[g] if l == 0 else yout[g]
        WIN = wview(w_in, l)

        def hrhs(kc, hf):
            return h[:, kc, hf * 512:(hf + 1) * 512]

        def rmsnorm_mod(es, Xres, which):
            sq = [k.sbg(es, "sq", [128, 512], BF16) for _ in range(2)]
            tmp = [k.sbg(es, "ntmp", [128, 512]) for _ in range(2)]
            rs = k.sbg(es, "rs", [128, 512])
            sh_base = 0 if which == 0 else 48
            for hf in range(2):
                sl = slice(hf * 512, (hf + 1) * 512)
                for c in range(16):
                    act(sq[c % 2][:], Xres[:, c, sl], AF.Square)
                    mm(PS[6][:], ONES2048, sq[c % 2][:], start=(c == 0), stop=(c == 15))
                rsqrt_to(rs[:], PS[6][:])
                for c in range(16):
                    stt(tmp[c % 2][:], Xres[:, c, sl], der[:, l, gi, which, c:c + 1], rs[:], ALU.mult, ALU.mult)
                    act(h[:, c, sl], tmp[c % 2][:], AF.Identity, bias=mod_sb[l][:, sh_base + c, gi:gi + 1])

        if l == 1:
            bg_until(len(bg))
        with ExitStack() as es:
            Xres = k.sbg(es, "Xres", [128, 16, T])
            for c4 in range(4):
                k.dma("sp", Xres[:, 4 * c4:4 * c4 + 4, :], src[:, 4 * c4:4 * c4 + 4, :])
            rmsnorm_mod(es, Xres, 0)
            k.barrier()
        ck("norm1_%s%d" % (g, l), h=h[:])

        def conv_bufs(es):
            ub = k.sbg(es, "ub", [128, 1032])
            k.op("dve", "memset", [ub[:]], [], ub[:], 0.0)
            accs = [k.sbg(es, "cacc", [128, T]) for _ in range(2)]
            return dict(ub=ub, accs=accs, i=0)

        def conv_evac(cb, banks, wname, bname, ci, nch, out_ap, func):
            ub = cb["ub"]
            ubv = ub[:, 0:nseg * (L + 2)].rearrange("p (s l) -> p s l", s=nseg)
            acc = cb["accs"][cb["i"] % 2]
            cb["i"] += 1
            accv = acc[:].rearrange("p (s l) -> p s l", s=nseg)
            for hf in range(2):
                if lat:
                    act(ub[:, 1 + 512 * hf:513 + 512 * hf], banks[hf][:], AF.Copy)
                else:
                    act(ubv[:, 2 * hf:2 * hf + 2, 1:L + 1], banks[hf][:].rearrange("p (s l) -> p s l", s=2), AF.Copy)
            ow, _ = COLS(P + wname)
            obb, _ = COLS(P + bname)
            w0 = cst[:, ow + 0 * nch + ci:ow + 0 * nch + ci + 1]
            w1 = cst[:, ow + 1 * nch + ci:ow + 1 * nch + ci + 1]
            w2 = cst[:, ow + 2 * nch + ci:ow + 2 * nch + ci + 1]
            bb = cst[:, obb + ci:obb + ci + 1]
            ts(accv, ubv[:, :, 1:L + 1], w1, bb, ALU.mult, ALU.add)
            stt(accv, ubv[:, :, 0:L], w0, accv, ALU.mult, ALU.add)
            stt(accv, ubv[:, :, 2:L + 2], w2, accv, ALU.mult, ALU.add)
            if func is not None:
                act(out_ap, acc[:], func)
            return acc

        with ExitStack() as es2:
            ybr = k.sbg(es2, "ybr", [128, 8, T], BF16)
            sg_all = [k.sbg(es2, "sg", [128, 512]) for _ in range(4)]

            def gated_branch(bi, first):
                WB = wview(w_br[bi], l)
                sg = sg_all
                gcol = C_G + bi * D
                esb = ExitStack()
                bsl = [k.sbg(esb, "bslot", [128, 8, 256], BF16) for _ in range(2)]
                for c0 in range(0, D, 256):
                    sgw = wload(WIN, gcol + c0, 256, 16)
                    sbw = bsl[(c0 // 256) % 2]
                    k.dma("pool", sbw[:], WB[:, :, c0:c0 + 256])
                    for mc in range(2):
                        m = c0 // 128 + mc
                        gb = [PS[0], PS[1]] if m % 2 == 0 else [PS[2], PS[3]]
                        pb = [PS[4], PS[5]] if m % 2 == 0 else [PS[6], PS[7]]
                        for kc in range(16):
                            for hf in range(2):
                                mm(gb[hf][:], sgw[:, kc, mc * 128:(mc + 1) * 128], hrhs(kc, hf), start=(kc == 0), stop=(kc == 15))
                        for kc in range(8):
                            for hf in range(2):
                                mm(pb[hf][:], sbw[:, kc, mc * 128:(mc + 1) * 128], ybr[:, kc, hf * 512:(hf + 1) * 512],
                                   start=(kc == 0), stop=(kc == 7))
                        for hf in range(2):
                            sl = slice(hf * 512, (hf + 1) * 512)
                            sg_ = sg[2 * (m % 2) + hf]
                            act(sg_[:], gb[hf][:], AF.Sigmoid)
                            if first:
                                tt(merged[:, m, sl], sg_[:], pb[hf][:], ALU.mult)
                            else:
                                tt(sg_[:], sg_[:], pb[hf][:], ALU.mult)
                                tt(merged[:, m, sl], sg_[:], merged[:, m, sl], ALU.add)
                    bg_step()
                k.barrier()
                esb.close()

            with ExitStack() as es:
                xbcT = k.sbg(es, "xbcT", [128, 12, T], BF16)
                yT = k.sbg(es, "yT", [128, 8, T], BF16)
                sinr = k.sbg(es, "sinr", [128, 8, 1024], BF16)
                with ExitStack() as esc:
                    cb = conv_bufs(esc)
                    linear_fm(WIN, C_XBC, 1536, 16, hrhs,
                              lambda m, banks: conv_evac(cb, banks, "scw", "scb", m, 12, xbcT[:, m, :], AF.Silu))
                    k.barrier()
                ck("xbc_%s%d" % (g, l), xbcT=xbcT[:])
                wdt = wslot()
                k.dma("pool", wdt[:, :, 0:32], WIN[:, :, C_DT:C_DT + 32])
                for tl in range(8):
                    for kc in range(16):
                        mm(PS[4][:, tl * 32:(tl + 1) * 32], h[:, kc, tl * 128:(tl + 1) * 128], wdt[:, kc, 0:32],
                           start=(kc == 0), stop=(kc == 15))
                dt = k.sbg(es, "dt", [128, 8, 32])
                la = k.sbg(es, "la", [128, 8, 32])
                odt, _ = COLS(P + "dtb")
                tt(dt[:], PS[4][:, 0:256].rearrange("p (a b) -> p a b", a=8),
                   cst[:, odt:odt + 32].unsqueeze(1).broadcast_to([128, 8, 32]), ALU.add)
                act(dt[:], dt[:], AF.Exp)
                act(dt[:], dt[:], AF.Ln, bias=1.0)
                tt(la[:], dt[:], nega[:, l, :].unsqueeze(1).broadcast_to([128, 8, 32]), ALU.mult)
                ck("dt_%s%d" % (g, l), dt=dt[:], la=la[:], xbcT=xbcT[:])

                ess = ExitStack()
                xs_tm = k.sbg(ess, "xs_tm", [128, 1024], BF16)
                b_tm = k.sbg(ess, "b_tm", [128, 256], BF16)
                xdd = k.sbg(ess, "xdd", [128, 1024], BF16)
                xdf = k.sbg(ess, "xdf", [128, 1024], BF16)
                xdr = k.sbg(ess, "xdr", [128, 1024], BF16)
                dec = k.sbg(ess, "dec", [128, 16])
                ea = k.sbg(ess, "ea", [128, 32])
                S = [k.sbg(ess, "Sst%d" % d, [128, 1024]) for d in range(2)]
                Sfb = k.sbg(ess, "Sfb", [128, 1024], BF16)
                bcm = k.sbg(ess, "bcm", [128, 2, 2, 128])
                r1s = [k.sbg(ess, "r1", [128, 512]) for _ in range(3)]
                gts = [k.sbg(ess, "gt", [128, 512], BF16) for _ in range(3)]
                css = [k.sbg(ess, "cs", [128, 512], BF16) for _ in range(3)]
                etmps = [k.sbg(ess, "etmp", [128, 512]) for _ in range(6)]
                PSB = PS[5][:].bitcast(BF16)

                def to_tm(tl):
                    tsl = slice(tl * 128, (tl + 1) * 128)
                    for c in range(8):
                        k.op("pe", "transpose", [PS[5][:]], [xbcT[:], IDENT], PSB[:, c * 128:(c + 1) * 128], xbcT[:, c, tsl], IDENT)
                    cp(xs_tm[:], PSB)
                    for c in range(2):
                        k.op("pe", "transpose", [PS[5][:]], [xbcT[:], IDENT], PSB[:, c * 128:(c + 1) * 128], xbcT[:, 8 + c, tsl], IDENT)
                    cp(b_tm[:], PSB[:, 0:256])

                def chunk_state(tl, d):
                    lad = la[:, tl, d * 16:(d + 1) * 16]
                    mm(PS[4][:, 0:16], M_GT if d == 0 else M_LT, lad)
                    mm(PS[4][:, 32:64], ONESF, la[:, tl, :])
                    act(dec[:], PS[4][:, 0:16], AF.Exp)
                    act(ea[:], PS[4][:, 32:64], AF.Exp)
                    tt(dec[:], dec[:], dt[:, tl, d * 16:(d + 1) * 16], ALU.mult)
                    tt(xdd[:].rearrange("p (a b) -> p a b", a=16), xs_tm[:].rearrange("p (a b) -> p a b", a=16),
                       dec[:].unsqueeze(2).broadcast_to([128, 16, 64]), ALU.mult)
                    for gg in range(2):
                        mm(PS[6 + gg][:], b_tm[:, gg * 128:(gg + 1) * 128], xdd[:, gg * 512:(gg + 1) * 512])

                def state_update(d):
                    Sv = S[d][:].rearrange("p (a b) -> p a b", a=16)
                    tt(Sv, Sv, ea[:, d * 16:(d + 1) * 16].unsqueeze(2).broadcast_to([128, 16, 64]), ALU.mult)
                    for gg in range(2):
                        tt(S[d][:, gg * 512:(gg + 1) * 512], S[d][:, gg * 512:(gg + 1) * 512], PS[6 + gg][:], ALU.add)

                for si, (t0, Ls) in enumerate(segs):
                    tiles = list(range(t0 // 128, (t0 + Ls) // 128))
                    if lat:
                        k.dma("sp", S[1][:], sst_d[l, :, 1024:2048])
                        k.dma("sp", S[0][:], sst_d[l, :, 0:1024])
                    else:
                        k.op("dve", "memset", [S[1][:]], [], S[1][:], 0.0)
                        k.op("dve", "memset", [S[0][:]], [], S[0][:], 0.0)
                    for tl in reversed(tiles):
                        to_tm(tl)
                        ck("s1", xs_tm=xs_tm[:], b_tm=b_tm[:])
                        cp(sinr[:, tl, :], S[1][:])
                        chunk_state(tl, 1)
                        ck("s2", dec=dec[:], ea=ea[:], xdd=xdd[:])
                        state_update(1)
                        ck("s3", S1=S[1][:])
                    if not lat:
                        k.dma("sp", nss_o[l, si, :, 1024:2048], S[1][:])
                    for tl in tiles:
                        tsl = slice(tl * 128, (tl + 1) * 128)
                        to_tm(tl)
                        cp(Sfb[:], S[0][:])
                        tt(xdf[:].rearrange("p (a b) -> p a b", a=16), xs_tm[:].rearrange("p (a b) -> p a b", a=16),
                           dt[:, tl, 0:16].unsqueeze(2).broadcast_to([128, 16, 64]), ALU.mult)
                        tt(xdr[:].rearrange("p (a b) -> p a b", a=16), xs_tm[:].rearrange("p (a b) -> p a b", a=16),
                           dt[:, tl, 16:32].unsqueeze(2).broadcast_to([128, 16, 64]), ALU.mult)
                        for gg in range(2):
                            mm(PS[4][:, gg * 128:(gg + 1) * 128], xbcT[:, 8 + gg, tsl], xbcT[:, 10 + gg, tsl])
                        for gg in range(2):
                            tt(bcm[:, gg, 0, :], PS[4][:, gg * 128:(gg + 1) * 128], M_LE, ALU.mult)
                            tt(bcm[:, gg, 1, :], PS[4][:, gg * 128:(gg + 1) * 128], M_GE, ALU.mult)
                        ck("s4", bcm=bcm[:], xdf=xdf[:])
                        def stA(u):
                            b4, d = u // 2, u % 2
                            r1 = r1s[u % 3]
                            ladb = la[:, tl, d * 16 + 4 * b4:d * 16 + 4 * b4 + 4]
                            tt(r1[:].rearrange("p (a b) -> p a b", a=4),
                               (M_LE if d == 0 else M_GE).unsqueeze(1).broadcast_to([128, 4, 128]),
                               ladb.unsqueeze(2).broadcast_to([128, 4, 128]), ALU.mult)
                            mm(PS[2 * (u % 3)][:], M_GT if d == 0 else M_LT, r1[:])
                            mm(PS[2 * (u % 3) + 1][:], ONESF, r1[:])

                        def stB(u):
                            b4, d = u // 2, u % 2
                            gg = b4 // 2
                            e0, e1 = etmps[2 * (u % 3)], etmps[2 * (u % 3) + 1]
                            act(e0[:], PS[2 * (u % 3)][:], AF.Exp)
                            tt(gts[u % 3][:].rearrange("p (a b) -> p a b", a=4), e0[:].rearrange("p (a b) -> p a b", a=4),
                               bcm[:, gg, d, :].unsqueeze(1).broadcast_to([128, 4, 128]), ALU.mult)
                            act(e1[:], PS[2 * (u % 3) + 1][:], AF.Exp)
                            tt(css[u % 3][:].rearrange("p (a b) -> p a b", a=4), e1[:].rearrange("p (a b) -> p a b", a=4),
                               xbcT[:, 10 + gg, tsl].unsqueeze(1).broadcast_to([128, 4, 128]), ALU.mult)

                        def stC(b4):
                            u0, u1 = 2 * b4, 2 * b4 + 1
                            for j in range(4):
                                hh = 4 * b4 + j
                                pr, e = hh // 2, hh % 2
                                yo = PS[6 + pr // 4][e * 64:(e + 1) * 64, (pr % 4) * 128:(pr % 4 + 1) * 128]
                                hs = slice(hh * 64, (hh + 1) * 64)
                                js = slice(j * 128, (j + 1) * 128)
                                tp = (0, e * 64)
                                mm(yo, xdf[:, hs], gts[u0 % 3][:, js], start=True, stop=False, tile_position=tp)
                                mm(yo, Sfb[:, hs], css[u0 % 3][:, js], start=False, stop=False, tile_position=tp)
                                mm(yo, xdr[:, hs], gts[u1 % 3][:, js], start=False, stop=False, tile_position=tp)
                                mm(yo, sinr[:, tl, hs], css[u1 % 3][:, js], start=False, stop=True, tile_position=tp)
                        stA(0)
                        stA(1)
                        for u in range(8):
                            if u + 2 < 8:
                                stA(u + 2)
                            stB(u)
                            if u % 2 == 1:
                                stC(u // 2)
                        ck("s6", xdr=xdr[:])
                        osd, _ = COLS(P + "sdd")
                        for pr in range(8):
                            stt(yT[:, pr, tsl], xbcT[:, pr, tsl], cst[:, osd + pr:osd + pr + 1],
                                PS[6 + pr // 4][:, (pr % 4) * 128:(pr % 4 + 1) * 128], ALU.mult, ALU.add)
                        ck("s7", yT=yT[:])
                        chunk_state(tl, 0)
                        state_update(0)
                        ck("s8", S0=S[0][:])
                    if not lat:
                        k.dma("sp", nss_o[l, si, :, 0:1024], S[0][:])
                    ck("seg%d" % si, S0=S[0][:], yT=yT[:])
                ck("scan_%s%d" % (g, l), yT=yT[:], sinr=sinr[:], S0=S[0][:], S1=S[1][:])
                k.barrier()
                ess.close()
                sz = [k.sbg(es, "sz", [128, 512]) for _ in range(2)]

                def z_consume(m, banks):
                    for hf in range(2):
                        sl = slice(hf * 512, (hf + 1) * 512)
                        act(sz[hf][:], banks[hf][:], AF.Silu)
                        tt(yT[:, m, sl], yT[:, m, sl], sz[hf][:], ALU.mult)
                linear_fm(WIN, C_Z, 1024, 16, hrhs, z_consume)
                sqg = [k.sbg(es, "sqg", [128, 512], BF16) for _ in range(2)]
                rsg = k.sbg(es, "rsg", [128, 512])
                osn, _ = COLS(P + "snw")
                for gg in range(2):
                    for hf in range(2):
                        sl = slice(hf * 512, (hf + 1) * 512)
                        for c4 in range(4):
                            act(sqg[c4 % 2][:], yT[:, 4 * gg + c4, sl], AF.Square)
                            mm(PS[6][:], ONES512, sqg[c4 % 2][:], start=(c4 == 0), stop=(c4 == 3))
                        rsqrt_to(rsg[:], PS[6][:])
                        for c4 in range(4):
                            c = 4 * gg + c4
                            stt(ybr[:, c, sl], yT[:, c, sl], cst[:, osn + c:osn + c + 1], rsg[:], ALU.mult, ALU.mult)
                k.barrier()
            ck("ya_%s%d" % (g, l), ybr=ybr[:])
            merged = k.sbg(es2, "merged", [128, 16, T], BF16)
            esbg = ExitStack()
            if bgpos[0] < len(bg):
                bgslots.extend([k.sbg(esbg, "bgslot", [128, 16, 256], BF16) for _ in range(2)])
            gated_branch(0, True)
            k.barrier()
            ck("mA_%s%d" % (g, l), merged=merged[:])

            def attention_branch(diff):
                with ExitStack() as es:
                    nk = 1280 if lat else 1024
                    nkt = nk // 128
                    qT = k.sbg(es, "qT", [128, 8, T], BF16)
                    kT = k.sbg(es, "kT", [128, 8, nk], BF16)
                    v_tm = k.sbg(es, "v_tm", [128, nkt, 1024], BF16)
                    nsq = [k.sbg(es, "nsq", [128, 512], BF16) for _ in range(2)]
                    nrs = k.sbg(es, "nrs", [128, 512])
                    sti = [0]
                    cq_, ck_, cv_ = (C_QD, C_KD, C_VD) if diff else (C_QN, C_KN, C_VN)
                    wq = smallv[:, l, 2:3] if diff else smallv[:, l, 3:4]
                    wk = ccol(P + ("dkn" if diff else "nkn"))
                    ko_d = ndk_o if diff else nnk_o
                    vo_d = ndv_o if diff else nnv_o
                    rope = lat and diff
                    if rope:
                        rope_sb = k.sbg(es, "rope_sb", [128, 2, T])
                        pmT = k.sbg(es, "pmT", [128, 128])
                        k.dma("sp", rope_sb[:], rope_d)
                        k.dma("sp", pmT[:], pmt_d)
                        rtmp = k.sbg(es, "rtmp", [128, 512])
                    if lat:
                        k.dma("pool", kT[:, :, 1024:1280], (cdk_d if diff else cnk_d)[l])
                        k.dma("pool", v_tm[:, 8:10, :], (cdv_d if diff else cnv_d)[l])

                    esq = ExitStack()
                    stage = [k.sbg(esq, "stage", [128, 512]) for _ in range(4)]
                    nrs2 = [nrs, k.sbg(esq, "nrs2", [128, 512])]
                    if rope:
                        rtmps = [rtmp, k.sbg(esq, "rtmp2", [128, 512])]

                    def qk_consume(dest, wcol, is_k):
                        pend_rope = []

                        def f(m, banks):
                            sts = []
                            for hf in range(2):
                                st = stage[sti[0] % 4]
                                sti[0] += 1
                                sts.append(st)
                                act(nsq[hf][:], banks[hf][:], AF.Square)
                                act(st[:], banks[hf][:], AF.Copy)
                            while pend_rope:
                                pend_rope.pop(0)()
                            for hf in range(2):
                                mm(PS[6 + hf][:], BLK64, nsq[hf][:])
                            for hf in range(2):
                                rsqrt_to(nrs2[hf][:], PS[6 + hf][:])
                                stt(sts[hf][:], sts[hf][:], wcol, nrs2[hf][:], ALU.mult, ALU.mult)
                            for hf in range(2):
                                sl = slice(hf * 512, (hf + 1) * 512)
                                st = sts[hf]
                                if rope:
                                    def rp(st=st, hf=hf, sl=sl, m=m):
                                        mm(PS[6 + hf][:], pmT[:], st[:])
                                        tt(rtmps[hf][:], PS[6 + hf][:], rope_sb[:, 1, sl], ALU.mult)
                                        tt(st[:], st[:], rope_sb[:, 0, sl], ALU.mult)
                                        tt(dest[:, m, sl], st[:], rtmps[hf][:], ALU.add)
                                    pend_rope.append(rp)
                                else:
                                    act(dest[:, m, sl], st[:], AF.Copy)
                                    if is_k and not lat:
                                        k.dma("sp", ko_d[l, :, m, sl], st[:])

                        def flush():
                            while pend_rope:
                                pend_rope.pop(0)()
                        f.flush = flush
                        return f
                    cf = qk_consume(qT, wq, False)
                    linear_fm(WIN, cq_, 1024, 16, hrhs, cf)
                    cf.flush()
                    ck("q1", qT=qT[:])
                    cf = qk_consume(kT, wk, True)
                    linear_fm(WIN, ck_, 1024, 16, hrhs, cf)
                    cf.flush()
                    ck("q2", kT=kT[:])
                    esv = ExitStack()
                    vT = [k.sbg(esv, "vT", [128, T], BF16) for _ in range(2)]
                    PSBv = PS[6][:].bitcast(BF16)

                    def v_consume(m, banks):
                        vt_ = vT[m % 2]
                        for hf in range(2):
                            sl = slice(hf * 512, (hf + 1) * 512)
                            if not lat:
                                st = stage[sti[0] % 4]
                                sti[0] += 1
                                act(st[:], banks[hf][:], AF.Copy)
                                k.dma("sp", vo_d[l, :, m, sl], st[:])
                                cp(vt_[:, sl], st[:])
                            else:
                                act(vt_[:, sl], banks[hf][:], AF.Copy)
                        for tl in range(8):
                            k.op("pe", "transpose", [PS[6][:]], [vt_[:], IDENT], PSBv[:, tl * 128:(tl + 1) * 128],
                                 vt_[:, tl * 128:(tl + 1) * 128], IDENT)
                        cp(v_tm[:, 0:8, m * 128:(m + 1) * 128], PSBv.rearrange("p (a b) -> p a b", a=8))
                    linear_fm(WIN, cv_, 1024, 16, hrhs, v_consume)
                    k.barrier()
                    esv.close()
                    esq.close()
                    ck("qkv%d_%s%d" % (int(diff), g, l), qT=qT[:], kT=kT[:], v_tm=v_tm[:])
                    pT = [k.sbg(es, "pT", [128, 512], BF16) for _ in range(4)]
                    pti = [0]
                    rd = [k.sbg(es, "rd", [128, 512]) for _ in range(2 if diff else 1)]
                    if diff:
                        oa = k.sbg(es, "oa", [128, 512])
                        ob2 = k.sbg(es, "ob2", [128, 512])
                    if lat:
                        qblocks = [(hf * 512, 512, list(range(10))) for hf in range(2)]
                    else:
                        qblocks = [(s0, Ls, [s0 // 128, s0 // 128 + 1]) for (s0, Ls) in segs]
                    if lat and not diff:
                        tbs = [k.sbg(es, "tbs", [128, 1920], BF16) for _ in range(4)]
                        rowoh = k.sbg(es, "rowoh", [16, 8, 128], BF16)
                        rowpen = k.sbg(es, "rowpen", [16, T], BF16)
                        k.dma("pool", rowoh[:], rowoh_d)
                        k.dma("pool", rowpen[:], rowpen_d)
                    sbi = [0]

                    def sbank():
                        b = PS[sbi[0] % 4]
                        sbi[0] += 1
                        return b
                    LOOK = 2

                    def run_pipe(steps, score_fn, consume_fn):
                        banks = {}
                        n = len(steps)
                        for i in range(min(LOOK, n)):
                            banks[i] = score_fn(steps[i])
                        for i in range(n):
                            if i + LOOK < n:
                                banks[i + LOOK] = score_fn(steps[i + LOOK])
                            consume_fn(steps[i], banks.pop(i))

                    if diff:
                        steps = []
                        for hh in range(8):
                            for (q0, nq, ktl) in qblocks:
                                for ji, j in enumerate(ktl):
                                    for t2 in range(2):
                                        steps.append((hh, q0, nq, ji, j, t2, len(ktl)))

                        def score_fn(st_):
                            hh, q0, nq, ji, j, t2, nkt_ = st_
                            ps_ = slice(t2 * 64, (t2 + 1) * 64)
                            sb_ = sbank()
                            mm(sb_[:, 0:nq], kT[ps_, hh, j * 128:(j + 1) * 128], qT[ps_, hh, q0:q0 + nq])
                            return sb_

                        def consume_fn(st_, sb_):
                            hh, q0, nq, ji, j, t2, nkt_ = st_
                            qs = slice(q0, q0 + nq)
                            p_ = pT[pti[0] % 4]
                            pti[0] += 1
                            act(p_[:, 0:nq], sb_[:, 0:nq], AF.Exp)
                            mm(PS[4 + 2 * t2][:, 0:nq], v_tm[:, j, hh * 128:(hh + 1) * 128], p_[:, 0:nq],
                               start=(ji == 0), stop=(ji == nkt_ - 1))
                            mm(PS[5 + 2 * t2][:, 0:nq], ONES, p_[:, 0:nq], start=(ji == 0), stop=(ji == nkt_ - 1))
                            if not (ji == nkt_ - 1 and t2 == 1):
                                return
                            for t3 in range(2):
                                act(rd[t3][:, 0:nq], PS[5 + 2 * t3][:, 0:nq], AF.Ln)
                                act(rd[t3][:, 0:nq], rd[t3][:, 0:nq], AF.Exp, scale=-1.0)
                            tt(oa[:, 0:nq], PS[4][:, 0:nq], rd[0][:, 0:nq], ALU.mult)
                            tt(ob2[:, 0:nq], PS[6][:, 0:nq], rd[1][:, 0:nq], ALU.mult)
                            stt(oa[:, 0:nq], ob2[:, 0:nq], smallv[:, l, 0:1], oa[:, 0:nq], ALU.mult, ALU.add)
                            act(nsq[0][:, 0:nq], oa[:, 0:nq], AF.Square)
                            mm(PS[5][:, 0:nq], ONES128, nsq[0][:, 0:nq])
                            rsqrt_to(nrs[:, 0:nq], PS[5][:, 0:nq])
                            stt(ybr[:, hh, qs], oa[:, 0:nq], smallv[:, l, 1:2], nrs[:, 0:nq], ALU.mult, ALU.mult)
                        run_pipe(steps, score_fn, consume_fn)
                    else:
                        if lat:
                            qblocks = [(0, 512, [0, 1, 2, 3, 4, 5, 8, 9]), (512, 512, [2, 3, 4, 5, 6, 7, 8, 9])]
                        steps = []
                        for hp in range(8):
                            for (q0, nq, ktl) in qblocks:
                                for ji, j in enumerate(ktl):
                                    for e in range(2):
                                        steps.append((hp, q0, nq, ji, j, e, len(ktl)))
                        tb_loaded = set()

                        def score_fn(st_):
                            hp, q0, nq, ji, j, e, nkt_ = st_
                            ps_ = slice(e * 64, (e + 1) * 64)
                            hf = q0 // 512
                            local = lat and j < 8
                            if local and hp not in tb_loaded:
                                tb_loaded.add(hp)
                                for e2 in range(2):
                                    k.dma("pool", tbs[2 * (hp % 2) + e2][:], tb_d[l, 2 * hp + e2])
                            sb_ = sbank()
                            mm(sb_[:, 0:nq], kT[ps_, hp, j * 128:(j + 1) * 128], qT[ps_, hp, q0:q0 + nq], start=True, stop=(not local))
                            if local:
                                i0 = 14 - 2 * j + 8 * hf
                                mm(sb_[:, 0:nq], IDENT, tbs[2 * (hp % 2) + e][:, i0 * 64:i0 * 64 + 512], start=False, stop=False)
                                mm(sb_[:, 0:nq], rowoh[:, j, :], rowpen[:, q0:q0 + nq], start=False, stop=True)
                            return sb_

                        def consume_fn(st_, sb_):
                            hp, q0, nq, ji, j, e, nkt_ = st_
                            ps_ = slice(e * 64, (e + 1) * 64)
                            p_ = pT[pti[0] % 4]
                            pti[0] += 1
                            act(p_[:, 0:nq], sb_[:, 0:nq], AF.Exp)
                            hh = 2 * hp + e
                            tp = (0, e * 64)
                            mm(PS[4][ps_, 0:nq], v_tm[:, j, hh * 64:(hh + 1) * 64], p_[:, 0:nq],
                               start=(ji == 0), stop=(ji == nkt_ - 1), tile_position=tp)
                            mm(PS[5][ps_, 0:nq], ONES[:, 0:64], p_[:, 0:nq],
                               start=(ji == 0), stop=(ji == nkt_ - 1), tile_position=tp)
                            if not (ji == nkt_ - 1 and e == 1):
                                return
                            act(rd[0][:, 0:nq], PS[5][:, 0:nq], AF.Ln)
                            act(rd[0][:, 0:nq], rd[0][:, 0:nq], AF.Exp, scale=-1.0)
                            tt(ybr[:, hp, q0:q0 + nq], PS[4][:, 0:nq], rd[0][:, 0:nq], ALU.mult)
                        run_pipe(steps, score_fn, consume_fn)
                    k.barrier()

            attention_branch(True)
            ck("yb_%s%d" % (g, l), ybr=ybr[:])
            gated_branch(1, False)
            k.barrier()
            attention_branch(False)
            ck("yc_%s%d" % (g, l), ybr=ybr[:])
            gated_branch(2, False)
            k.barrier()
            ck("merged_%s%d" % (g, l), merged=merged[:])

            bg_until(BG_L0)
            k.barrier()
            del bgslots[:]
            esbg.close()
            with ExitStack() as es:
                Xres = k.sbg(es, "Xres2", [128, 16, T])
                xi = [k.sbg(es, "xi", [128, T]) for _ in range(2)]
                og1 = 32

                def wout_consume(m, banks):
                    x_ = xi[m % 2]
                    k.dma("sp", x_[:], src[:, m, :])
                    for hf in range(2):
                        sl = slice(hf * 512, (hf + 1) * 512)
                        stt(Xres[:, m, sl], banks[hf][:], mod_sb[l][:, og1 + m, gi:gi + 1], x_[:, sl], ALU.mult, ALU.add)
                linear_fm(wview(w_out, l), 0, D, 16, lambda kc, hf: merged[:, kc, hf * 512:(hf + 1) * 512], wout_consume)
                for c4 in range(4):
                    k.dma("sp", xmid1[g][:, 4 * c4:4 * c4 + 4, :], Xres[:, 4 * c4:4 * c4 + 4, :])
                rmsnorm_mod(es, Xres, 1)
                k.barrier()
                ck("x1_%s%d" % (g, l), Xres=Xres[:], h=h[:])

        with ExitStack() as es:
            aT = k.sbg(es, "aT", [128, 44, T], BF16)
            esf = ExitStack()
            if bgpos[0] < len(bg):
                bgslots.extend([k.sbg(esf, "bgslot", [128, 16, 256], BF16) for _ in range(2)])
            cb = conv_bufs(esf)
            vt = [k.sbg(esf, "vt", [128, T]) for _ in range(2)]
            gtm = k.sbg(esf, "gtm", [128, T])
            WUP = wview(w_up, l)
            for fp in range(22):
                sv = wload(WUP, fp * 256, 256, 16)
                sgt = wload(WUP, DFF + fp * 256, 256, 16)
                for mc in range(2):
                    f = 2 * fp + mc
                    banks = bankpair()
                    for kc in range(16):
                        for hf in range(2):
                            mm(banks[hf][:], sv[:, kc, mc * 128:(mc + 1) * 128], hrhs(kc, hf), start=(kc == 0), stop=(kc == 15))
                    conv_evac(cb, banks, "fcw", "fcb", f, 88, vt[mc][:], AF.Identity)
                for mc in range(2):
                    f = 2 * fp + mc
                    banks = bankpair()
                    for kc in range(16):
                        for hf in range(2):
                            mm(banks[hf][:], sgt[:, kc, mc * 128:(mc + 1) * 128], hrhs(kc, hf), start=(kc == 0), stop=(kc == 15))
                    conv_evac(cb, banks, "fcw", "fcb", 44 + f, 88, gtm[:], AF.Silu)
                    tt(aT[:, f, :], vt[mc][:], gtm[:], ALU.mult)
                bg_step(2)
            bg_until(len(bg))
            k.barrier()
            del bgslots[:]
            esf.close()
            ck("aT_%s%d" % (g, l), aT=aT[:])
            dsl_own = k.sbg(es, "dslot", [128, 44, 256], BF16)
            dsl_h = h[:, 0:11, :].rearrange("p a (b c) -> p (a b) c", c=256)
            xi = [k.sbg(es, "xi2", [128, T]) for _ in range(2)]
            xo = [k.sbg(es, "xo", [128, T]) for _ in range(2)]
            WDN = wview(w_dn, l)
            og2 = 80
            for mb in range(8):
                ds_ = dsl_own[:] if mb % 2 == 0 else dsl_h
                for k0 in range(0, 44, 11):
                    k.dma("pool", ds_[:, k0:k0 + 11, :], WDN[:, k0:k0 + 11, mb * 256:(mb + 1) * 256])
                for mc in range(2):
                    m = 2 * mb + mc
                    x_ = xi[m % 2]
                    k.dma("sp", x_[:], xmid1[g][:, m, :])
                    banks = bankpair()
                    for kc in range(44):
                        for hf in range(2):
                            mm(banks[hf][:], ds_[:, kc, mc * 128:(mc + 1) * 128], aT[:, kc, hf * 512:(hf + 1) * 512],
                               start=(kc == 0), stop=(kc == 43))
                    o_ = xo[m % 2]
                    for hf in range(2):
                        sl = slice(hf * 512, (hf + 1) * 512)
                        stt(o_[:, sl], banks[hf][:], mod_sb[l][:, og2 + m, gi:gi + 1], x_[:, sl], ALU.mult, ALU.add)
                    k.dma("sp", dst[:, m, :], o_[:])
            k.barrier()
        if stop == "pass_%s%d" % (g, l):
            with ExitStack() as es:
                rb = k.sbg(es, "rb", [128, 16, T])
                k.dma("sp", rb[:], dst)
                ck("pass_%s%d" % (g, l), x2=rb[:])

    try:
        for g in ("ctx", "lat"):
            for l in range(DEPTH):
                if not stopped and (only is None or (g, l) in only):
                    run_pass(l, g)
    except _Stop:
        pass
    k.finish()
    print("program: %d instructions, %d dma sems" % (nc.n_instructions(), len(k.dsems)))
    return nc


def _fm(v):
    v = np.asarray(v, np.float32)
    return np.ascontiguousarray(v.reshape(-1, 128).T)


def _structural():
    idx = np.arange(128)
    kk, ll = idx[:, None], idx[None, :]
    matb = np.zeros((128, N_MATS, 128), np.float32)
    matb[:, 0] = np.eye(128)
    matb[:, 1] = 1.0
    matb[:, 2] = 1.0 / 2048
    matb[:, 3] = 1.0 / 512
    matb[:, 4] = 1.0 / 128
    matb[:, 5] = ((kk // 64) == (ll // 64)) / 64.0
    matf = np.zeros((128, N_MATF, 128), np.float32)
    matf[:, 0] = kk <= ll
    matf[:, 1] = kk >= ll
    matf[:, 2] = kk < ll
    matf[:, 3] = kk > ll
    matf[:, 4] = 1.0
    t = np.arange(T)
    row = (t // 64).astype(np.float32)
    col = (t % 64).astype(np.float32)
    inv = (10000.0 ** (-np.arange(16, dtype=np.float32) / 16)).astype(np.float32)
    rope = np.zeros((128, 2, T), np.float32)
    pm = np.zeros((128, 128), np.float32)
    for p in range(128):
        d = p % 64
        pos = row if d < 32 else col
        ang = pos * inv[d % 16]
        rope[p, 0] = np.cos(ang)
        rope[p, 1] = np.sin(ang)
        if d % 32 < 16:
            pm[p, p + 16] = -1.0
        else:
            pm[p, p - 16] = 1.0
    pmT = np.ascontiguousarray(pm.T)
    rowoh = np.zeros((16, 8, 128), np.float32)
    for j in range(8):
        for kx in range(128):
            rowoh[2 * j + kx // 64, j, kx] = 1.0
    rowpen = np.zeros((16, T), np.float32)
    for qr in range(16):
        w0 = min(max(qr - 4, 0), 8)
        for kr in range(16):
            if not (w0 <= kr < w0 + 8):
                rowpen[kr, qr * 64:(qr + 1) * 64] = NEG
    return matb, matf, rope, pmT, rowoh, rowpen


def _na_tables(rpb):
    kc = np.arange(64)[:, None]
    qc = np.arange(64)[None, :]
    wst = np.clip(qc - 8, 0, 48)
    colok = (kc >= wst) & (kc < wst + 16)
    dc = np.clip(kc - qc + 15, 0, 30)
    out = np.full((DEPTH, 16, 2, 64, 30, 64), NEG, np.float32)
    for krl in range(2):
        for i in range(30):
            a = 21 - i + krl
            if 0 <= a <= 14:
                blk = rpb[:, :, a][:, :, dc]
                out[:, :, krl, :, i, :] = np.where(colok[None, None], blk, NEG)
    return np.ascontiguousarray(out.reshape(DEPTH, 16, 128, 1920))


_PROG = {}


def kernel(**inp):
    f32 = lambda a: np.ascontiguousarray(np.asarray(a, np.float32))
    x_prompt, x_sample = f32(inp["x_prompt"]), f32(inp["x_sample"])
    matb, matf, rope, pmT, rowoh, rowpen = _structural()
    tb = _na_tables(f32(inp["na_rpb"]))
    shared = {}
    for nm in ["w_ada", "w_in", "w_branch_a", "w_branch_b", "w_branch_c", "w_out", "ffn_w_up", "ffn_w_down"]:
        shared[nm] = f32(inp[nm])
    shared.update(matb=matb, matf=matf, rope=rope, pmT=pmT, rowoh=rowoh, rowpen=rowpen, tb=tb)

    def to_fm(x):
        return np.ascontiguousarray(x.reshape(T, 16, 128).transpose(2, 1, 0))

    in_maps = []
    for ci in range(NCORES):
        sb = ci // 4
        m = dict(shared)
        m["xp"] = to_fm(x_prompt[4 * ci:4 * ci + 4].reshape(T, D))
        m["xs"] = to_fm(x_sample[sb])
        cst = np.zeros((128, COLS.n), np.float32)

        def put(name, arr):
            o, w = COLS(name)
            assert arr.shape == (128, w), (name, arr.shape, w)
            cst[:, o:o + w] = arr
        cs = np.stack([_fm(inp["c_ctx"]), _fm(inp["c"][sb])], axis=2)
        put("csil", cs.reshape(128, 32))
        for l in range(DEPTH):
            p = "L%d_" % l
            put(p + "n1w", _fm(inp["norm1_w"][l]))
            put(p + "n2w", _fm(inp["norm2_w"][l]))
            put(p + "bada", _fm(inp["b_ada"][l]))
            put(p + "scw", np.concatenate([_fm(inp["ssd_conv_w"][l][j]) for j in range(3)], axis=1))
            put(p + "scb", _fm(inp["ssd_conv_b"][l]))
            put(p + "fcw", np.concatenate([_fm(inp["ffn_conv_w"][l][j]) for j in range(3)], axis=1))
            put(p + "fcb", _fm(inp["ffn_conv_b"][l]))
            put(p + "snw", _fm(inp["ssd_norm_w"][l]))
            put(p + "sdd", _fm(np.repeat(f32(inp["ssd_d"][l]), 64)))
            rep2 = lambda v: np.tile(f32(v), 2).reshape(128, 1)
            put(p + "dqn", rep2(inp["diff_q_norm"][l]))
            put(p + "dkn", rep2(inp["diff_k_norm"][l]))
            put(p + "sub", f32(inp["diff_subln_w"][l]).reshape(128, 1))
            put(p + "nqn", rep2(inp["na_q_norm"][l]))
            put(p + "nkn", rep2(inp["na_k_norm"][l]))
            put(p + "dtb", np.tile(f32(inp["ssd_dt_bias"][l]).reshape(1, 32), (128, 1)))
            put(p + "alog", np.tile(f32(inp["ssd_a_log"][l]).reshape(1, 32), (128, 1)))
            put(p + "lam", np.tile(f32(inp["diff_lam"][l]).reshape(1, 256), (128, 1)))
        m["cst"] = cst
        cdk = f32(inp["cache_diff_k"][sb])
        m["cdk"] = np.ascontiguousarray(cdk.transpose(0, 3, 4, 2, 1).reshape(DEPTH, 128, 8, 256))
        cdv = f32(inp["cache_diff_v"][sb])
        m["cdv"] = np.ascontiguousarray(cdv.reshape(DEPTH, 2, 128, 1024).transpose(0, 2, 1, 3))
        cnk = f32(inp["cache_na_k"][sb])
        m["cnk"] = np.ascontiguousarray(cnk.reshape(DEPTH, 256, 8, 128).transpose(0, 3, 2, 1))
        cnv = f32(inp["cache_na_v"][sb])
        m["cnv"] = np.ascontiguousarray(cnv.reshape(DEPTH, 2, 128, 1024).transpose(0, 2, 1, 3))
        sst = f32(inp["state_ssm"][sb])
        m["sst"] = np.ascontiguousarray(sst.transpose(0, 4, 1, 2, 3).reshape(DEPTH, 128, 2048))
        in_maps.append(m)

    if "nc" not in _PROG:
        _PROG["nc"] = build_program()
    res = run_bass_kernel_spmd(_PROG["nc"], in_maps, core_ids=list(range(NCORES)))
    R = res.results

    def from_fm(a):
        return a.transpose(2, 1, 0).reshape(T, D)

    B, S = 32, 256
    y_p = np.zeros((B, S, D), np.float32)
    y_s = np.zeros((2, 1024, D), np.float32)
    ndk = np.zeros((B, DEPTH, S, 8, 2, 64), np.float32)
    ndv = np.zeros((B, DEPTH, S, 8, 128), np.float32)
    nnk = np.zeros((B, DEPTH, S, 16, 64), np.float32)
    nnv = np.zeros((B, DEPTH, S, 16, 64), np.float32)
    nss = np.zeros((B, DEPTH, 2, 16, 64, 128), np.float32)
    for ci in range(NCORES):
        r = R[ci]
        y_p[4 * ci:4 * ci + 4] = from_fm(np.asarray(r["ypT"])).reshape(4, S, D)
        if ci % 4 == 0:
            y_s[ci // 4] = from_fm(np.asarray(r["ysT"]))
        kd = np.asarray(r["ndk"])
        ndk[4 * ci:4 * ci + 4] = kd.reshape(DEPTH, 2, 64, 8, 4, S).transpose(4, 0, 5, 3, 1, 2)
        vd = np.asarray(r["ndv"])
        ndv[4 * ci:4 * ci + 4] = vd.reshape(DEPTH, 128, 8, 4, S).transpose(3, 0, 4, 2, 1)
        kn = np.asarray(r["nnk"])
        nnk[4 * ci:4 * ci + 4] = kn.reshape(DEPTH, 2, 64, 8, 4, S).transpose(4, 0, 5, 3, 1, 2).reshape(4, DEPTH, S, 16, 64)
        vn = np.asarray(r["nnv"])
        nnv[4 * ci:4 * ci + 4] = vn.reshape(DEPTH, 2, 64, 8, 4, S).transpose(4, 0, 5, 3, 1, 2).reshape(4, DEPTH, S, 16, 64)
        ss = np.asarray(r["nss"])
        nss[4 * ci:4 * ci + 4] = ss.reshape(DEPTH, 4, 128, 2, 16, 64).transpose(1, 0, 3, 4, 5, 2)
    return (y_p, y_s, ndk, ndv, nnk, nnv, nss)
```

```python
import math
from contextlib import ExitStack
import numpy as np
import concourse.bass as bass
import concourse.mybir as mybir
from concourse.bass_utils import run_bass_kernel_spmd

F32 = mybir.dt.float32
BF16 = mybir.dt.bfloat16
AF = mybir.ActivationFunctionType
ALU = mybir.AluOpType
AX = mybir.AxisListType

D = 2048
T = 1024
EPS = 1e-6
NCORES = 8
DEPTH = 2
IN_COLS = 14880
DFF = 5632
C_Z, C_XBC, C_DT, C_QD, C_KD, C_VD, C_QN, C_KN, C_VN, C_G = 0, 1024, 2560, 2592, 3616, 4640, 5664, 6688, 7712, 8736
NEG = -30000.0


class _Rec:
    __slots__ = ("w", "r")

    def __init__(self):
        self.w = None
        self.r = {}


class KB:
    def __init__(self):
        self.nc = bass.Bass("TRN2", target_bir_lowering=False)
        nc = self.nc
        self.eng = {"pe": nc.tensor, "act": nc.scalar, "dve": nc.vector, "pool": nc.gpsimd, "sp": nc.sync}
        self.sems = {e: nc.alloc_semaphore("sem_" + e) for e in self.eng}
        self.cnt = {e: 0 for e in self.eng}
        self.waited = {e: {} for e in self.eng}
        self.recs = {}
        self.dsems = []
        self.dma_of = {}
        self.free_d = []
        self.tracked = set()
        self.dram = set()
        self.uid = 0

    def name(self, base):
        self.uid += 1
        return "%s_%d" % (base, self.uid)

    def track(self, t):
        self.tracked.add(t.name)
        return t

    def sb(self, name, shape, dtype=F32):
        return self.track(self.nc.alloc_sbuf_tensor(self.name(name), list(shape), dtype))

    def sbg(self, es, name, shape, dtype=F32):
        t = es.enter_context(self.nc.sbuf_tensor(self.name(name), list(shape), dtype))
        return self.track(t)

    def ps(self, name, shape, dtype=F32):
        return self.track(self.nc.alloc_psum_tensor(name, list(shape), dtype))

    def _recs_for(self, tn):
        if tn not in self.recs:
            self.recs[tn] = _Rec()
        return self.recs[tn]

    def _split(self, outs, ins):
        rd, wr = [], []
        for a in ins:
            if a is None or isinstance(a, (int, float)):
                continue
            tn = a.tensor.name
            if tn in self.tracked:
                rd.append(tn)
        for a in outs:
            tn = a.tensor.name
            if tn in self.tracked:
                wr.append(tn)
        return rd, wr

    def _deps(self, rd, wr):
        deps = {}

        def need(s, v):
            if deps.get(s, 0) < v:
                deps[s] = v
        for tn in rd:
            rec = self._recs_for(tn)
            if rec.w is not None:
                need(*rec.w)
            if tn.startswith("psb"):
                for s, v in rec.r.items():
                    need(s, v)
        for tn in wr:
            rec = self._recs_for(tn)
            if rec.w is not None:
                need(*rec.w)
            for s, v in rec.r.items():
                need(s, v)
        return deps

    def _semobj(self, s):
        if s in self.sems:
            return self.sems[s]
        return self.dsems[int(s[1:])][0]

    def _waits(self, e, deps):
        for s, v in deps.items():
            if e == "pe" and s == "pe":
                continue
            if self.waited[e].get(s, 0) >= v:
                continue
            self.eng[e].wait_ge(self._semobj(s), v)
            self.waited[e][s] = v

    def _update(self, rd, wr, s, v):
        for tn in rd:
            self.recs[tn].r[s] = v
        for tn in wr:
            rec = self.recs[tn]
            rec.w = (s, v)
            rec.r = {}

    def op(self, e, fname, outs, ins, *args, **kw):
        rd, wr = self._split(outs, ins)
        self._waits(e, self._deps(rd, wr))
        i = getattr(self.eng[e], fname)(*args, **kw)
        self.cnt[e] += 1
        i.then_inc(self.sems[e], 1)
        self._update(rd, wr, e, self.cnt[e])
        return i

    def dma(self, q, out, in_):
        otn, itn = out.tensor.name, in_.tensor.name
        rd, wr = [], []
        if itn in self.tracked:
            rd.append(itn)
        if otn in self.tracked:
            wr.append(otn)
        if otn in self.tracked and otn not in self.dram:
            sbn = otn
        else:
            sbn = itn
        self._waits(q, self._deps(rd, wr))
        idx = self.dma_of.get(sbn)
        if idx is None:
            if self.free_d:
                idx = self.free_d.pop()
            else:
                idx = len(self.dsems)
                self.dsems.append([self.nc.alloc_semaphore("dsem%d" % idx), 0])
            self.dma_of[sbn] = idx
        ds = self.dsems[idx]
        i = self.eng[q].dma_start(out=out, in_=in_)
        ds[1] += 16
        i.then_inc(ds[0], 16)
        self._update(rd, wr, "D%d" % idx, ds[1])
        return i

    def barrier(self):
        for e in self.eng:
            deps = {}
            for f in self.eng:
                if f != e and self.cnt[f] > 0:
                    deps[f] = self.cnt[f]
            for i, (s, c) in enumerate(self.dsems):
                if c > 0:
                    deps["D%d" % i] = c
            self._waits(e, deps)
        for rec in self.recs.values():
            rec.w = None
            rec.r = {}
        self.dma_of = {}
        self.free_d = list(range(len(self.dsems)))

    def finish(self):
        for i, (s, c) in enumerate(self.dsems):
            if c > 0 and self.waited["sp"].get("D%d" % i, 0) < c:
                self.eng["sp"].wait_ge(s, c)


class Cols:
    def __init__(self):
        self.off = {}
        self.n = 0

    def add(self, name, w):
        self.off[name] = (self.n, w)
        self.n += w

    def __call__(self, name):
        return self.off[name]


def make_cols():
    c = Cols()
    c.add("csil", 32)
    for l in range(DEPTH):
        p = "L%d_" % l
        for nm, w in [("n1w", 16), ("n2w", 16), ("bada", 96), ("scw", 36), ("scb", 12), ("fcw", 264), ("fcb", 88),
                      ("snw", 8), ("sdd", 8), ("dqn", 1), ("dkn", 1), ("sub", 1), ("nqn", 1), ("nkn", 1),
                      ("dtb", 32), ("alog", 32), ("lam", 256)]:
            c.add(p + nm, w)
    return c


COLS = make_cols()
N_MATS = 6
N_MATF = 5


class _Stop(Exception):
    pass


MARKS = []


def build_program(stop=None, only=None):
    k = KB()
    nc = k.nc
    taps = {}

    def ck(label, **aps):
        MARKS.append((label, k.cnt["pe"]))
        if stop != label:
            return
        k.barrier()
        for nm, ap in aps.items():
            d = nc.dram_tensor("dbg_" + nm, list(ap.shape), ap.dtype, kind="ExternalOutput").ap()
            k.dma("sp", d, ap)
            taps[nm] = d
        raise _Stop()

    def din(name, shape, dt=F32):
        return nc.dram_tensor(name, list(shape), dt, kind="ExternalInput").ap()

    def dout(name, shape, dt=F32):
        return nc.dram_tensor(name, list(shape), dt, kind="ExternalOutput").ap()

    def dscr(name, shape, dt=F32):
        t = nc.dram_tensor(name, list(shape), dt, kind="Internal")
        k.tracked.add(t.name)
        k.dram.add(t.name)
        return t.ap()

    xin = {"ctx": din("xp", [128, 16, T]), "lat": din("xs", [128, 16, T])}
    yout = {"ctx": dout("ypT", [128, 16, T]), "lat": dout("ysT", [128, 16, T])}
    cst_d = din("cst", [128, COLS.n])
    matb_d = din("matb", [128, N_MATS, 128])
    matf_d = din("matf", [128, N_MATF, 128])
    rope_d = din("rope", [128, 2, T])
    pmt_d = din("pmT", [128, 128])
    tb_d = din("tb", [DEPTH, 16, 128, 1920])
    rowoh_d = din("rowoh", [16, 8, 128])
    rowpen_d = din("rowpen", [16, T])
    cdk_d = din("cdk", [DEPTH, 128, 8, 256])
    cdv_d = din("cdv", [DEPTH, 128, 2, 1024])
    cnk_d = din("cnk", [DEPTH, 128, 8, 256])
    cnv_d = din("cnv", [DEPTH, 128, 2, 1024])
    sst_d = din("sst", [DEPTH, 128, 2048])
    w_ada = din("w_ada", [DEPTH, D, 6 * D])
    w_in = din("w_in", [DEPTH, D, IN_COLS])
    w_br = [din("w_branch_a", [DEPTH, 1024, D]), din("w_branch_b", [DEPTH, 1024, D]), din("w_branch_c", [DEPTH, 1024, D])]
    w_out = din("w_out", [DEPTH, D, D])
    w_up = din("ffn_w_up", [DEPTH, D, 2 * DFF])
    w_dn = din("ffn_w_down", [DEPTH, DFF, D])
    ndk_o = dout("ndk", [DEPTH, 128, 8, T])
    ndv_o = dout("ndv", [DEPTH, 128, 8, T])
    nnk_o = dout("nnk", [DEPTH, 128, 8, T])
    nnv_o = dout("nnv", [DEPTH, 128, 8, T])
    nss_o = dout("nss", [DEPTH, 4, 128, 2048])
    xmid1 = {g: dscr("xmid1_" + g, [128, 16, T]) for g in ("ctx", "lat")}
    xmid2 = {g: dscr("xmid2_" + g, [128, 16, T]) for g in ("ctx", "lat")}

    cst = k.sb("cst", [128, COLS.n])
    matb = k.sb("matb", [128, N_MATS, 128], BF16)
    matf = k.sb("matf", [128, N_MATF, 128])
    h = k.sb("h", [128, 16, T], BF16)
    wslots = [k.sb("wslot%d" % i, [128, 16, 256], BF16) for i in range(2)]
    mod_sb = [k.sb("mod%d" % l, [128, 96, 2]) for l in range(DEPTH)]
    der = k.sb("der", [128, DEPTH, 2, 2, 16])
    smallv = k.sb("smallv", [128, DEPTH, 8])
    nega = k.sb("nega", [128, DEPTH, 32])
    PS = [k.ps("psb%d" % i, [128, 512]) for i in range(8)]
    IDENT, ONES, ONES2048, ONES512, ONES128, BLK64 = [matb[:, i, :] for i in range(6)]
    M_LE, M_GE, M_LT, M_GT, ONESF = [matf[:, i, :] for i in range(5)]

    def ccol(name, i=0, w=1):
        o, _ = COLS(name)
        return cst[:, o + i:o + i + w]

    def mm(out, lhsT, rhs, start=True, stop=True, **kw):
        return k.op("pe", "matmul", [out], [lhsT, rhs], out, lhsT, rhs, start=start, stop=stop, **kw)

    def act(out, in_, func, bias=None, scale=None):
        kw = {}
        ins = [in_]
        if bias is not None:
            kw["bias"] = bias
            ins.append(bias)
        if scale is not None:
            kw["scale"] = scale
            ins.append(scale)
        return k.op("act", "activation", [out], ins, out=out, in_=in_, func=func, **kw)

    def tt(out, in0, in1, op, e="dve"):
        return k.op(e, "tensor_tensor", [out], [in0, in1], out=out, in0=in0, in1=in1, op=op)

    def ts(out, in0, s1, s2, op0, op1=None, e="dve"):
        kw = dict(out=out, in0=in0, scalar1=s1, scalar2=s2, op0=op0)
        if op1 is not None:
            kw["op1"] = op1
        return k.op(e, "tensor_scalar", [out], [in0, s1, s2], **kw)

    def stt(out, in0, scalar, in1, op0, op1):
        return k.op("dve", "scalar_tensor_tensor", [out], [in0, scalar, in1], out=out, in0=in0, scalar=scalar,
                    in1=in1, op0=op0, op1=op1)

    def cp(out, in_, e="dve"):
        return k.op(e, "tensor_copy", [out], [in_], out=out, in_=in_)

    def rsqrt_to(out, in_):
        act(out, in_, AF.Ln, bias=EPS)
        act(out, out, AF.Exp, scale=-0.5)

    wrr = [0]

    def wslot():
        s = wslots[wrr[0] % len(wslots)]
        wrr[0] += 1
        return s

    def wload(W3, c0, w, KC):
        s = wslot()
        hk = max(1, KC // 2)
        for k0 in range(0, KC, hk):
            k.dma("pool", s[:, k0:k0 + hk, :w], W3[:, k0:k0 + hk, c0:c0 + w])
        return s

    prr = [0]

    def bankpair():
        i = prr[0] % 3
        prr[0] += 1
        return [PS[2 * i], PS[2 * i + 1]]

    def linear_fm(W3, col0, ncols, KC, rhs, consume, banks_fn=bankpair, defer=1):
        m = 0
        pend = []
        for c0 in range(col0, col0 + ncols, 256):
            w = min(256, col0 + ncols - c0)
            s = wload(W3, c0, w, KC)
            for mc in range(w // 128):
                banks = banks_fn()
                for kc in range(KC):
                    for hf in range(2):
                        mm(banks[hf][:], s[:, kc, mc * 128:(mc + 1) * 128], rhs(kc, hf), start=(kc == 0), stop=(kc == KC - 1))
                pend.append((m, banks))
                if len(pend) > defer:
                    consume(*pend.pop(0))
                m += 1
            bg_step()
        while pend:
            consume(*pend.pop(0))

    def wview(w, l):
        return w[l].rearrange("(c p) n -> p c n", p=128)

    k.dma("sp", cst[:], cst_d)
    k.dma("pool", matb[:], matb_d)
    k.dma("sp", matf[:], matf_d)
    scb = k.sb("scb", [128, 16, 2], BF16)
    o, _ = COLS("csil")
    act(scb[:].rearrange("p a b -> p (a b)"), cst[:, o:o + 32], AF.Silu)
    def mod_block(l, b):
        W3 = wview(w_ada, l)
        ob, _ = COLS("L%d_bada" % l)
        if bgslots:
            s = bgslots[b % 2]
            for k0 in (0, 8):
                k.dma("pool", s[:, k0:k0 + 8, :], W3[:, k0:k0 + 8, b * 256:(b + 1) * 256])
        else:
            s = wload(W3, b * 256, 256, 16)
        for mc in range(2):
            m = 2 * b + mc
            pb = PS[7]
            for kc in range(16):
                mm(pb[:, 2 * mc:2 * mc + 2], s[:, kc, mc * 128:(mc + 1) * 128], scb[:, kc, :], start=(kc == 0), stop=(kc == 15))
            tt(mod_sb[l][:, m, :], pb[:, 2 * mc:2 * mc + 2], cst[:, ob + m:ob + m + 1].broadcast_to([128, 2]), ALU.add)

    def mod_der(l, which):
        onw, _ = COLS("L%d_%s" % (l, "n1w" if which == 0 else "n2w"))
        base = 16 if which == 0 else 64
        for g in range(2):
            stt(der[:, l, g, which, :], mod_sb[l][:, base:base + 16, g], 1.0, cst[:, onw:onw + 16], ALU.add, ALU.mult)

    bg = []
    bgslots = []
    for b in range(16):
        mod_block(0, b)
    mod_der(0, 0)
    for b in range(16, 48):
        bg.append(lambda b=b: mod_block(0, b))
    bg.append(lambda: mod_der(0, 1))
    BG_L0 = len(bg)
    for b in range(48):
        bg.append(lambda b=b: mod_block(1, b))
    bg.append(lambda: (mod_der(1, 0), mod_der(1, 1)))
    bgpos = [0]

    def bg_step(n=1, force=False):
        if not force and not bgslots:
            return
        for _ in range(n):
            if bgpos[0] < len(bg):
                bg[bgpos[0]]()
                bgpos[0] += 1

    def bg_until(pos):
        while bgpos[0] < min(pos, len(bg)):
            bg_step(force=True)

    for l in range(DEPTH):
        lam_init = 0.8 - 0.6 * math.exp(-0.3 * l)
        ol, _ = COLS("L%d_lam" % l)
        ltmp = k.sb("ltmp", [128, 128])
        lred = k.sb("lred", [128, 2])
        tt(ltmp[:, 0:64], cst[:, ol:ol + 64], cst[:, ol + 64:ol + 128], ALU.mult)
        tt(ltmp[:, 64:128], cst[:, ol + 128:ol + 192], cst[:, ol + 192:ol + 256], ALU.mult)
        k.op("dve", "tensor_reduce", [lred[:]], [ltmp[:]], out=lred[:], in_=ltmp[:].rearrange("p (a b) -> p a b", a=2),
             axis=AX.X, op=ALU.add)
        act(lred[:], lred[:], AF.Exp)
        tt(smallv[:, l, 0:1], lred[:, 1:2], lred[:, 0:1], ALU.subtract)
        ts(smallv[:, l, 0:1], smallv[:, l, 0:1], -lam_init, None, ALU.add)
        ts(smallv[:, l, 1:2], ccol("L%d_sub" % l), 1.0 - lam_init, None, ALU.mult)
        ts(smallv[:, l, 2:3], ccol("L%d_dqn" % l), 0.125, None, ALU.mult)
        ts(smallv[:, l, 3:4], ccol("L%d_nqn" % l), 0.125, None, ALU.mult)
        oa, _ = COLS("L%d_alog" % l)
        act(nega[:, l, :], cst[:, oa:oa + 32], AF.Exp)
        ts(nega[:, l, :], nega[:, l, :], -1.0, None, ALU.mult)

    stopped = False
    try:
        ck("setup", mod0=mod_sb[0][:], der=der[:], smallv=smallv[:], nega=nega[:])
    except _Stop:
        stopped = True
    GROUPS = {"ctx": dict(gi=0, segs=[(0, 256), (256, 256), (512, 256), (768, 256)]),
              "lat": dict(gi=1, segs=[(0, 1024)])}

    def run_pass(l, g):
        G = GROUPS[g]
        gi = G["gi"]
        segs = G["segs"]
        nseg = len(segs)
        L = segs[0][1]
        lat = (g == "lat")
        P = "L%d_" % l
        src = xin[g] if l == 0 else xmid2[g]
        dst = xmid2[g] if l == 0 else yout[g]
        WIN = wview(w_in, l)

        def hrhs(kc, hf):
            return h[:, kc, hf * 512:(hf + 1) * 512]

        def rmsnorm_mod(es, Xres, which):
            sq = [k.sbg(es, "sq", [128, 512], BF16) for _ in range(2)]
            tmp = [k.sbg(es, "ntmp", [128, 512]) for _ in range(2)]
            rs = k.sbg(es, "rs", [128, 512])
            sh_base = 0 if which == 0 else 48
            for hf in range(2):
                sl = slice(hf * 512, (hf + 1) * 512)
                for c in range(16):
                    act(sq[c % 2][:], Xres[:, c, sl], AF.Square)
                    mm(PS[6][:], ONES2048, sq[c % 2][:], start=(c == 0), stop=(c == 15))
                rsqrt_to(rs[:], PS[6][:])
                for c in range(16):
                    stt(tmp[c % 2][:], Xres[:, c, sl], der[:, l, gi, which, c:c + 1], rs[:], ALU.mult, ALU.mult)
                    act(h[:, c, sl], tmp[c % 2][:], AF.Identity, bias=mod_sb[l][:, sh_base + c, gi:gi + 1])

        if l == 1:
            bg_until(len(bg))
        with ExitStack() as es:
            Xres = k.sbg(es, "Xres", [128, 16, T])
            for c4 in range(4):
                k.dma("sp", Xres[:, 4 * c4:4 * c4 + 4, :], src[:, 4 * c4:4 * c4 + 4, :])
            rmsnorm_mod(es, Xres, 0)
            k.barrier()
        ck("norm1_%s%d" % (g, l), h=h[:])

        def conv_bufs(es):
            ub = k.sbg(es, "ub", [128, 1032])
            k.op("dve", "memset", [ub[:]], [], ub[:], 0.0)
            accs = [k.sbg(es, "cacc", [128, T]) for _ in range(2)]
            return dict(ub=ub, accs=accs, i=0)

        def conv_evac(cb, banks, wname, bname, ci, nch, out_ap, func):
            ub = cb["ub"]
            ubv = ub[:, 0:nseg * (L + 2)].rearrange("p (s l) -> p s l", s=nseg)
            acc = cb["accs"][cb["i"] % 2]
            cb["i"] += 1
            accv = acc[:].rearrange("p (s l) -> p s l", s=nseg)
            for hf in range(2):
                if lat:
                    act(ub[:, 1 + 512 * hf:513 + 512 * hf], banks[hf][:], AF.Copy)
                else:
                    act(ubv[:, 2 * hf:2 * hf + 2, 1:L + 1], banks[hf][:].rearrange("p (s l) -> p s l", s=2), AF.Copy)
            ow, _ = COLS(P + wname)
            obb, _ = COLS(P + bname)
            w0 = cst[:, ow + 0 * nch + ci:ow + 0 * nch + ci + 1]
            w1 = cst[:, ow + 1 * nch + ci:ow + 1 * nch + ci + 1]
            w2 = cst[:, ow + 2 * nch + ci:ow + 2 * nch + ci + 1]
            bb = cst[:, obb + ci:obb + ci + 1]
            ts(accv, ubv[:, :, 1:L + 1], w1, bb, ALU.mult, ALU.add)
            stt(accv, ubv[:, :, 0:L], w0, accv, ALU.mult, ALU.add)
            stt(accv, ubv[:, :, 2:L + 2], w2, accv, ALU.mult, ALU.add)
            if func is not None:
                act(out_ap, acc[:], func)
            return acc

        with ExitStack() as es2:
            ybr = k.sbg(es2, "ybr", [128, 8, T], BF16)
            sg_all = [k.sbg(es2, "sg", [128, 512]) for _ in range(4)]

            def gated_branch(bi, first):
                WB = wview(w_br[bi], l)
                sg = sg_all
                gcol = C_G + bi * D
                esb = ExitStack()
                bsl = [k.sbg(esb, "bslot", [128, 8, 256], BF16) for _ in range(2)]
                for c0 in range(0, D, 256):
                    sgw = wload(WIN, gcol + c0, 256, 16)
                    sbw = bsl[(c0 // 256) % 2]
                    k.dma("pool", sbw[:], WB[:, :, c0:c0 + 256])
                    for mc in range(2):
                        m = c0 // 128 + mc
                        gb = [PS[0], PS[1]] if m % 2 == 0 else [PS[2], PS[3]]
                        pb = [PS[4], PS[5]] if m % 2 == 0 else [PS[6], PS[7]]
                        for kc in range(16):
                            for hf in range(2):
                                mm(gb[hf][:], sgw[:, kc, mc * 128:(mc + 1) * 128], hrhs(kc, hf), start=(kc == 0), stop=(kc == 15))
                        for kc in range(8):
                            for hf in range(2):
                                mm(pb[hf][:], sbw[:, kc, mc * 128:(mc + 1) * 128], ybr[:, kc, hf * 512:(hf + 1) * 512],
                                   start=(kc == 0), stop=(kc == 7))
                        for hf in range(2):
                            sl = slice(hf * 512, (hf + 1) * 512)
                            sg_ = sg[2 * (m % 2) + hf]
                            act(sg_[:], gb[hf][:], AF.Sigmoid)
                            if first:
                                tt(merged[:, m, sl], sg_[:], pb[hf][:], ALU.mult)
                            else:
                                tt(sg_[:], sg_[:], pb[hf][:], ALU.mult)
                                tt(merged[:, m, sl], sg_[:], merged[:, m, sl], ALU.add)
                    bg_step()
                k.barrier()
                esb.close()

            with ExitStack() as es:
                xbcT = k.sbg(es, "xbcT", [128, 12, T], BF16)
                yT = k.sbg(es, "yT", [128, 8, T], BF16)
                sinr = k.sbg(es, "sinr", [128, 8, 1024], BF16)
                with ExitStack() as esc:
                    cb = conv_bufs(esc)
                    linear_fm(WIN, C_XBC, 1536, 16, hrhs,
                              lambda m, banks: conv_evac(cb, banks, "scw", "scb", m, 12, xbcT[:, m, :], AF.Silu))
                    k.barrier()
                ck("xbc_%s%d" % (g, l), xbcT=xbcT[:])
                wdt = wslot()
                k.dma("pool", wdt[:, :, 0:32], WIN[:, :, C_DT:C_DT + 32])
                for tl in range(8):
                    for kc in range(16):
                        mm(PS[4][:, tl * 32:(tl + 1) * 32], h[:, kc, tl * 128:(tl + 1) * 128], wdt[:, kc, 0:32],
                           start=(kc == 0), stop=(kc == 15))
                dt = k.sbg(es, "dt", [128, 8, 32])
                la = k.sbg(es, "la", [128, 8, 32])
                odt, _ = COLS(P + "dtb")
                tt(dt[:], PS[4][:, 0:256].rearrange("p (a b) -> p a b", a=8),
                   cst[:, odt:odt + 32].unsqueeze(1).broadcast_to([128, 8, 32]), ALU.add)
                act(dt[:], dt[:], AF.Exp)
                act(dt[:], dt[:], AF.Ln, bias=1.0)
                tt(la[:], dt[:], nega[:, l, :].unsqueeze(1).broadcast_to([128, 8, 32]), ALU.mult)
                ck("dt_%s%d" % (g, l), dt=dt[:], la=la[:], xbcT=xbcT[:])

                ess = ExitStack()
                xs_tm = k.sbg(ess, "xs_tm", [128, 1024], BF16)
                b_tm = k.sbg(ess, "b_tm", [128, 256], BF16)
                xdd = k.sbg(ess, "xdd", [128, 1024], BF16)
                xdf = k.sbg(ess, "xdf", [128, 1024], BF16)
                xdr = k.sbg(ess, "xdr", [128, 1024], BF16)
                dec = k.sbg(ess, "dec", [128, 16])
                ea = k.sbg(ess, "ea", [128, 32])
                S = [k.sbg(ess, "Sst%d" % d, [128, 1024]) for d in range(2)]
                Sfb = k.sbg(ess, "Sfb", [128, 1024], BF16)
                bcm = k.sbg(ess, "bcm", [128, 2, 2, 128])
                r1s = [k.sbg(ess, "r1", [128, 512]) for _ in range(3)]
                gts = [k.sbg(ess, "gt", [128, 512], BF16) for _ in range(3)]
                css = [k.sbg(ess, "cs", [128, 512], BF16) for _ in range(3)]
                etmps = [k.sbg(ess, "etmp", [128, 512]) for _ in range(6)]
                PSB = PS[5][:].bitcast(BF16)

                def to_tm(tl):
                    tsl = slice(tl * 128, (tl + 1) * 128)
                    for c in range(8):
                        k.op("pe", "transpose", [PS[5][:]], [xbcT[:], IDENT], PSB[:, c * 128:(c + 1) * 128], xbcT[:, c, tsl], IDENT)
                    cp(xs_tm[:], PSB)
                    for c in range(2):
                        k.op("pe", "transpose", [PS[5][:]], [xbcT[:], IDENT], PSB[:, c * 128:(c + 1) * 128], xbcT[:, 8 + c, tsl], IDENT)
                    cp(b_tm[:], PSB[:, 0:256])

                def chunk_state(tl, d):
                    lad = la[:, tl, d * 16:(d + 1) * 16]
                    mm(PS[4][:, 0:16], M_GT if d == 0 else M_LT, lad)
                    mm(PS[4][:, 32:64], ONESF, la[:, tl, :])
                    act(dec[:], PS[4][:, 0:16], AF.Exp)
                    act(ea[:], PS[4][:, 32:64], AF.Exp)
                    tt(dec[:], dec[:], dt[:, tl, d * 16:(d + 1) * 16], ALU.mult)
                    tt(xdd[:].rearrange("p (a b) -> p a b", a=16), xs_tm[:].rearrange("p (a b) -> p a b", a=16),
                       dec[:].unsqueeze(2).broadcast_to([128, 16, 64]), ALU.mult)
                    for gg in range(2):
                        mm(PS[6 + gg][:], b_tm[:, gg * 128:(gg + 1) * 128], xdd[:, gg * 512:(gg + 1) * 512])

                def state_update(d):
                    Sv = S[d][:].rearrange("p (a b) -> p a b", a=16)
                    tt(Sv, Sv, ea[:, d * 16:(d + 1) * 16].unsqueeze(2).broadcast_to([128, 16, 64]), ALU.mult)
                    for gg in range(2):
                        tt(S[d][:, gg * 512:(gg + 1) * 512], S[d][:, gg * 512:(gg + 1) * 512], PS[6 + gg][:], ALU.add)

                for si, (t0, Ls) in enumerate(segs):
                    tiles = list(range(t0 // 128, (t0 + Ls) // 128))
                    if lat:
                        k.dma("sp", S[1][:], sst_d[l, :, 1024:2048])
                        k.dma("sp", S[0][:], sst_d[l, :, 0:1024])
                    else:
                        k.op("dve", "memset", [S[1][:]], [], S[1][:], 0.0)
                        k.op("dve", "memset", [S[0][:]], [], S[0][:], 0.0)
                    for tl in reversed(tiles):
                        to_tm(tl)
                        ck("s1", xs_tm=xs_tm[:], b_tm=b_tm[:])
                        cp(sinr[:, tl, :], S[1][:])
                        chunk_state(tl, 1)
                        ck("s2", dec=dec[:], ea=ea[:], xdd=xdd[:])
                        state_update(1)
                        ck("s3", S1=S[1][:])
                    if not lat:
                        k.dma("sp", nss_o[l, si, :, 1024:2048], S[1][:])
                    for tl in tiles:
                        tsl = slice(tl * 128, (tl + 1) * 128)
                        to_tm(tl)
                        cp(Sfb[:], S[0][:])
                        tt(xdf[:].rearrange("p (a b) -> p a b", a=16), xs_tm[:].rearrange("p (a b) -> p a b", a=16),
                           dt[:, tl, 0:16].unsqueeze(2).broadcast_to([128, 16, 64]), ALU.mult)
                        tt(xdr[:].rearrange("p (a b) -> p a b", a=16), xs_tm[:].rearrange("p (a b) -> p a b", a=16),
                           dt[:, tl, 16:32].unsqueeze(2).broadcast_to([128, 16, 64]), ALU.mult)
                        for gg in range(2):
                            mm(PS[4][:, gg * 128:(gg + 1) * 128], xbcT[:, 8 + gg, tsl], xbcT[:, 10 + gg, tsl])
                        for gg in range(2):
                            tt(bcm[:, gg, 0, :], PS[4][:, gg * 128:(gg + 1) * 128], M_LE, ALU.mult)
                            tt(bcm[:, gg, 1, :], PS[4][:, gg * 128:(gg + 1) * 128], M_GE, ALU.mult)
                        ck("s4", bcm=bcm[:], xdf=xdf[:])
                        def stA(u):
                            b4, d = u // 2, u % 2
                            r1 = r1s[u % 3]
                            ladb = la[:, tl, d * 16 + 4 * b4:d * 16 + 4 * b4 + 4]
                            tt(r1[:].rearrange("p (a b) -> p a b", a=4),
                               (M_LE if d == 0 else M_GE).unsqueeze(1).broadcast_to([128, 4, 128]),
                               ladb.unsqueeze(2).broadcast_to([128, 4, 128]), ALU.mult)
                            mm(PS[2 * (u % 3)][:], M_GT if d == 0 else M_LT, r1[:])
                            mm(PS[2 * (u % 3) + 1][:], ONESF, r1[:])

                        def stB(u):
                            b4, d = u // 2, u % 2
                            gg = b4 // 2
                            e0, e1 = etmps[2 * (u % 3)], etmps[2 * (u % 3) + 1]
                            act(e0[:], PS[2 * (u % 3)][:], AF.Exp)
                            tt(gts[u % 3][:].rearrange("p (a b) -> p a b", a=4), e0[:].rearrange("p (a b) -> p a b", a=4),
                               bcm[:, gg, d, :].unsqueeze(1).broadcast_to([128, 4, 128]), ALU.mult)
                            act(e1[:], PS[2 * (u % 3) + 1][:], AF.Exp)
                            tt(css[u % 3][:].rearrange("p (a b) -> p a b", a=4), e1[:].rearrange("p (a b) -> p a b", a=4),
                               xbcT[:, 10 + gg, tsl].unsqueeze(1).broadcast_to([128, 4, 128]), ALU.mult)

                        def stC(b4):
                            u0, u1 = 2 * b4, 2 * b4 + 1
                            for j in range(4):
                                hh = 4 * b4 + j
                                pr, e = hh // 2, hh % 2
                                yo = PS[6 + pr // 4][e * 64:(e + 1) * 64, (pr % 4) * 128:(pr % 4 + 1) * 128]
                                hs = slice(hh * 64, (hh + 1) * 64)
                                js = slice(j * 128, (j + 1) * 128)
                                tp = (0, e * 64)
                                mm(yo, xdf[:, hs], gts[u0 % 3][:, js], start=True, stop=False, tile_position=tp)
                                mm(yo, Sfb[:, hs], css[u0 % 3][:, js], start=False, stop=False, tile_position=tp)
                                mm(yo, xdr[:, hs], gts[u1 % 3][:, js], start=False, stop=False, tile_position=tp)
                                mm(yo, sinr[:, tl, hs], css[u1 % 3][:, js], start=False, stop=True, tile_position=tp)
                        stA(0)
                        stA(1)
                        for u in range(8):
                            if u + 2 < 8:
                                stA(u + 2)
                            stB(u)
                            if u % 2 == 1:
                                stC(u // 2)
                        ck("s6", xdr=xdr[:])
                        osd, _ = COLS(P + "sdd")
                        for pr in range(8):
                            stt(yT[:, pr, tsl], xbcT[:, pr, tsl], cst[:, osd + pr:osd + pr + 1],
                                PS[6 + pr // 4][:, (pr % 4) * 128:(pr % 4 + 1) * 128], ALU.mult, ALU.add)
                        ck("s7", yT=yT[:])
                        chunk_state(tl, 0)
                        state_update(0)
                        ck("s8", S0=S[0][:])
                    if not lat:
                        k.dma("sp", nss_o[l, si, :, 0:1024], S[0][:])
                    ck("seg%d" % si, S0=S[0][:], yT=yT[:])
                ck("scan_%s%d" % (g, l), yT=yT[:], sinr=sinr[:], S0=S[0][:], S1=S[1][:])
                k.barrier()
                ess.close()
                sz = [k.sbg(es, "sz", [128, 512]) for _ in range(2)]

                def z_consume(m, banks):
                    for hf in range(2):
                        sl = slice(hf * 512, (hf + 1) * 512)
                        act(sz[hf][:], banks[hf][:], AF.Silu)
                        tt(yT[:, m, sl], yT[:, m, sl], sz[hf][:], ALU.mult)
                linear_fm(WIN, C_Z, 1024, 16, hrhs, z_consume)
                sqg = [k.sbg(es, "sqg", [128, 512], BF16) for _ in range(2)]
                rsg = k.sbg(es, "rsg", [128, 512])
                osn, _ = COLS(P + "snw")
                for gg in range(2):
                    for hf in range(2):
                        sl = slice(hf * 512, (hf + 1) * 512)
                        for c4 in range(4):
                            act(sqg[c4 % 2][:], yT[:, 4 * gg + c4, sl], AF.Square)
                            mm(PS[6][:], ONES512, sqg[c4 % 2][:], start=(c4 == 0), stop=(c4 == 3))
                        rsqrt_to(rsg[:], PS[6][:])
                        for c4 in range(4):
                            c = 4 * gg + c4
                            stt(ybr[:, c, sl], yT[:, c, sl], cst[:, osn + c:osn + c + 1], rsg[:], ALU.mult, ALU.mult)
                k.barrier()
            ck("ya_%s%d" % (g, l), ybr=ybr[:])
            merged = k.sbg(es2, "merged", [128, 16, T], BF16)
            esbg = ExitStack()
            if bgpos[0] < len(bg):
                bgslots.extend([k.sbg(esbg, "bgslot", [128, 16, 256], BF16) for _ in range(2)])
            gated_branch(0, True)
            k.barrier()
            ck("mA_%s%d" % (g, l), merged=merged[:])

            def attention_branch(diff):
                with ExitStack() as es:
                    nk = 1280 if lat else 1024
                    nkt = nk // 128
                    qT = k.sbg(es, "qT", [128, 8, T], BF16)
                    kT = k.sbg(es, "kT", [128, 8, nk], BF16)
                    v_tm = k.sbg(es, "v_tm", [128, nkt, 1024], BF16)
                    nsq = [k.sbg(es, "nsq", [128, 512], BF16) for _ in range(2)]
                    nrs = k.sbg(es, "nrs", [128, 512])
                    sti = [0]
                    cq_, ck_, cv_ = (C_QD, C_KD, C_VD) if diff else (C_QN, C_KN, C_VN)
                    wq = smallv[:, l, 2:3] if diff else smallv[:, l, 3:4]
                    wk = ccol(P + ("dkn" if diff else "nkn"))
                    ko_d = ndk_o if diff else nnk_o
                    vo_d = ndv_o if diff else nnv_o
                    rope = lat and diff
                    if rope:
                        rope_sb = k.sbg(es, "rope_sb", [128, 2, T])
                        pmT = k.sbg(es, "pmT", [128, 128])
                        k.dma("sp", rope_sb[:], rope_d)
                        k.dma("sp", pmT[:], pmt_d)
                        rtmp = k.sbg(es, "rtmp", [128, 512])
                    if lat:
                        k.dma("pool", kT[:, :, 1024:1280], (cdk_d if diff else cnk_d)[l])
                        k.dma("pool", v_tm[:, 8:10, :], (cdv_d if diff else cnv_d)[l])

                    esq = ExitStack()
                    stage = [k.sbg(esq, "stage", [128, 512]) for _ in range(4)]
                    nrs2 = [nrs, k.sbg(esq, "nrs2", [128, 512])]
                    if rope:
                        rtmps = [rtmp, k.sbg(esq, "rtmp2", [128, 512])]

                    def qk_consume(dest, wcol, is_k):
                        pend_rope = []

                        def f(m, banks):
                            sts = []
                            for hf in range(2):
                                st = stage[sti[0] % 4]
                                sti[0] += 1
                                sts.append(st)
                                act(nsq[hf][:], banks[hf][:], AF.Square)
                                act(st[:], banks[hf][:], AF.Copy)
                            while pend_rope:
                                pend_rope.pop(0)()
                            for hf in range(2):
                                mm(PS[6 + hf][:], BLK64, nsq[hf][:])
                            for hf in range(2):
                                rsqrt_to(nrs2[hf][:], PS[6 + hf][:])
                                stt(sts[hf][:], sts[hf][:], wcol, nrs2[hf][:], ALU.mult, ALU.mult)
                            for hf in range(2):
                                sl = slice(hf * 512, (hf + 1) * 512)
                                st = sts[hf]
                                if rope:
                                    def rp(st=st, hf=hf, sl=sl, m=m):
                                        mm(PS[6 + hf][:], pmT[:], st[:])
                                        tt(rtmps[hf][:], PS[6 + hf][:], rope_sb[:, 1, sl], ALU.mult)
                                        tt(st[:], st[:], rope_sb[:, 0, sl], ALU.mult)
                                        tt(dest[:, m, sl], st[:], rtmps[hf][:], ALU.add)
                                    pend_rope.append(rp)
                                else:
                                    act(dest[:, m, sl], st[:], AF.Copy)
                                    if is_k and not lat:
                                        k.dma("sp", ko_d[l, :, m, sl], st[:])

                        def flush():
                            while pend_rope:
                                pend_rope.pop(0)()
                        f.flush = flush
                        return f
                    cf = qk_consume(qT, wq, False)
                    linear_fm(WIN, cq_, 1024, 16, hrhs, cf)
                    cf.flush()
                    ck("q1", qT=qT[:])
                    cf = qk_consume(kT, wk, True)
                    linear_fm(WIN, ck_, 1024, 16, hrhs, cf)
                    cf.flush()
                    ck("q2", kT=kT[:])
                    esv = ExitStack()
                    vT = [k.sbg(esv, "vT", [128, T], BF16) for _ in range(2)]
                    PSBv = PS[6][:].bitcast(BF16)

                    def v_consume(m, banks):
                        vt_ = vT[m % 2]
                        for hf in range(2):
                            sl = slice(hf * 512, (hf + 1) * 512)
                            if not lat:
                                st = stage[sti[0] % 4]
                                sti[0] += 1
                                act(st[:], banks[hf][:], AF.Copy)
                                k.dma("sp", vo_d[l, :, m, sl], st[:])
                                cp(vt_[:, sl], st[:])
                            else:
                                act(vt_[:, sl], banks[hf][:], AF.Copy)
                        for tl in range(8):
                            k.op("pe", "transpose", [PS[6][:]], [vt_[:], IDENT], PSBv[:, tl * 128:(tl + 1) * 128],
                                 vt_[:, tl * 128:(tl + 1) * 128], IDENT)
                        cp(v_tm[:, 0:8, m * 128:(m + 1) * 128], PSBv.rearrange("p (a b) -> p a b", a=8))
                    linear_fm(WIN, cv_, 1024, 16, hrhs, v_consume)
                    k.barrier()
                    esv.close()
                    esq.close()
                    ck("qkv%d_%s%d" % (int(diff), g, l), qT=qT[:], kT=kT[:], v_tm=v_tm[:])
                    pT = [k.sbg(es, "pT", [128, 512], BF16) for _ in range(4)]
                    pti = [0]
                    rd = [k.sbg(es, "rd", [128, 512]) for _ in range(2 if diff else 1)]
                    if diff:
                        oa = k.sbg(es, "oa", [128, 512])
                        ob2 = k.sbg(es, "ob2", [128, 512])
                    if lat:
                        qblocks = [(hf * 512, 512, list(range(10))) for hf in range(2)]
                    else:
                        qblocks = [(s0, Ls, [s0 // 128, s0 // 128 + 1]) for (s0, Ls) in segs]
                    if lat and not diff:
                        tbs = [k.sbg(es, "tbs", [128, 1920], BF16) for _ in range(4)]
                        rowoh = k.sbg(es, "rowoh", [16, 8, 128], BF16)
                        rowpen = k.sbg(es, "rowpen", [16, T], BF16)
                        k.dma("pool", rowoh[:], rowoh_d)
                        k.dma("pool", rowpen[:], rowpen_d)
                    sbi = [0]

                    def sbank():
                        b = PS[sbi[0] % 4]
                        sbi[0] += 1
                        return b
                    LOOK = 2

                    def run_pipe(steps, score_fn, consume_fn):
                        banks = {}
                        n = len(steps)
                        for i in range(min(LOOK, n)):
                            banks[i] = score_fn(steps[i])
                        for i in range(n):
                            if i + LOOK < n:
                                banks[i + LOOK] = score_fn(steps[i + LOOK])
                            consume_fn(steps[i], banks.pop(i))

                    if diff:
                        steps = []
                        gid = 0
                        for hh in range(8):
                            for (q0, nq, ktl) in qblocks:
                                si_ = 0
                                for ji, j in enumerate(ktl):
                                    for t2 in range(2):
                                        steps.append((hh, q0, nq, ji, j, t2, len(ktl), gid, si_))
                                        si_ += 1
                                gid += 1
                        oas = [oa, k.sbg(es, "oa2", [128, 512])]
                        pendf = []

                        def score_fn(st_):
                            hh, q0, nq, ji, j, t2, nkt_, gid_, si_ = st_
                            ps_ = slice(t2 * 64, (t2 + 1) * 64)
                            if si_ == 2:
                                while pendf:
                                    pendf.pop(0)()
                            sb_ = sbank()
                            mm(sb_[:, 0:nq], kT[ps_, hh, j * 128:(j + 1) * 128], qT[ps_, hh, q0:q0 + nq])
                            return sb_

                        def consume_fn(st_, sb_):
                            hh, q0, nq, ji, j, t2, nkt_, gid_, si_ = st_
                            qs = slice(q0, q0 + nq)
                            p_ = pT[pti[0] % 4]
                            pti[0] += 1
                            act(p_[:, 0:nq], sb_[:, 0:nq], AF.Exp)
                            mm(PS[4 + 2 * t2][:, 0:nq], v_tm[:, j, hh * 128:(hh + 1) * 128], p_[:, 0:nq],
                               start=(ji == 0), stop=(ji == nkt_ - 1))
                            mm(PS[5 + 2 * t2][:, 0:nq], ONES, p_[:, 0:nq], start=(ji == 0), stop=(ji == nkt_ - 1))
                            if not (ji == nkt_ - 1 and t2 == 1):
                                return
                            oa_ = oas[gid_ % 2]
                            nsq_ = nsq[gid_ % 2]
                            for t3 in range(2):
                                act(rd[t3][:, 0:nq], PS[5 + 2 * t3][:, 0:nq], AF.Ln)
                            cp(oa_[:, 0:nq], PS[4][:, 0:nq])
                            cp(ob2[:, 0:nq], PS[6][:, 0:nq])
                            for t3 in range(2):
                                act(rd[t3][:, 0:nq], rd[t3][:, 0:nq], AF.Exp, scale=-1.0)
                            tt(oa_[:, 0:nq], oa_[:, 0:nq], rd[0][:, 0:nq], ALU.mult)
                            tt(ob2[:, 0:nq], ob2[:, 0:nq], rd[1][:, 0:nq], ALU.mult)
                            stt(oa_[:, 0:nq], ob2[:, 0:nq], smallv[:, l, 0:1], oa_[:, 0:nq], ALU.mult, ALU.add)
                            act(nsq_[:, 0:nq], oa_[:, 0:nq], AF.Square)

                            def fin(oa_=oa_, nsq_=nsq_, nq=nq, hh=hh, qs=qs):
                                sb2 = sbank()
                                mm(sb2[:, 0:nq], ONES128, nsq_[:, 0:nq])
                                rsqrt_to(nrs[:, 0:nq], sb2[:, 0:nq])
                                stt(ybr[:, hh, qs], oa_[:, 0:nq], smallv[:, l, 1:2], nrs[:, 0:nq], ALU.mult, ALU.mult)
                            pendf.append(fin)
                        run_pipe(steps, score_fn, consume_fn)
                        while pendf:
                            pendf.pop(0)()
                    else:
                        if lat:
                            qblocks = [(0, 512, [0, 1, 2, 3, 4, 5, 8, 9]), (512, 512, [2, 3, 4, 5, 6, 7, 8, 9])]
                        steps = []
                        gid = 0
                        for hp in range(8):
                            for (q0, nq, ktl) in qblocks:
                                for ji, j in enumerate(ktl):
                                    for e in range(2):
                                        steps.append((hp, q0, nq, ji, j, e, len(ktl), gid))
                                gid += 1
                        tb_loaded = set()

                        def score_fn(st_):
                            hp, q0, nq, ji, j, e, nkt_, gid_ = st_
                            ps_ = slice(e * 64, (e + 1) * 64)
                            hf = q0 // 512
                            local = lat and j < 8
                            if local and hp not in tb_loaded:
                                tb_loaded.add(hp)
                                for e2 in range(2):
                                    k.dma("pool", tbs[2 * (hp % 2) + e2][:], tb_d[l, 2 * hp + e2])
                            sb_ = sbank()
                            mm(sb_[:, 0:nq], kT[ps_, hp, j * 128:(j + 1) * 128], qT[ps_, hp, q0:q0 + nq], start=True, stop=(not local))
                            if local:
                                i0 = 14 - 2 * j + 8 * hf
                                mm(sb_[:, 0:nq], IDENT, tbs[2 * (hp % 2) + e][:, i0 * 64:i0 * 64 + 512], start=False, stop=False)
                                mm(sb_[:, 0:nq], rowoh[:, j, :], rowpen[:, q0:q0 + nq], start=False, stop=True)
                            return sb_

                        def consume_fn(st_, sb_):
                            hp, q0, nq, ji, j, e, nkt_, gid_ = st_
                            ps_ = slice(e * 64, (e + 1) * 64)
                            p_ = pT[pti[0] % 4]
                            pti[0] += 1
                            act(p_[:, 0:nq], sb_[:, 0:nq], AF.Exp)
                            hh = 2 * hp + e
                            tp = (0, e * 64)
                            pn, pd = PS[4 + 2 * (gid_ % 2)], PS[5 + 2 * (gid_ % 2)]
                            mm(pn[ps_, 0:nq], v_tm[:, j, hh * 64:(hh + 1) * 64], p_[:, 0:nq],
                               start=(ji == 0), stop=(ji == nkt_ - 1), tile_position=tp)
                            mm(pd[ps_, 0:nq], ONES[:, 0:64], p_[:, 0:nq],
                               start=(ji == 0), stop=(ji == nkt_ - 1), tile_position=tp)
                            if not (ji == nkt_ - 1 and e == 1):
                                return
                            act(rd[0][:, 0:nq], pd[:, 0:nq], AF.Ln)
                            act(rd[0][:, 0:nq], rd[0][:, 0:nq], AF.Exp, scale=-1.0)
                            tt(ybr[:, hp, q0:q0 + nq], pn[:, 0:nq], rd[0][:, 0:nq], ALU.mult)
                        run_pipe(steps, score_fn, consume_fn)
                    k.barrier()

            attention_branch(True)
            ck("yb_%s%d" % (g, l), ybr=ybr[:])
            gated_branch(1, False)
            k.barrier()
            attention_branch(False)
            ck("yc_%s%d" % (g, l), ybr=ybr[:])
            gated_branch(2, False)
            k.barrier()
            ck("merged_%s%d" % (g, l), merged=merged[:])

            bg_until(BG_L0)
            k.barrier()
            del bgslots[:]
            esbg.close()
            with ExitStack() as es:
                Xres = k.sbg(es, "Xres2", [128, 16, T])
                xi = [k.sbg(es, "xi", [128, T]) for _ in range(2)]
                og1 = 32

                def wout_consume(m, banks):
                    x_ = xi[m % 2]
                    k.dma("sp", x_[:], src[:, m, :])
                    for hf in range(2):
                        sl = slice(hf * 512, (hf + 1) * 512)
                        stt(Xres[:, m, sl], banks[hf][:], mod_sb[l][:, og1 + m, gi:gi + 1], x_[:, sl], ALU.mult, ALU.add)
                linear_fm(wview(w_out, l), 0, D, 16, lambda kc, hf: merged[:, kc, hf * 512:(hf + 1) * 512], wout_consume)
                for c4 in range(4):
                    k.dma("sp", xmid1[g][:, 4 * c4:4 * c4 + 4, :], Xres[:, 4 * c4:4 * c4 + 4, :])
                rmsnorm_mod(es, Xres, 1)
                k.barrier()
                ck("x1_%s%d" % (g, l), Xres=Xres[:], h=h[:])

        with ExitStack() as es:
            aT = k.sbg(es, "aT", [128, 44, T], BF16)
            esf = ExitStack()
            if bgpos[0] < len(bg):
                bgslots.extend([k.sbg(esf, "bgslot", [128, 16, 256], BF16) for _ in range(2)])
            cb = conv_bufs(esf)
            vt = [k.sbg(esf, "vt", [128, T]) for _ in range(2)]
            gtm = k.sbg(esf, "gtm", [128, T])
            WUP = wview(w_up, l)
            for fp in range(22):
                sv = wload(WUP, fp * 256, 256, 16)
                sgt = wload(WUP, DFF + fp * 256, 256, 16)
                for mc in range(2):
                    f = 2 * fp + mc
                    banks = bankpair()
                    for kc in range(16):
                        for hf in range(2):
                            mm(banks[hf][:], sv[:, kc, mc * 128:(mc + 1) * 128], hrhs(kc, hf), start=(kc == 0), stop=(kc == 15))
                    conv_evac(cb, banks, "fcw", "fcb", f, 88, vt[mc][:], AF.Identity)
                for mc in range(2):
                    f = 2 * fp + mc
                    banks = bankpair()
                    for kc in range(16):
                        for hf in range(2):
                            mm(banks[hf][:], sgt[:, kc, mc * 128:(mc + 1) * 128], hrhs(kc, hf), start=(kc == 0), stop=(kc == 15))
                    conv_evac(cb, banks, "fcw", "fcb", 44 + f, 88, gtm[:], AF.Silu)
                    tt(aT[:, f, :], vt[mc][:], gtm[:], ALU.mult)
                bg_step(2)
            bg_until(len(bg))
            k.barrier()
            del bgslots[:]
            esf.close()
            ck("aT_%s%d" % (g, l), aT=aT[:])
            dsl_own = k.sbg(es, "dslot", [128, 44, 256], BF16)
            dsl_h = h[:, 0:11, :].rearrange("p a (b c) -> p (a b) c", c=256)
            xi = [k.sbg(es, "xi2", [128, T]) for _ in range(2)]
            xo = [k.sbg(es, "xo", [128, T]) for _ in range(2)]
            WDN = wview(w_dn, l)
            og2 = 80
            for mb in range(8):
                ds_ = dsl_own[:] if mb % 2 == 0 else dsl_h
                for k0 in range(0, 44, 11):
                    k.dma("pool", ds_[:, k0:k0 + 11, :], WDN[:, k0:k0 + 11, mb * 256:(mb + 1) * 256])
                for mc in range(2):
                    m = 2 * mb + mc
                    x_ = xi[m % 2]
                    k.dma("sp", x_[:], xmid1[g][:, m, :])
                    banks = bankpair()
                    for kc in range(44):
                        for hf in range(2):
                            mm(banks[hf][:], ds_[:, kc, mc * 128:(mc + 1) * 128], aT[:, kc, hf * 512:(hf + 1) * 512],
                               start=(kc == 0), stop=(kc == 43))
                    o_ = xo[m % 2]
                    for hf in range(2):
                        sl = slice(hf * 512, (hf + 1) * 512)
                        stt(o_[:, sl], banks[hf][:], mod_sb[l][:, og2 + m, gi:gi + 1], x_[:, sl], ALU.mult, ALU.add)
                    k.dma("sp", dst[:, m, :], o_[:])
            k.barrier()
        if stop == "pass_%s%d" % (g, l):
            with ExitStack() as es:
                rb = k.sbg(es, "rb", [128, 16, T])
                k.dma("sp", rb[:], dst)
                ck("pass_%s%d" % (g, l), x2=rb[:])

    try:
        for g in ("ctx", "lat"):
            for l in range(DEPTH):
                if not stopped and (only is None or (g, l) in only):
                    run_pass(l, g)
    except _Stop:
        pass
    k.finish()
    print("program: %d instructions, %d dma sems" % (nc.n_instructions(), len(k.dsems)))
    return nc


def _fm(v):
    v = np.asarray(v, np.float32)
    return np.ascontiguousarray(v.reshape(-1, 128).T)


def _structural():
    idx = np.arange(128)
    kk, ll = idx[:, None], idx[None, :]
    matb = np.zeros((128, N_MATS, 128), np.float32)
    matb[:, 0] = np.eye(128)
    matb[:, 1] = 1.0
    matb[:, 2] = 1.0 / 2048
    matb[:, 3] = 1.0 / 512
    matb[:, 4] = 1.0 / 128
    matb[:, 5] = ((kk // 64) == (ll // 64)) / 64.0
    matf = np.zeros((128, N_MATF, 128), np.float32)
    matf[:, 0] = kk <= ll
    matf[:, 1] = kk >= ll
    matf[:, 2] = kk < ll
    matf[:, 3] = kk > ll
    matf[:, 4] = 1.0
    t = np.arange(T)
    row = (t // 64).astype(np.float32)
    col = (t % 64).astype(np.float32)
    inv = (10000.0 ** (-np.arange(16, dtype=np.float32) / 16)).astype(np.float32)
    rope = np.zeros((128, 2, T), np.float32)
    pm = np.zeros((128, 128), np.float32)
    for p in range(128):
        d = p % 64
        pos = row if d < 32 else col
        ang = pos * inv[d % 16]
        rope[p, 0] = np.cos(ang)
        rope[p, 1] = np.sin(ang)
        if d % 32 < 16:
            pm[p, p + 16] = -1.0
        else:
            pm[p, p - 16] = 1.0
    pmT = np.ascontiguousarray(pm.T)
    rowoh = np.zeros((16, 8, 128), np.float32)
    for j in range(8):
        for kx in range(128):
            rowoh[2 * j + kx // 64, j, kx] = 1.0
    rowpen = np.zeros((16, T), np.float32)
    for qr in range(16):
        w0 = min(max(qr - 4, 0), 8)
        for kr in range(16):
            if not (w0 <= kr < w0 + 8):
                rowpen[kr, qr * 64:(qr + 1) * 64] = NEG
    return matb, matf, rope, pmT, rowoh, rowpen


def _na_tables(rpb):
    kc = np.arange(64)[:, None]
    qc = np.arange(64)[None, :]
    wst = np.clip(qc - 8, 0, 48)
    colok = (kc >= wst) & (kc < wst + 16)
    dc = np.clip(kc - qc + 15, 0, 30)
    out = np.full((DEPTH, 16, 2, 64, 30, 64), NEG, np.float32)
    for krl in range(2):
        for i in range(30):
            a = 21 - i + krl
            if 0 <= a <= 14:
                blk = rpb[:, :, a][:, :, dc]
                out[:, :, krl, :, i, :] = np.where(colok[None, None], blk, NEG)
    return np.ascontiguousarray(out.reshape(DEPTH, 16, 128, 1920))


_PROG = {}


def kernel(**inp):
    f32 = lambda a: np.ascontiguousarray(np.asarray(a, np.float32))
    x_prompt, x_sample = f32(inp["x_prompt"]), f32(inp["x_sample"])
    matb, matf, rope, pmT, rowoh, rowpen = _structural()
    tb = _na_tables(f32(inp["na_rpb"]))
    shared = {}
    for nm in ["w_ada", "w_in", "w_branch_a", "w_branch_b", "w_branch_c", "w_out", "ffn_w_up", "ffn_w_down"]:
        shared[nm] = f32(inp[nm])
    shared.update(matb=matb, matf=matf, rope=rope, pmT=pmT, rowoh=rowoh, rowpen=rowpen, tb=tb)

    def to_fm(x):
        return np.ascontiguousarray(x.reshape(T, 16, 128).transpose(2, 1, 0))

    in_maps = []
    for ci in range(NCORES):
        sb = ci // 4
        m = dict(shared)
        m["xp"] = to_fm(x_prompt[4 * ci:4 * ci + 4].reshape(T, D))
        m["xs"] = to_fm(x_sample[sb])
        cst = np.zeros((128, COLS.n), np.float32)

        def put(name, arr):
            o, w = COLS(name)
            assert arr.shape == (128, w), (name, arr.shape, w)
            cst[:, o:o + w] = arr
        cs = np.stack([_fm(inp["c_ctx"]), _fm(inp["c"][sb])], axis=2)
        put("csil", cs.reshape(128, 32))
        for l in range(DEPTH):
            p = "L%d_" % l
            put(p + "n1w", _fm(inp["norm1_w"][l]))
            put(p + "n2w", _fm(inp["norm2_w"][l]))
            put(p + "bada", _fm(inp["b_ada"][l]))
            put(p + "scw", np.concatenate([_fm(inp["ssd_conv_w"][l][j]) for j in range(3)], axis=1))
            put(p + "scb", _fm(inp["ssd_conv_b"][l]))
            put(p + "fcw", np.concatenate([_fm(inp["ffn_conv_w"][l][j]) for j in range(3)], axis=1))
            put(p + "fcb", _fm(inp["ffn_conv_b"][l]))
            put(p + "snw", _fm(inp["ssd_norm_w"][l]))
            put(p + "sdd", _fm(np.repeat(f32(inp["ssd_d"][l]), 64)))
            rep2 = lambda v: np.tile(f32(v), 2).reshape(128, 1)
            put(p + "dqn", rep2(inp["diff_q_norm"][l]))
            put(p + "dkn", rep2(inp["diff_k_norm"][l]))
            put(p + "sub", f32(inp["diff_subln_w"][l]).reshape(128, 1))
            put(p + "nqn", rep2(inp["na_q_norm"][l]))
            put(p + "nkn", rep2(inp["na_k_norm"][l]))
            put(p + "dtb", np.tile(f32(inp["ssd_dt_bias"][l]).reshape(1, 32), (128, 1)))
            put(p + "alog", np.tile(f32(inp["ssd_a_log"][l]).reshape(1, 32), (128, 1)))
            put(p + "lam", np.tile(f32(inp["diff_lam"][l]).reshape(1, 256), (128, 1)))
        m["cst"] = cst
        cdk = f32(inp["cache_diff_k"][sb])
        m["cdk"] = np.ascontiguousarray(cdk.transpose(0, 3, 4, 2, 1).reshape(DEPTH, 128, 8, 256))
        cdv = f32(inp["cache_diff_v"][sb])
        m["cdv"] = np.ascontiguousarray(cdv.reshape(DEPTH, 2, 128, 1024).transpose(0, 2, 1, 3))
        cnk = f32(inp["cache_na_k"][sb])
        m["cnk"] = np.ascontiguousarray(cnk.reshape(DEPTH, 256, 8, 128).transpose(0, 3, 2, 1))
        cnv = f32(inp["cache_na_v"][sb])
        m["cnv"] = np.ascontiguousarray(cnv.reshape(DEPTH, 2, 128, 1024).transpose(0, 2, 1, 3))
        sst = f32(inp["state_ssm"][sb])
        m["sst"] = np.ascontiguousarray(sst.transpose(0, 4, 1, 2, 3).reshape(DEPTH, 128, 2048))
        in_maps.append(m)

    if "nc" not in _PROG:
        _PROG["nc"] = build_program()
    res = run_bass_kernel_spmd(_PROG["nc"], in_maps, core_ids=list(range(NCORES)))
    R = res.results

    def from_fm(a):
        return a.transpose(2, 1, 0).reshape(T, D)

    B, S = 32, 256
    y_p = np.zeros((B, S, D), np.float32)
    y_s = np.zeros((2, 1024, D), np.float32)
    ndk = np.zeros((B, DEPTH, S, 8, 2, 64), np.float32)
    ndv = np.zeros((B, DEPTH, S, 8, 128), np.float32)
    nnk = np.zeros((B, DEPTH, S, 16, 64), np.float32)
    nnv = np.zeros((B, DEPTH, S, 16, 64), np.float32)
    nss = np.zeros((B, DEPTH, 2, 16, 64, 128), np.float32)
    for ci in range(NCORES):
        r = R[ci]
        y_p[4 * ci:4 * ci + 4] = from_fm(np.asarray(r["ypT"])).reshape(4, S, D)
        if ci % 4 == 0:
            y_s[ci // 4] = from_fm(np.asarray(r["ysT"]))
        kd = np.asarray(r["ndk"])
        ndk[4 * ci:4 * ci + 4] = kd.reshape(DEPTH, 2, 64, 8, 4, S).transpose(4, 0, 5, 3, 1, 2)
        vd = np.asarray(r["ndv"])
        ndv[4 * ci:4 * ci + 4] = vd.reshape(DEPTH, 128, 8, 4, S).transpose(3, 0, 4, 2, 1)
        kn = np.asarray(r["nnk"])
        nnk[4 * ci:4 * ci + 4] = kn.reshape(DEPTH, 2, 64, 8, 4, S).transpose(4, 0, 5, 3, 1, 2).reshape(4, DEPTH, S, 16, 64)
        vn = np.asarray(r["nnv"])
        nnv[4 * ci:4 * ci + 4] = vn.reshape(DEPTH, 2, 64, 8, 4, S).transpose(4, 0, 5, 3, 1, 2).reshape(4, DEPTH, S, 16, 64)
        ss = np.asarray(r["nss"])
        nss[4 * ci:4 * ci + 4] = ss.reshape(DEPTH, 4, 128, 2, 16, 64).transpose(1, 0, 3, 4, 5, 2)
    return (y_p, y_s, ndk, ndv, nnk, nnv, nss)
```

```python
import math
from contextlib import ExitStack
import numpy as np
import concourse.bass as bass
import concourse.mybir as mybir
from concourse.bass_utils import run_bass_kernel_spmd

F32 = mybir.dt.float32
BF16 = mybir.dt.bfloat16
AF = mybir.ActivationFunctionType
ALU = mybir.AluOpType
AX = mybir.AxisListType

D = 2048
T = 1024
EPS = 1e-6
NCORES = 8
DEPTH = 2
IN_COLS = 14880
DFF = 5632
C_Z, C_XBC, C_DT, C_QD, C_KD, C_VD, C_QN, C_KN, C_VN, C_G = 0, 1024, 2560, 2592, 3616, 4640, 5664, 6688, 7712, 8736
NEG = -30000.0


class _Rec:
    __slots__ = ("w", "r")

    def __init__(self):
        self.w = None
        self.r = {}


class KB:
    def __init__(self):
        self.nc = bass.Bass("TRN2", target_bir_lowering=False)
        nc = self.nc
        self.eng = {"pe": nc.tensor, "act": nc.scalar, "dve": nc.vector, "pool": nc.gpsimd, "sp": nc.sync}
        self.sems = {e: nc.alloc_semaphore("sem_" + e) for e in self.eng}
        self.cnt = {e: 0 for e in self.eng}
        self.waited = {e: {} for e in self.eng}
        self.recs = {}
        self.dsems = []
        self.dma_of = {}
        self.free_d = []
        self.tracked = set()
        self.dram = set()
        self.uid = 0

    def name(self, base):
        self.uid += 1
        return "%s_%d" % (base, self.uid)

    def track(self, t):
        self.tracked.add(t.name)
        return t

    def sb(self, name, shape, dtype=F32):
        return self.track(self.nc.alloc_sbuf_tensor(self.name(name), list(shape), dtype))

    def sbg(self, es, name, shape, dtype=F32):
        t = es.enter_context(self.nc.sbuf_tensor(self.name(name), list(shape), dtype))
        return self.track(t)

    def ps(self, name, shape, dtype=F32):
        return self.track(self.nc.alloc_psum_tensor(name, list(shape), dtype))

    def _recs_for(self, tn):
        if tn not in self.recs:
            self.recs[tn] = _Rec()
        return self.recs[tn]

    def _split(self, outs, ins):
        rd, wr = [], []
        for a in ins:
            if a is None or isinstance(a, (int, float)):
                continue
            tn = a.tensor.name
            if tn in self.tracked:
                rd.append(tn)
        for a in outs:
            tn = a.tensor.name
            if tn in self.tracked:
                wr.append(tn)
        return rd, wr

    def _deps(self, rd, wr):
        deps = {}

        def need(s, v):
            if deps.get(s, 0) < v:
                deps[s] = v
        for tn in rd:
            rec = self._recs_for(tn)
            if rec.w is not None:
                need(*rec.w)
            if tn.startswith("psb"):
                for s, v in rec.r.items():
                    need(s, v)
        for tn in wr:
            rec = self._recs_for(tn)
            if rec.w is not None:
                need(*rec.w)
            for s, v in rec.r.items():
                need(s, v)
        return deps

    def _semobj(self, s):
        if s in self.sems:
            return self.sems[s]
        return self.dsems[int(s[1:])][0]

    def _waits(self, e, deps):
        for s, v in deps.items():
            if e == "pe" and s == "pe":
                continue
            if self.waited[e].get(s, 0) >= v:
                continue
            self.eng[e].wait_ge(self._semobj(s), v)
            self.waited[e][s] = v

    def _update(self, rd, wr, s, v):
        for tn in rd:
            self.recs[tn].r[s] = v
        for tn in wr:
            rec = self.recs[tn]
            rec.w = (s, v)
            rec.r = {}

    def op(self, e, fname, outs, ins, *args, **kw):
        rd, wr = self._split(outs, ins)
        self._waits(e, self._deps(rd, wr))
        i = getattr(self.eng[e], fname)(*args, **kw)
        self.cnt[e] += 1
        i.then_inc(self.sems[e], 1)
        self._update(rd, wr, e, self.cnt[e])
        return i

    def dma(self, q, out, in_):
        otn, itn = out.tensor.name, in_.tensor.name
        rd, wr = [], []
        if itn in self.tracked:
            rd.append(itn)
        if otn in self.tracked:
            wr.append(otn)
        if otn in self.tracked and otn not in self.dram:
            sbn = otn
        else:
            sbn = itn
        self._waits(q, self._deps(rd, wr))
        idx = self.dma_of.get(sbn)
        if idx is None:
            if self.free_d:
                idx = self.free_d.pop()
            else:
                idx = len(self.dsems)
                self.dsems.append([self.nc.alloc_semaphore("dsem%d" % idx), 0])
            self.dma_of[sbn] = idx
        ds = self.dsems[idx]
        i = self.eng[q].dma_start(out=out, in_=in_)
        ds[1] += 16
        i.then_inc(ds[0], 16)
        self._update(rd, wr, "D%d" % idx, ds[1])
        return i

    def barrier(self):
        for e in self.eng:
            deps = {}
            for f in self.eng:
                if f != e and self.cnt[f] > 0:
                    deps[f] = self.cnt[f]
            for i, (s, c) in enumerate(self.dsems):
                if c > 0:
                    deps["D%d" % i] = c
            self._waits(e, deps)
        for rec in self.recs.values():
            rec.w = None
            rec.r = {}
        self.dma_of = {}
        self.free_d = list(range(len(self.dsems)))

    def finish(self):
        for i, (s, c) in enumerate(self.dsems):
            if c > 0 and self.waited["sp"].get("D%d" % i, 0) < c:
                self.eng["sp"].wait_ge(s, c)


class Cols:
    def __init__(self):
        self.off = {}
        self.n = 0

    def add(self, name, w):
        self.off[name] = (self.n, w)
        self.n += w

    def __call__(self, name):
        return self.off[name]


def make_cols():
    c = Cols()
    c.add("csil", 32)
    for l in range(DEPTH):
        p = "L%d_" % l
        for nm, w in [("n1w", 16), ("n2w", 16), ("bada", 96), ("scw", 36), ("scb", 12), ("fcw", 264), ("fcb", 88),
                      ("snw", 8), ("sdd", 8), ("dqn", 1), ("dkn", 1), ("sub", 1), ("nqn", 1), ("nkn", 1),
                      ("dtb", 32), ("alog", 32), ("lam", 256)]:
            c.add(p + nm, w)
    return c


COLS = make_cols()
N_MATS = 6
N_MATF = 5


class _Stop(Exception):
    pass


MARKS = []


def build_program(stop=None, only=None):
    k = KB()
    nc = k.nc
    taps = {}

    def ck(label, **aps):
        MARKS.append((label, k.cnt["pe"]))
        if stop != label:
            return
        k.barrier()
        for nm, ap in aps.items():
            d = nc.dram_tensor("dbg_" + nm, list(ap.shape), ap.dtype, kind="ExternalOutput").ap()
            k.dma("sp", d, ap)
            taps[nm] = d
        raise _Stop()

    def din(name, shape, dt=F32):
        return nc.dram_tensor(name, list(shape), dt, kind="ExternalInput").ap()

    def dout(name, shape, dt=F32):
        return nc.dram_tensor(name, list(shape), dt, kind="ExternalOutput").ap()

    def dscr(name, shape, dt=F32):
        t = nc.dram_tensor(name, list(shape), dt, kind="Internal")
        k.tracked.add(t.name)
        k.dram.add(t.name)
        return t.ap()

    xin = {"ctx": din("xp", [128, 16, T]), "lat": din("xs", [128, 16, T])}
    yout = {"ctx": dout("ypT", [128, 16, T]), "lat": dout("ysT", [128, 16, T])}
    cst_d = din("cst", [128, COLS.n])
    matb_d = din("matb", [128, N_MATS, 128])
    matf_d = din("matf", [128, N_MATF, 128])
    rope_d = din("rope", [128, 2, T])
    pmt_d = din("pmT", [128, 128])
    tb_d = din("tb", [DEPTH, 16, 128, 1920])
    rowoh_d = din("rowoh", [16, 8, 128])
    rowpen_d = din("rowpen", [16, T])
    cdk_d = din("cdk", [DEPTH, 128, 8, 256])
    cdv_d = din("cdv", [DEPTH, 128, 2, 1024])
    cnk_d = din("cnk", [DEPTH, 128, 8, 256])
    cnv_d = din("cnv", [DEPTH, 128, 2, 1024])
    sst_d = din("sst", [DEPTH, 128, 2048])
    w_ada = din("w_ada", [DEPTH, D, 6 * D])
    w_in = din("w_in", [DEPTH, D, IN_COLS])
    w_br = [din("w_branch_a", [DEPTH, 1024, D]), din("w_branch_b", [DEPTH, 1024, D]), din("w_branch_c", [DEPTH, 1024, D])]
    w_out = din("w_out", [DEPTH, D, D])
    w_up = din("ffn_w_up", [DEPTH, D, 2 * DFF])
    w_dn = din("ffn_w_down", [DEPTH, DFF, D])
    ndk_o = dout("ndk", [DEPTH, 128, 8, T])
    ndv_o = dout("ndv", [DEPTH, 128, 8, T])
    nnk_o = dout("nnk", [DEPTH, 128, 8, T])
    nnv_o = dout("nnv", [DEPTH, 128, 8, T])
    nss_o = dout("nss", [DEPTH, 4, 128, 2048])
    xmid1 = {g: dscr("xmid1_" + g, [128, 16, T]) for g in ("ctx", "lat")}
    xmid2 = {g: dscr("xmid2_" + g, [128, 16, T]) for g in ("ctx", "lat")}

    cst = k.sb("cst", [128, COLS.n])
    matb = k.sb("matb", [128, N_MATS, 128], BF16)
    matf = k.sb("matf", [128, N_MATF, 128])
    h = k.sb("h", [128, 16, T], BF16)
    wslots = [k.sb("wslot%d" % i, [128, 16, 256], BF16) for i in range(2)]
    mod_sb = [k.sb("mod%d" % l, [128, 96, 2]) for l in range(DEPTH)]
    der = k.sb("der", [128, DEPTH, 2, 2, 16])
    smallv = k.sb("smallv", [128, DEPTH, 8])
    nega = k.sb("nega", [128, DEPTH, 32])
    PS = [k.ps("psb%d" % i, [128, 512]) for i in range(8)]
    IDENT, ONES, ONES2048, ONES512, ONES128, BLK64 = [matb[:, i, :] for i in range(6)]
    M_LE, M_GE, M_LT, M_GT, ONESF = [matf[:, i, :] for i in range(5)]

    def ccol(name, i=0, w=1):
        o, _ = COLS(name)
        return cst[:, o + i:o + i + w]

    def mm(out, lhsT, rhs, start=True, stop=True, **kw):
        return k.op("pe", "matmul", [out], [lhsT, rhs], out, lhsT, rhs, start=start, stop=stop, **kw)

    def act(out, in_, func, bias=None, scale=None):
        kw = {}
        ins = [in_]
        if bias is not None:
            kw["bias"] = bias
            ins.append(bias)
        if scale is not None:
            kw["scale"] = scale
            ins.append(scale)
        return k.op("act", "activation", [out], ins, out=out, in_=in_, func=func, **kw)

    def tt(out, in0, in1, op, e="dve"):
        return k.op(e, "tensor_tensor", [out], [in0, in1], out=out, in0=in0, in1=in1, op=op)

    def ts(out, in0, s1, s2, op0, op1=None, e="dve"):
        kw = dict(out=out, in0=in0, scalar1=s1, scalar2=s2, op0=op0)
        if op1 is not None:
            kw["op1"] = op1
        return k.op(e, "tensor_scalar", [out], [in0, s1, s2], **kw)

    def stt(out, in0, scalar, in1, op0, op1):
        return k.op("dve", "scalar_tensor_tensor", [out], [in0, scalar, in1], out=out, in0=in0, scalar=scalar,
                    in1=in1, op0=op0, op1=op1)

    def cp(out, in_, e="dve"):
        return k.op(e, "tensor_copy", [out], [in_], out=out, in_=in_)

    def rsqrt_to(out, in_):
        act(out, in_, AF.Ln, bias=EPS)
        act(out, out, AF.Exp, scale=-0.5)

    wrr = [0]

    def wslot():
        s = wslots[wrr[0] % len(wslots)]
        wrr[0] += 1
        return s

    def wload(W3, c0, w, KC):
        s = wslot()
        hk = max(1, KC // 2)
        for k0 in range(0, KC, hk):
            k.dma("pool", s[:, k0:k0 + hk, :w], W3[:, k0:k0 + hk, c0:c0 + w])
        return s

    prr = [0]

    def bankpair():
        i = prr[0] % 3
        prr[0] += 1
        return [PS[2 * i], PS[2 * i + 1]]

    def linear_fm(W3, col0, ncols, KC, rhs, consume, banks_fn=bankpair, defer=1):
        m = 0
        pend = []
        for c0 in range(col0, col0 + ncols, 256):
            w = min(256, col0 + ncols - c0)
            s = wload(W3, c0, w, KC)
            for mc in range(w // 128):
                banks = banks_fn()
                for kc in range(KC):
                    for hf in range(2):
                        mm(banks[hf][:], s[:, kc, mc * 128:(mc + 1) * 128], rhs(kc, hf), start=(kc == 0), stop=(kc == KC - 1))
                pend.append((m, banks))
                if len(pend) > defer:
                    consume(*pend.pop(0))
                m += 1
            bg_step()
        while pend:
            consume(*pend.pop(0))

    def wview(w, l):
        return w[l].rearrange("(c p) n -> p c n", p=128)

    k.dma("sp", cst[:], cst_d)
    k.dma("pool", matb[:], matb_d)
    k.dma("sp", matf[:], matf_d)
    scb = k.sb("scb", [128, 16, 2], BF16)
    o, _ = COLS("csil")
    act(scb[:].rearrange("p a b -> p (a b)"), cst[:, o:o + 32], AF.Silu)
    def mod_block(l, b):
        W3 = wview(w_ada, l)
        ob, _ = COLS("L%d_bada" % l)
        if bgslots:
            s = bgslots[b % 2]
            for k0 in (0, 8):
                k.dma("pool", s[:, k0:k0 + 8, :], W3[:, k0:k0 + 8, b * 256:(b + 1) * 256])
        else:
            s = wload(W3, b * 256, 256, 16)
        for mc in range(2):
            m = 2 * b + mc
            pb = PS[7]
            for kc in range(16):
                mm(pb[:, 2 * mc:2 * mc + 2], s[:, kc, mc * 128:(mc + 1) * 128], scb[:, kc, :], start=(kc == 0), stop=(kc == 15))
            tt(mod_sb[l][:, m, :], pb[:, 2 * mc:2 * mc + 2], cst[:, ob + m:ob + m + 1].broadcast_to([128, 2]), ALU.add)

    def mod_der(l, which):
        onw, _ = COLS("L%d_%s" % (l, "n1w" if which == 0 else "n2w"))
        base = 16 if which == 0 else 64
        for g in range(2):
            stt(der[:, l, g, which, :], mod_sb[l][:, base:base + 16, g], 1.0, cst[:, onw:onw + 16], ALU.add, ALU.mult)

    bg = []
    bgslots = []
    for b in range(16):
        mod_block(0, b)
    mod_der(0, 0)
    for b in range(16, 48):
        bg.append(lambda b=b: mod_block(0, b))
    bg.append(lambda: mod_der(0, 1))
    BG_L0 = len(bg)
    for b in range(48):
        bg.append(lambda b=b: mod_block(1, b))
    bg.append(lambda: (mod_der(1, 0), mod_der(1, 1)))
    bgpos = [0]

    def bg_step(n=1, force=False):
        if not force and not bgslots:
            return
        for _ in range(n):
            if bgpos[0] < len(bg):
                bg[bgpos[0]]()
                bgpos[0] += 1

    def bg_until(pos):
        while bgpos[0] < min(pos, len(bg)):
            bg_step(force=True)

    for l in range(DEPTH):
        lam_init = 0.8 - 0.6 * math.exp(-0.3 * l)
        ol, _ = COLS("L%d_lam" % l)
        ltmp = k.sb("ltmp", [128, 128])
        lred = k.sb("lred", [128, 2])
        tt(ltmp[:, 0:64], cst[:, ol:ol + 64], cst[:, ol + 64:ol + 128], ALU.mult)
        tt(ltmp[:, 64:128], cst[:, ol + 128:ol + 192], cst[:, ol + 192:ol + 256], ALU.mult)
        k.op("dve", "tensor_reduce", [lred[:]], [ltmp[:]], out=lred[:], in_=ltmp[:].rearrange("p (a b) -> p a b", a=2),
             axis=AX.X, op=ALU.add)
        act(lred[:], lred[:], AF.Exp)
        tt(smallv[:, l, 0:1], lred[:, 1:2], lred[:, 0:1], ALU.subtract)
        ts(smallv[:, l, 0:1], smallv[:, l, 0:1], -lam_init, None, ALU.add)
        ts(smallv[:, l, 1:2], ccol("L%d_sub" % l), 1.0 - lam_init, None, ALU.mult)
        ts(smallv[:, l, 2:3], ccol("L%d_dqn" % l), 0.125, None, ALU.mult)
        ts(smallv[:, l, 3:4], ccol("L%d_nqn" % l), 0.125, None, ALU.mult)
        oa, _ = COLS("L%d_alog" % l)
        act(nega[:, l, :], cst[:, oa:oa + 32], AF.Exp)
        ts(nega[:, l, :], nega[:, l, :], -1.0, None, ALU.mult)

    stopped = False
    try:
        ck("setup", mod0=mod_sb[0][:], der=der[:], smallv=smallv[:], nega=nega[:])
    except _Stop:
        stopped = True
    GROUPS = {"ctx": dict(gi=0, segs=[(0, 256), (256, 256), (512, 256), (768, 256)]),
              "lat": dict(gi=1, segs=[(0, 1024)])}

    def run_pass(l, g):
        G = GROUPS[g]
        gi = G["gi"]
        segs = G["segs"]
        nseg = len(segs)
        L = segs[0][1]
        lat = (g == "lat")
        P = "L%d_" % l
        src = xin[g] if l == 0 else xmid2[g]
        dst = xmid2[g] if l == 0 else yout[g]
        WIN = wview(w_in, l)

        def hrhs(kc, hf):
            return h[:, kc, hf * 512:(hf + 1) * 512]

        def rmsnorm_mod(es, Xres, which):
            sq = [k.sbg(es, "sq", [128, 512], BF16) for _ in range(2)]
            tmp = [k.sbg(es, "ntmp", [128, 512]) for _ in range(2)]
            rs = k.sbg(es, "rs", [128, 512])
            sh_base = 0 if which == 0 else 48
            for hf in range(2):
                sl = slice(hf * 512, (hf + 1) * 512)
                for c in range(16):
                    act(sq[c % 2][:], Xres[:, c, sl], AF.Square)
                    mm(PS[6][:], ONES2048, sq[c % 2][:], start=(c == 0), stop=(c == 15))
                rsqrt_to(rs[:], PS[6][:])
                for c in range(16):
                    stt(tmp[c % 2][:], Xres[:, c, sl], der[:, l, gi, which, c:c + 1], rs[:], ALU.mult, ALU.mult)
                    act(h[:, c, sl], tmp[c % 2][:], AF.Identity, bias=mod_sb[l][:, sh_base + c, gi:gi + 1])

        if l == 1:
            bg_until(len(bg))
        with ExitStack() as es:
            Xres = k.sbg(es, "Xres", [128, 16, T])
            for c4 in range(4):
                k.dma("sp", Xres[:, 4 * c4:4 * c4 + 4, :], src[:, 4 * c4:4 * c4 + 4, :])
            rmsnorm_mod(es, Xres, 0)
            k.barrier()
        ck("norm1_%s%d" % (g, l), h=h[:])

        def conv_bufs(es):
            ub = k.sbg(es, "ub", [128, 1032])
            k.op("dve", "memset", [ub[:]], [], ub[:], 0.0)
            accs = [k.sbg(es, "cacc", [128, T]) for _ in range(2)]
            return dict(ub=ub, accs=accs, i=0)

        def conv_evac(cb, banks, wname, bname, ci, nch, out_ap, func):
            ub = cb["ub"]
            ubv = ub[:, 0:nseg * (L + 2)].rearrange("p (s l) -> p s l", s=nseg)
            acc = cb["accs"][cb["i"] % 2]
            cb["i"] += 1
            accv = acc[:].rearrange("p (s l) -> p s l", s=nseg)
            for hf in range(2):
                if lat:
                    act(ub[:, 1 + 512 * hf:513 + 512 * hf], banks[hf][:], AF.Copy)
                else:
                    act(ubv[:, 2 * hf:2 * hf + 2, 1:L + 1], banks[hf][:].rearrange("p (s l) -> p s l", s=2), AF.Copy)
            ow, _ = COLS(P + wname)
            obb, _ = COLS(P + bname)
            w0 = cst[:, ow + 0 * nch + ci:ow + 0 * nch + ci + 1]
            w1 = cst[:, ow + 1 * nch + ci:ow + 1 * nch + ci + 1]
            w2 = cst[:, ow + 2 * nch + ci:ow + 2 * nch + ci + 1]
            bb = cst[:, obb + ci:obb + ci + 1]
            ts(accv, ubv[:, :, 1:L + 1], w1, bb, ALU.mult, ALU.add)
            stt(accv, ubv[:, :, 0:L], w0, accv, ALU.mult, ALU.add)
            stt(accv, ubv[:, :, 2:L + 2], w2, accv, ALU.mult, ALU.add)
            if func is not None:
                act(out_ap, acc[:], func)
            return acc

        with ExitStack() as es2:
            ybr = k.sbg(es2, "ybr", [128, 8, T], BF16)
            sg_all = [k.sbg(es2, "sg", [128, 512]) for _ in range(4)]

            def gated_branch(bi, first):
                WB = wview(w_br[bi], l)
                sg = sg_all
                gcol = C_G + bi * D
                esb = ExitStack()
                bsl = [k.sbg(esb, "bslot", [128, 8, 256], BF16) for _ in range(2)]
                for c0 in range(0, D, 256):
                    sgw = wload(WIN, gcol + c0, 256, 16)
                    sbw = bsl[(c0 // 256) % 2]
                    k.dma("pool", sbw[:], WB[:, :, c0:c0 + 256])
                    for mc in range(2):
                        m = c0 // 128 + mc
                        gb = [PS[0], PS[1]] if m % 2 == 0 else [PS[2], PS[3]]
                        pb = [PS[4], PS[5]] if m % 2 == 0 else [PS[6], PS[7]]
                        for kc in range(16):
                            for hf in range(2):
                                mm(gb[hf][:], sgw[:, kc, mc * 128:(mc + 1) * 128], hrhs(kc, hf), start=(kc == 0), stop=(kc == 15))
                        for kc in range(8):
                            for hf in range(2):
                                mm(pb[hf][:], sbw[:, kc, mc * 128:(mc + 1) * 128], ybr[:, kc, hf * 512:(hf + 1) * 512],
                                   start=(kc == 0), stop=(kc == 7))
                        for hf in range(2):
                            sl = slice(hf * 512, (hf + 1) * 512)
                            sg_ = sg[2 * (m % 2) + hf]
                            act(sg_[:], gb[hf][:], AF.Sigmoid)
                            if first:
                                tt(merged[:, m, sl], sg_[:], pb[hf][:], ALU.mult)
                            else:
                                tt(sg_[:], sg_[:], pb[hf][:], ALU.mult)
                                tt(merged[:, m, sl], sg_[:], merged[:, m, sl], ALU.add)
                    bg_step()
                k.barrier()
                esb.close()

            with ExitStack() as es:
                xbcT = k.sbg(es, "xbcT", [128, 12, T], BF16)
                yT = k.sbg(es, "yT", [128, 8, T], BF16)
                sinr = k.sbg(es, "sinr", [128, 8, 1024], BF16)
                with ExitStack() as esc:
                    cb = conv_bufs(esc)
                    linear_fm(WIN, C_XBC, 1536, 16, hrhs,
                              lambda m, banks: conv_evac(cb, banks, "scw", "scb", m, 12, xbcT[:, m, :], AF.Silu))
                    k.barrier()
                ck("xbc_%s%d" % (g, l), xbcT=xbcT[:])
                wdt = wslot()
                k.dma("pool", wdt[:, :, 0:32], WIN[:, :, C_DT:C_DT + 32])
                for tl in range(8):
                    for kc in range(16):
                        mm(PS[4][:, tl * 32:(tl + 1) * 32], h[:, kc, tl * 128:(tl + 1) * 128], wdt[:, kc, 0:32],
                           start=(kc == 0), stop=(kc == 15))
                dt = k.sbg(es, "dt", [128, 8, 32])
                la = k.sbg(es, "la", [128, 8, 32])
                odt, _ = COLS(P + "dtb")
                tt(dt[:], PS[4][:, 0:256].rearrange("p (a b) -> p a b", a=8),
                   cst[:, odt:odt + 32].unsqueeze(1).broadcast_to([128, 8, 32]), ALU.add)
                act(dt[:], dt[:], AF.Exp)
                act(dt[:], dt[:], AF.Ln, bias=1.0)
                tt(la[:], dt[:], nega[:, l, :].unsqueeze(1).broadcast_to([128, 8, 32]), ALU.mult)
                ck("dt_%s%d" % (g, l), dt=dt[:], la=la[:], xbcT=xbcT[:])

                ess = ExitStack()
                xs_tm = k.sbg(ess, "xs_tm", [128, 1024], BF16)
                b_tm = k.sbg(ess, "b_tm", [128, 256], BF16)
                xdd = k.sbg(ess, "xdd", [128, 1024], BF16)
                xdf = k.sbg(ess, "xdf", [128, 1024], BF16)
                xdr = k.sbg(ess, "xdr", [128, 1024], BF16)
                dec = k.sbg(ess, "dec", [128, 16])
                ea = k.sbg(ess, "ea", [128, 32])
                S = [k.sbg(ess, "Sst%d" % d, [128, 1024]) for d in range(2)]
                Sfb = k.sbg(ess, "Sfb", [128, 1024], BF16)
                bcm = k.sbg(ess, "bcm", [128, 2, 2, 128])
                r1s = [k.sbg(ess, "r1", [128, 512]) for _ in range(3)]
                gts = [k.sbg(ess, "gt", [128, 512], BF16) for _ in range(3)]
                css = [k.sbg(ess, "cs", [128, 512], BF16) for _ in range(3)]
                etmps = [k.sbg(ess, "etmp", [128, 512]) for _ in range(6)]
                PSB = PS[5][:].bitcast(BF16)

                def to_tm(tl):
                    tsl = slice(tl * 128, (tl + 1) * 128)
                    for c in range(8):
                        k.op("pe", "transpose", [PS[5][:]], [xbcT[:], IDENT], PSB[:, c * 128:(c + 1) * 128], xbcT[:, c, tsl], IDENT)
                    cp(xs_tm[:], PSB)
                    for c in range(2):
                        k.op("pe", "transpose", [PS[5][:]], [xbcT[:], IDENT], PSB[:, c * 128:(c + 1) * 128], xbcT[:, 8 + c, tsl], IDENT)
                    cp(b_tm[:], PSB[:, 0:256])

                def chunk_state(tl, d):
                    lad = la[:, tl, d * 16:(d + 1) * 16]
                    mm(PS[4][:, 0:16], M_GT if d == 0 else M_LT, lad)
                    mm(PS[4][:, 32:64], ONESF, la[:, tl, :])
                    act(dec[:], PS[4][:, 0:16], AF.Exp)
                    act(ea[:], PS[4][:, 32:64], AF.Exp)
                    tt(dec[:], dec[:], dt[:, tl, d * 16:(d + 1) * 16], ALU.mult)
                    tt(xdd[:].rearrange("p (a b) -> p a b", a=16), xs_tm[:].rearrange("p (a b) -> p a b", a=16),
                       dec[:].unsqueeze(2).broadcast_to([128, 16, 64]), ALU.mult)
                    for gg in range(2):
                        mm(PS[6 + gg][:], b_tm[:, gg * 128:(gg + 1) * 128], xdd[:, gg * 512:(gg + 1) * 512])

                def state_update(d):
                    Sv = S[d][:].rearrange("p (a b) -> p a b", a=16)
                    tt(Sv, Sv, ea[:, d * 16:(d + 1) * 16].unsqueeze(2).broadcast_to([128, 16, 64]), ALU.mult)
                    for gg in range(2):
                        tt(S[d][:, gg * 512:(gg + 1) * 512], S[d][:, gg * 512:(gg + 1) * 512], PS[6 + gg][:], ALU.add)

                for si, (t0, Ls) in enumerate(segs):
                    tiles = list(range(t0 // 128, (t0 + Ls) // 128))
                    if lat:
                        k.dma("sp", S[1][:], sst_d[l, :, 1024:2048])
                        k.dma("sp", S[0][:], sst_d[l, :, 0:1024])
                    else:
                        k.op("dve", "memset", [S[1][:]], [], S[1][:], 0.0)
                        k.op("dve", "memset", [S[0][:]], [], S[0][:], 0.0)
                    for tl in reversed(tiles):
                        to_tm(tl)
                        ck("s1", xs_tm=xs_tm[:], b_tm=b_tm[:])
                        cp(sinr[:, tl, :], S[1][:])
                        chunk_state(tl, 1)
                        ck("s2", dec=dec[:], ea=ea[:], xdd=xdd[:])
                        state_update(1)
                        ck("s3", S1=S[1][:])
                    if not lat:
                        k.dma("sp", nss_o[l, si, :, 1024:2048], S[1][:])
                    for tl in tiles:
                        tsl = slice(tl * 128, (tl + 1) * 128)
                        to_tm(tl)
                        cp(Sfb[:], S[0][:])
                        tt(xdf[:].rearrange("p (a b) -> p a b", a=16), xs_tm[:].rearrange("p (a b) -> p a b", a=16),
                           dt[:, tl, 0:16].unsqueeze(2).broadcast_to([128, 16, 64]), ALU.mult)
                        tt(xdr[:].rearrange("p (a b) -> p a b", a=16), xs_tm[:].rearrange("p (a b) -> p a b", a=16),
                           dt[:, tl, 16:32].unsqueeze(2).broadcast_to([128, 16, 64]), ALU.mult)
                        for gg in range(2):
                            mm(PS[4][:, gg * 128:(gg + 1) * 128], xbcT[:, 8 + gg, tsl], xbcT[:, 10 + gg, tsl])
                        for gg in range(2):
                            tt(bcm[:, gg, 0, :], PS[4][:, gg * 128:(gg + 1) * 128], M_LE, ALU.mult)
                            tt(bcm[:, gg, 1, :], PS[4][:, gg * 128:(gg + 1) * 128], M_GE, ALU.mult)
                        ck("s4", bcm=bcm[:], xdf=xdf[:])
                        def stA(u):
                            b4, d = u // 2, u % 2
                            r1 = r1s[u % 3]
                            ladb = la[:, tl, d * 16 + 4 * b4:d * 16 + 4 * b4 + 4]
                            tt(r1[:].rearrange("p (a b) -> p a b", a=4),
                               (M_LE if d == 0 else M_GE).unsqueeze(1).broadcast_to([128, 4, 128]),
                               ladb.unsqueeze(2).broadcast_to([128, 4, 128]), ALU.mult)
                            mm(PS[2 * (u % 3)][:], M_GT if d == 0 else M_LT, r1[:])
                            mm(PS[2 * (u % 3) + 1][:], ONESF, r1[:])

                        def stB(u):
                            b4, d = u // 2, u % 2
                            gg = b4 // 2
                            e0, e1 = etmps[2 * (u % 3)], etmps[2 * (u % 3) + 1]
                            act(e0[:], PS[2 * (u % 3)][:], AF.Exp)
                            tt(gts[u % 3][:].rearrange("p (a b) -> p a b", a=4), e0[:].rearrange("p (a b) -> p a b", a=4),
                               bcm[:, gg, d, :].unsqueeze(1).broadcast_to([128, 4, 128]), ALU.mult)
                            act(e1[:], PS[2 * (u % 3) + 1][:], AF.Exp)
                            tt(css[u % 3][:].rearrange("p (a b) -> p a b", a=4), e1[:].rearrange("p (a b) -> p a b", a=4),
                               xbcT[:, 10 + gg, tsl].unsqueeze(1).broadcast_to([128, 4, 128]), ALU.mult)

                        def stC(b4):
                            u0, u1 = 2 * b4, 2 * b4 + 1
                            for j in range(4):
                                hh = 4 * b4 + j
                                pr, e = hh // 2, hh % 2
                                yo = PS[6 + pr // 4][e * 64:(e + 1) * 64, (pr % 4) * 128:(pr % 4 + 1) * 128]
                                hs = slice(hh * 64, (hh + 1) * 64)
                                js = slice(j * 128, (j + 1) * 128)
                                tp = (0, e * 64)
                                mm(yo, xdf[:, hs], gts[u0 % 3][:, js], start=True, stop=False, tile_position=tp)
                                mm(yo, Sfb[:, hs], css[u0 % 3][:, js], start=False, stop=False, tile_position=tp)
                                mm(yo, xdr[:, hs], gts[u1 % 3][:, js], start=False, stop=False, tile_position=tp)
                                mm(yo, sinr[:, tl, hs], css[u1 % 3][:, js], start=False, stop=True, tile_position=tp)
                        stA(0)
                        stA(1)
                        for u in range(8):
                            if u + 2 < 8:
                                stA(u + 2)
                            stB(u)
                            if u % 2 == 1:
                                stC(u // 2)
                        ck("s6", xdr=xdr[:])
                        osd, _ = COLS(P + "sdd")
                        for pr in range(8):
                            stt(yT[:, pr, tsl], xbcT[:, pr, tsl], cst[:, osd + pr:osd + pr + 1],
                                PS[6 + pr // 4][:, (pr % 4) * 128:(pr % 4 + 1) * 128], ALU.mult, ALU.add)
                        ck("s7", yT=yT[:])
                        chunk_state(tl, 0)
                        state_update(0)
                        ck("s8", S0=S[0][:])
                    if not lat:
                        k.dma("sp", nss_o[l, si, :, 0:1024], S[0][:])
                    ck("seg%d" % si, S0=S[0][:], yT=yT[:])
                ck("scan_%s%d" % (g, l), yT=yT[:], sinr=sinr[:], S0=S[0][:], S1=S[1][:])
                k.barrier()
                ess.close()
                sz = [k.sbg(es, "sz", [128, 512]) for _ in range(2)]

                def z_consume(m, banks):
                    for hf in range(2):
                        sl = slice(hf * 512, (hf + 1) * 512)
                        act(sz[hf][:], banks[hf][:], AF.Silu)
                        tt(yT[:, m, sl], yT[:, m, sl], sz[hf][:], ALU.mult)
                linear_fm(WIN, C_Z, 1024, 16, hrhs, z_consume)
                sqg = [k.sbg(es, "sqg", [128, 512], BF16) for _ in range(2)]
                rsg = k.sbg(es, "rsg", [128, 512])
                osn, _ = COLS(P + "snw")
                for gg in range(2):
                    for hf in range(2):
                        sl = slice(hf * 512, (hf + 1) * 512)
                        for c4 in range(4):
                            act(sqg[c4 % 2][:], yT[:, 4 * gg + c4, sl], AF.Square)
                            mm(PS[6][:], ONES512, sqg[c4 % 2][:], start=(c4 == 0), stop=(c4 == 3))
                        rsqrt_to(rsg[:], PS[6][:])
                        for c4 in range(4):
                            c = 4 * gg + c4
                            stt(ybr[:, c, sl], yT[:, c, sl], cst[:, osn + c:osn + c + 1], rsg[:], ALU.mult, ALU.mult)
                k.barrier()
            ck("ya_%s%d" % (g, l), ybr=ybr[:])
            merged = k.sbg(es2, "merged", [128, 16, T], BF16)
            esbg = ExitStack()
            if bgpos[0] < len(bg):
                bgslots.extend([k.sbg(esbg, "bgslot", [128, 16, 256], BF16) for _ in range(2)])
            gated_branch(0, True)
            k.barrier()
            ck("mA_%s%d" % (g, l), merged=merged[:])

            def attention_branch(diff):
                with ExitStack() as es:
                    nk = 1280 if lat else 1024
                    nkt = nk // 128
                    qT = k.sbg(es, "qT", [128, 8, T], BF16)
                    kT = k.sbg(es, "kT", [128, 8, nk], BF16)
                    v_tm = k.sbg(es, "v_tm", [128, nkt, 1024], BF16)
                    nsq = [k.sbg(es, "nsq", [128, 512], BF16) for _ in range(2)]
                    nrs = k.sbg(es, "nrs", [128, 512])
                    sti = [0]
                    cq_, ck_, cv_ = (C_QD, C_KD, C_VD) if diff else (C_QN, C_KN, C_VN)
                    wq = smallv[:, l, 2:3] if diff else smallv[:, l, 3:4]
                    wk = ccol(P + ("dkn" if diff else "nkn"))
                    ko_d = ndk_o if diff else nnk_o
                    vo_d = ndv_o if diff else nnv_o
                    rope = lat and diff
                    if rope:
                        rope_sb = k.sbg(es, "rope_sb", [128, 2, T])
                        pmT = k.sbg(es, "pmT", [128, 128])
                        k.dma("sp", rope_sb[:], rope_d)
                        k.dma("sp", pmT[:], pmt_d)
                        rtmp = k.sbg(es, "rtmp", [128, 512])
                    if lat:
                        k.dma("pool", kT[:, :, 1024:1280], (cdk_d if diff else cnk_d)[l])
                        k.dma("pool", v_tm[:, 8:10, :], (cdv_d if diff else cnv_d)[l])

                    esq = ExitStack()
                    stage = [k.sbg(esq, "stage", [128, 512]) for _ in range(4)]
                    nrs2 = [nrs, k.sbg(esq, "nrs2", [128, 512])]
                    if rope:
                        rtmps = [rtmp, k.sbg(esq, "rtmp2", [128, 512])]

                    def qk_consume(dest, wcol, is_k):
                        pend_rope = []

                        def f(m, banks):
                            sts = []
                            for hf in range(2):
                                st = stage[sti[0] % 4]
                                sti[0] += 1
                                sts.append(st)
                                act(nsq[hf][:], banks[hf][:], AF.Square)
                                act(st[:], banks[hf][:], AF.Copy)
                            while pend_rope:
                                pend_rope.pop(0)()
                            for hf in range(2):
                                mm(PS[6 + hf][:], BLK64, nsq[hf][:])
                            for hf in range(2):
                                rsqrt_to(nrs2[hf][:], PS[6 + hf][:])
                                stt(sts[hf][:], sts[hf][:], wcol, nrs2[hf][:], ALU.mult, ALU.mult)
                            for hf in range(2):
                                sl = slice(hf * 512, (hf + 1) * 512)
                                st = sts[hf]
                                if rope:
                                    def rp(st=st, hf=hf, sl=sl, m=m):
                                        mm(PS[6 + hf][:], pmT[:], st[:])
                                        tt(rtmps[hf][:], PS[6 + hf][:], rope_sb[:, 1, sl], ALU.mult)
                                        tt(st[:], st[:], rope_sb[:, 0, sl], ALU.mult)
                                        tt(dest[:, m, sl], st[:], rtmps[hf][:], ALU.add)
                                    pend_rope.append(rp)
                                else:
                                    act(dest[:, m, sl], st[:], AF.Copy)
                                    if is_k and not lat:
                                        k.dma("sp", ko_d[l, :, m, sl], st[:])

                        def flush():
                            while pend_rope:
                                pend_rope.pop(0)()
                        f.flush = flush
                        return f
                    cf = qk_consume(qT, wq, False)
                    linear_fm(WIN, cq_, 1024, 16, hrhs, cf)
                    cf.flush()
                    ck("q1", qT=qT[:])
                    cf = qk_consume(kT, wk, True)
                    linear_fm(WIN, ck_, 1024, 16, hrhs, cf)
                    cf.flush()
                    ck("q2", kT=kT[:])
                    esv = ExitStack()
                    vT = [k.sbg(esv, "vT", [128, T], BF16) for _ in range(2)]
                    PSBv = PS[6][:].bitcast(BF16)

                    def v_consume(m, banks):
                        vt_ = vT[m % 2]
                        for hf in range(2):
                            sl = slice(hf * 512, (hf + 1) * 512)
                            if not lat:
                                st = stage[sti[0] % 4]
                                sti[0] += 1
                                act(st[:], banks[hf][:], AF.Copy)
                                k.dma("sp", vo_d[l, :, m, sl], st[:])
                                cp(vt_[:, sl], st[:])
                            else:
                                act(vt_[:, sl], banks[hf][:], AF.Copy)
                        for tl in range(8):
                            k.op("pe", "transpose", [PS[6][:]], [vt_[:], IDENT], PSBv[:, tl * 128:(tl + 1) * 128],
                                 vt_[:, tl * 128:(tl + 1) * 128], IDENT)
                        cp(v_tm[:, 0:8, m * 128:(m + 1) * 128], PSBv.rearrange("p (a b) -> p a b", a=8))
                    linear_fm(WIN, cv_, 1024, 16, hrhs, v_consume)
                    k.barrier()
                    esv.close()
                    esq.close()
                    ck("qkv%d_%s%d" % (int(diff), g, l), qT=qT[:], kT=kT[:], v_tm=v_tm[:])
                    pT = [k.sbg(es, "pT", [128, 512], BF16) for _ in range(4)]
                    pti = [0]
                    rd = [k.sbg(es, "rd", [128, 512]) for _ in range(2 if diff else 1)]
                    if diff:
                        oa = k.sbg(es, "oa", [128, 512])
                        ob2 = k.sbg(es, "ob2", [128, 512])
                    if lat:
                        qblocks = [(hf * 512, 512, list(range(10))) for hf in range(2)]
                    else:
                        qblocks = [(s0, Ls, [s0 // 128, s0 // 128 + 1]) for (s0, Ls) in segs]
                    if lat and not diff:
                        tbs = [k.sbg(es, "tbs", [128, 1920], BF16) for _ in range(4)]
                        rowoh = k.sbg(es, "rowoh", [16, 8, 128], BF16)
                        rowpen = k.sbg(es, "rowpen", [16, T], BF16)
                        k.dma("pool", rowoh[:], rowoh_d)
                        k.dma("pool", rowpen[:], rowpen_d)
                    sbi = [0]

                    def sbank():
                        b = PS[sbi[0] % 4]
                        sbi[0] += 1
                        return b
                    LOOK = 2

                    def run_pipe(steps, score_fn, consume_fn):
                        banks = {}
                        n = len(steps)

                        def issue(idx):
                            idx = [i_ for i_ in idx if i_ < n]
                            if idx:
                                for i_, b_ in zip(idx, score_fn([steps[i_] for i_ in idx])):
                                    banks[i_] = b_
                        issue([0, 1])
                        for i in range(n):
                            if i % 2 == 0:
                                issue([i + 2, i + 3])
                            consume_fn(steps[i], banks.pop(i))

                    if diff:
                        steps = []
                        for hh in range(8):
                            for (q0, nq, ktl) in qblocks:
                                for ji, j in enumerate(ktl):
                                    for t2 in range(2):
                                        steps.append((hh, q0, nq, ji, j, t2, len(ktl)))

                        def score_fn(sts_):
                            out_ = []
                            for st_ in sts_:
                                hh, q0, nq, ji, j, t2, nkt_ = st_
                                ps_ = slice(t2 * 64, (t2 + 1) * 64)
                                sb_ = sbank()
                                mm(sb_[:, 0:nq], kT[ps_, hh, j * 128:(j + 1) * 128], qT[ps_, hh, q0:q0 + nq])
                                out_.append(sb_)
                            return out_

                        def consume_fn(st_, sb_):
                            hh, q0, nq, ji, j, t2, nkt_ = st_
                            qs = slice(q0, q0 + nq)
                            p_ = pT[pti[0] % 4]
                            pti[0] += 1
                            act(p_[:, 0:nq], sb_[:, 0:nq], AF.Exp)
                            mm(PS[4 + 2 * t2][:, 0:nq], v_tm[:, j, hh * 128:(hh + 1) * 128], p_[:, 0:nq],
                               start=(ji == 0), stop=(ji == nkt_ - 1))
                            mm(PS[5 + 2 * t2][:, 0:nq], ONES, p_[:, 0:nq], start=(ji == 0), stop=(ji == nkt_ - 1))
                            if not (ji == nkt_ - 1 and t2 == 1):
                                return
                            for t3 in range(2):
                                act(rd[t3][:, 0:nq], PS[5 + 2 * t3][:, 0:nq], AF.Ln)
                                act(rd[t3][:, 0:nq], rd[t3][:, 0:nq], AF.Exp, scale=-1.0)
                            tt(oa[:, 0:nq], PS[4][:, 0:nq], rd[0][:, 0:nq], ALU.mult)
                            tt(ob2[:, 0:nq], PS[6][:, 0:nq], rd[1][:, 0:nq], ALU.mult)
                            stt(oa[:, 0:nq], ob2[:, 0:nq], smallv[:, l, 0:1], oa[:, 0:nq], ALU.mult, ALU.add)
                            act(nsq[0][:, 0:nq], oa[:, 0:nq], AF.Square)
                            mm(PS[5][:, 0:nq], ONES128, nsq[0][:, 0:nq])
                            rsqrt_to(nrs[:, 0:nq], PS[5][:, 0:nq])
                            stt(ybr[:, hh, qs], oa[:, 0:nq], smallv[:, l, 1:2], nrs[:, 0:nq], ALU.mult, ALU.mult)
                        run_pipe(steps, score_fn, consume_fn)
                    else:
                        if lat:
                            qblocks = [(0, 512, [0, 1, 2, 3, 4, 5, 8, 9]), (512, 512, [2, 3, 4, 5, 6, 7, 8, 9])]
                        steps = []
                        for hp in range(8):
                            for (q0, nq, ktl) in qblocks:
                                for ji, j in enumerate(ktl):
                                    for e in range(2):
                                        steps.append((hp, q0, nq, ji, j, e, len(ktl)))
                        tb_loaded = set()

                        def score_fn(sts_):
                            out_ = []
                            for st_ in sts_:
                                hp, q0, nq, ji, j, e, nkt_ = st_
                                ps_ = slice(e * 64, (e + 1) * 64)
                                local = lat and j < 8
                                if local and hp not in tb_loaded:
                                    tb_loaded.add(hp)
                                    for e2 in range(2):
                                        k.dma("pool", tbs[2 * (hp % 2) + e2][:], tb_d[l, 2 * hp + e2])
                                sb_ = sbank()
                                mm(sb_[:, 0:nq], kT[ps_, hp, j * 128:(j + 1) * 128], qT[ps_, hp, q0:q0 + nq], start=True, stop=(not local))
                                out_.append(sb_)
                            for st_, sb_ in zip(sts_, out_):
                                hp, q0, nq, ji, j, e, nkt_ = st_
                                hf = q0 // 512
                                if lat and j < 8:
                                    i0 = 14 - 2 * j + 8 * hf
                                    mm(sb_[:, 0:nq], IDENT, tbs[2 * (hp % 2) + e][:, i0 * 64:i0 * 64 + 512], start=False, stop=False)
                                    mm(sb_[:, 0:nq], rowoh[:, j, :], rowpen[:, q0:q0 + nq], start=False, stop=True)
                            return out_

                        def consume_fn(st_, sb_):
                            hp, q0, nq, ji, j, e, nkt_ = st_
                            ps_ = slice(e * 64, (e + 1) * 64)
                            p_ = pT[pti[0] % 4]
                            pti[0] += 1
                            act(p_[:, 0:nq], sb_[:, 0:nq], AF.Exp)
                            hh = 2 * hp + e
                            tp = (0, e * 64)
                            mm(PS[4][ps_, 0:nq], v_tm[:, j, hh * 64:(hh + 1) * 64], p_[:, 0:nq],
                               start=(ji == 0), stop=(ji == nkt_ - 1), tile_position=tp)
                            mm(PS[5][ps_, 0:nq], ONES[:, 0:64], p_[:, 0:nq],
                               start=(ji == 0), stop=(ji == nkt_ - 1), tile_position=tp)
                            if not (ji == nkt_ - 1 and e == 1):
                                return
                            act(rd[0][:, 0:nq], PS[5][:, 0:nq], AF.Ln)
                            act(rd[0][:, 0:nq], rd[0][:, 0:nq], AF.Exp, scale=-1.0)
                            tt(ybr[:, hp, q0:q0 + nq], PS[4][:, 0:nq], rd[0][:, 0:nq], ALU.mult)
                        run_pipe(steps, score_fn, consume_fn)
                    k.barrier()

            attention_branch(True)
            ck("yb_%s%d" % (g, l), ybr=ybr[:])
            gated_branch(1, False)
            k.barrier()
            attention_branch(False)
            ck("yc_%s%d" % (g, l), ybr=ybr[:])
            gated_branch(2, False)
            k.barrier()
            ck("merged_%s%d" % (g, l), merged=merged[:])

            bg_until(BG_L0)
            k.barrier()
            del bgslots[:]
            esbg.close()
            with ExitStack() as es:
                Xres = k.sbg(es, "Xres2", [128, 16, T])
                xi = [k.sbg(es, "xi", [128, T]) for _ in range(2)]
                og1 = 32

                def wout_consume(m, banks):
                    x_ = xi[m % 2]
                    k.dma("sp", x_[:], src[:, m, :])
                    for hf in range(2):
                        sl = slice(hf * 512, (hf + 1) * 512)
                        stt(Xres[:, m, sl], banks[hf][:], mod_sb[l][:, og1 + m, gi:gi + 1], x_[:, sl], ALU.mult, ALU.add)
                linear_fm(wview(w_out, l), 0, D, 16, lambda kc, hf: merged[:, kc, hf * 512:(hf + 1) * 512], wout_consume)
                for c4 in range(4):
                    k.dma("sp", xmid1[g][:, 4 * c4:4 * c4 + 4, :], Xres[:, 4 * c4:4 * c4 + 4, :])
                rmsnorm_mod(es, Xres, 1)
                k.barrier()
                ck("x1_%s%d" % (g, l), Xres=Xres[:], h=h[:])

        with ExitStack() as es:
            aT = k.sbg(es, "aT", [128, 44, T], BF16)
            esf = ExitStack()
            if bgpos[0] < len(bg):
                bgslots.extend([k.sbg(esf, "bgslot", [128, 16, 256], BF16) for _ in range(2)])
            cb = conv_bufs(esf)
            vt = [k.sbg(esf, "vt", [128, T]) for _ in range(2)]
            gtm = k.sbg(esf, "gtm", [128, T])
            WUP = wview(w_up, l)
            for fp in range(22):
                sv = wload(WUP, fp * 256, 256, 16)
                sgt = wload(WUP, DFF + fp * 256, 256, 16)
                for mc in range(2):
                    f = 2 * fp + mc
                    banks = bankpair()
                    for kc in range(16):
                        for hf in range(2):
                            mm(banks[hf][:], sv[:, kc, mc * 128:(mc + 1) * 128], hrhs(kc, hf), start=(kc == 0), stop=(kc == 15))
                    conv_evac(cb, banks, "fcw", "fcb", f, 88, vt[mc][:], AF.Identity)
                for mc in range(2):
                    f = 2 * fp + mc
                    banks = bankpair()
                    for kc in range(16):
                        for hf in range(2):
                            mm(banks[hf][:], sgt[:, kc, mc * 128:(mc + 1) * 128], hrhs(kc, hf), start=(kc == 0), stop=(kc == 15))
                    conv_evac(cb, banks, "fcw", "fcb", 44 + f, 88, gtm[:], AF.Silu)
                    tt(aT[:, f, :], vt[mc][:], gtm[:], ALU.mult)
                bg_step(2)
            bg_until(len(bg))
            k.barrier()
            del bgslots[:]
            esf.close()
            ck("aT_%s%d" % (g, l), aT=aT[:])
            dsl_own = k.sbg(es, "dslot", [128, 44, 256], BF16)
            dsl_h = h[:, 0:11, :].rearrange("p a (b c) -> p (a b) c", c=256)
            xi = [k.sbg(es, "xi2", [128, T]) for _ in range(2)]
            xo = [k.sbg(es, "xo", [128, T]) for _ in range(2)]
            WDN = wview(w_dn, l)
            og2 = 80
            for mb in range(8):
                ds_ = dsl_own[:] if mb % 2 == 0 else dsl_h
                for k0 in range(0, 44, 11):
                    k.dma("pool", ds_[:, k0:k0 + 11, :], WDN[:, k0:k0 + 11, mb * 256:(mb + 1) * 256])
                for mc in range(2):
                    m = 2 * mb + mc
                    x_ = xi[m % 2]
                    k.dma("sp", x_[:], xmid1[g][:, m, :])
                    banks = bankpair()
                    for kc in range(44):
                        for hf in range(2):
                            mm(banks[hf][:], ds_[:, kc, mc * 128:(mc + 1) * 128], aT[:, kc, hf * 512:(hf + 1) * 512],
                               start=(kc == 0), stop=(kc == 43))
                    o_ = xo[m % 2]
                    for hf in range(2):
                        sl = slice(hf * 512, (hf + 1) * 512)
                        stt(o_[:, sl], banks[hf][:], mod_sb[l][:, og2 + m, gi:gi + 1], x_[:, sl], ALU.mult, ALU.add)
                    k.dma("sp", dst[:, m, :], o_[:])
            k.barrier()
        if stop == "pass_%s%d" % (g, l):
            with ExitStack() as es:
                rb = k.sbg(es, "rb", [128, 16, T])
                k.dma("sp", rb[:], dst)
                ck("pass_%s%d" % (g, l), x2=rb[:])

    try:
        for g in ("ctx", "lat"):
            for l in range(DEPTH):
                if not stopped and (only is None or (g, l) in only):
                    run_pass(l, g)
    except _Stop:
        pass
    k.finish()
    print("program: %d instructions, %d dma sems" % (nc.n_instructions(), len(k.dsems)))
    return nc


def _fm(v):
    v = np.asarray(v, np.float32)
    return np.ascontiguousarray(v.reshape(-1, 128).T)


def _structural():
    idx = np.arange(128)
    kk, ll = idx[:, None], idx[None, :]
    matb = np.zeros((128, N_MATS, 128), np.float32)
    matb[:, 0] = np.eye(128)
    matb[:, 1] = 1.0
    matb[:, 2] = 1.0 / 2048
    matb[:, 3] = 1.0 / 512
    matb[:, 4] = 1.0 / 128
    matb[:, 5] = ((kk // 64) == (ll // 64)) / 64.0
    matf = np.zeros((128, N_MATF, 128), np.float32)
    matf[:, 0] = kk <= ll
    matf[:, 1] = kk >= ll
    matf[:, 2] = kk < ll
    matf[:, 3] = kk > ll
    matf[:, 4] = 1.0
    t = np.arange(T)
    row = (t // 64).astype(np.float32)
    col = (t % 64).astype(np.float32)
    inv = (10000.0 ** (-np.arange(16, dtype=np.float32) / 16)).astype(np.float32)
    rope = np.zeros((128, 2, T), np.float32)
    pm = np.zeros((128, 128), np.float32)
    for p in range(128):
        d = p % 64
        pos = row if d < 32 else col
        ang = pos * inv[d % 16]
        rope[p, 0] = np.cos(ang)
        rope[p, 1] = np.sin(ang)
        if d % 32 < 16:
            pm[p, p + 16] = -1.0
        else:
            pm[p, p - 16] = 1.0
    pmT = np.ascontiguousarray(pm.T)
    rowoh = np.zeros((16, 8, 128), np.float32)
    for j in range(8):
        for kx in range(128):
            rowoh[2 * j + kx // 64, j, kx] = 1.0
    rowpen = np.zeros((16, T), np.float32)
    for qr in range(16):
        w0 = min(max(qr - 4, 0), 8)
        for kr in range(16):
            if not (w0 <= kr < w0 + 8):
                rowpen[kr, qr * 64:(qr + 1) * 64] = NEG
    return matb, matf, rope, pmT, rowoh, rowpen


def _na_tables(rpb):
    kc = np.arange(64)[:, None]
    qc = np.arange(64)[None, :]
    wst = np.clip(qc - 8, 0, 48)
    colok = (kc >= wst) & (kc < wst + 16)
    dc = np.clip(kc - qc + 15, 0, 30)
    out = np.full((DEPTH, 16, 2, 64, 30, 64), NEG, np.float32)
    for krl in range(2):
        for i in range(30):
            a = 21 - i + krl
            if 0 <= a <= 14:
                blk = rpb[:, :, a][:, :, dc]
                out[:, :, krl, :, i, :] = np.where(colok[None, None], blk, NEG)
    return np.ascontiguousarray(out.reshape(DEPTH, 16, 128, 1920))


_PROG = {}


def kernel(**inp):
    f32 = lambda a: np.ascontiguousarray(np.asarray(a, np.float32))
    x_prompt, x_sample = f32(inp["x_prompt"]), f32(inp["x_sample"])
    matb, matf, rope, pmT, rowoh, rowpen = _structural()
    tb = _na_tables(f32(inp["na_rpb"]))
    shared = {}
    for nm in ["w_ada", "w_in", "w_branch_a", "w_branch_b", "w_branch_c", "w_out", "ffn_w_up", "ffn_w_down"]:
        shared[nm] = f32(inp[nm])
    shared.update(matb=matb, matf=matf, rope=rope, pmT=pmT, rowoh=rowoh, rowpen=rowpen, tb=tb)

    def to_fm(x):
        return np.ascontiguousarray(x.reshape(T, 16, 128).transpose(2, 1, 0))

    in_maps = []
    for ci in range(NCORES):
        sb = ci // 4
        m = dict(shared)
        m["xp"] = to_fm(x_prompt[4 * ci:4 * ci + 4].reshape(T, D))
        m["xs"] = to_fm(x_sample[sb])
        cst = np.zeros((128, COLS.n), np.float32)

        def put(name, arr):
            o, w = COLS(name)
            assert arr.shape == (128, w), (name, arr.shape, w)
            cst[:, o:o + w] = arr
        cs = np.stack([_fm(inp["c_ctx"]), _fm(inp["c"][sb])], axis=2)
        put("csil", cs.reshape(128, 32))
        for l in range(DEPTH):
            p = "L%d_" % l
            put(p + "n1w", _fm(inp["norm1_w"][l]))
            put(p + "n2w", _fm(inp["norm2_w"][l]))
            put(p + "bada", _fm(inp["b_ada"][l]))
            put(p + "scw", np.concatenate([_fm(inp["ssd_conv_w"][l][j]) for j in range(3)], axis=1))
            put(p + "scb", _fm(inp["ssd_conv_b"][l]))
            put(p + "fcw", np.concatenate([_fm(inp["ffn_conv_w"][l][j]) for j in range(3)], axis=1))
            put(p + "fcb", _fm(inp["ffn_conv_b"][l]))
            put(p + "snw", _fm(inp["ssd_norm_w"][l]))
            put(p + "sdd", _fm(np.repeat(f32(inp["ssd_d"][l]), 64)))
            rep2 = lambda v: np.tile(f32(v), 2).reshape(128, 1)
            put(p + "dqn", rep2(inp["diff_q_norm"][l]))
            put(p + "dkn", rep2(inp["diff_k_norm"][l]))
            put(p + "sub", f32(inp["diff_subln_w"][l]).reshape(128, 1))
            put(p + "nqn", rep2(inp["na_q_norm"][l]))
            put(p + "nkn", rep2(inp["na_k_norm"][l]))
            put(p + "dtb", np.tile(f32(inp["ssd_dt_bias"][l]).reshape(1, 32), (128, 1)))
            put(p + "alog", np.tile(f32(inp["ssd_a_log"][l]).reshape(1, 32), (128, 1)))
            put(p + "lam", np.tile(f32(inp["diff_lam"][l]).reshape(1, 256), (128, 1)))
        m["cst"] = cst
        cdk = f32(inp["cache_diff_k"][sb])
        m["cdk"] = np.ascontiguousarray(cdk.transpose(0, 3, 4, 2, 1).reshape(DEPTH, 128, 8, 256))
        cdv = f32(inp["cache_diff_v"][sb])
        m["cdv"] = np.ascontiguousarray(cdv.reshape(DEPTH, 2, 128, 1024).transpose(0, 2, 1, 3))
        cnk = f32(inp["cache_na_k"][sb])
        m["cnk"] = np.ascontiguousarray(cnk.reshape(DEPTH, 256, 8, 128).transpose(0, 3, 2, 1))
        cnv = f32(inp["cache_na_v"][sb])
        m["cnv"] = np.ascontiguousarray(cnv.reshape(DEPTH, 2, 128, 1024).transpose(0, 2, 1, 3))
        sst = f32(inp["state_ssm"][sb])
        m["sst"] = np.ascontiguousarray(sst.transpose(0, 4, 1, 2, 3).reshape(DEPTH, 128, 2048))
        in_maps.append(m)

    if "nc" not in _PROG:
        _PROG["nc"] = build_program()
    res = run_bass_kernel_spmd(_PROG["nc"], in_maps, core_ids=list(range(NCORES)))
    R = res.results

    def from_fm(a):
        return a.transpose(2, 1, 0).reshape(T, D)

    B, S = 32, 256
    y_p = np.zeros((B, S, D), np.float32)
    y_s = np.zeros((2, 1024, D), np.float32)
    ndk = np.zeros((B, DEPTH, S, 8, 2, 64), np.float32)
    ndv = np.zeros((B, DEPTH, S, 8, 128), np.float32)
    nnk = np.zeros((B, DEPTH, S, 16, 64), np.float32)
    nnv = np.zeros((B, DEPTH, S, 16, 64), np.float32)
    nss = np.zeros((B, DEPTH, 2, 16, 64, 128), np.float32)
    for ci in range(NCORES):
        r = R[ci]
        y_p[4 * ci:4 * ci + 4] = from_fm(np.asarray(r["ypT"])).reshape(4, S, D)
        if ci % 4 == 0:
            y_s[ci // 4] = from_fm(np.asarray(r["ysT"]))
        kd = np.asarray(r["ndk"])
        ndk[4 * ci:4 * ci + 4] = kd.reshape(DEPTH, 2, 64, 8, 4, S).transpose(4, 0, 5, 3, 1, 2)
        vd = np.asarray(r["ndv"])
        ndv[4 * ci:4 * ci + 4] = vd.reshape(DEPTH, 128, 8, 4, S).transpose(3, 0, 4, 2, 1)
        kn = np.asarray(r["nnk"])
        nnk[4 * ci:4 * ci + 4] = kn.reshape(DEPTH, 2, 64, 8, 4, S).transpose(4, 0, 5, 3, 1, 2).reshape(4, DEPTH, S, 16, 64)
        vn = np.asarray(r["nnv"])
        nnv[4 * ci:4 * ci + 4] = vn.reshape(DEPTH, 2, 64, 8, 4, S).transpose(4, 0, 5, 3, 1, 2).reshape(4, DEPTH, S, 16, 64)
        ss = np.asarray(r["nss"])
        nss[4 * ci:4 * ci + 4] = ss.reshape(DEPTH, 4, 128, 2, 16, 64).transpose(1, 0, 3, 4, 5, 2)
    return (y_p, y_s, ndk, ndv, nnk, nnv, nss)
```
